# Optimizing a Trainium2 kernel written in Bass

```python
import math
import jax, jax.numpy as jnp
from jax import lax
import numpy as np

D_MODEL = 1024
BATCH = 8
SEQ = 2048
DEPTH = 2

D_MIX = D_MODEL
SSD_INNER = D_MIX // 2
SSD_HEAD_DIM = 64
SSD_HEADS = SSD_INNER // SSD_HEAD_DIM
SSD_GROUPS = 2
SSD_STATE = 128
SSD_CONV = 4
SSD_CHUNK = 128
SSD_XBC = SSD_INNER + 2 * SSD_GROUPS * SSD_STATE
POOL_WIDTH = D_MIX // 4
POOL_GROUPS = 4
POOL_GROUP_DIM = POOL_WIDTH // POOL_GROUPS
ATTN_WIDTH = D_MIX - SSD_INNER - POOL_WIDTH
ATTN_HEAD_DIM = 64
ATTN_HEADS = ATTN_WIDTH // ATTN_HEAD_DIM
ROPE_DIM = ATTN_HEAD_DIM // 4
ROPE_THETA = 500000.0
MOBA_BLOCK = 256
MOBA_TOPK = 3
MOBA_QCHUNK = 64
D_FF = 2816
RMS_EPS = 1e-6

OFF_Z = 0
OFF_XBC = OFF_Z + SSD_INNER
OFF_DT = OFF_XBC + SSD_XBC
OFF_POOL = OFF_DT + SSD_HEADS
OFF_Q = OFF_POOL + POOL_WIDTH
OFF_K = OFF_Q + ATTN_WIDTH
OFF_V = OFF_K + ATTN_WIDTH
IN_COLS = OFF_V + ATTN_WIDTH

kernel_name = "hymba_ssd_pool_moba_macaron"


def _rmsnorm(x, g):
    x32 = x.astype(jnp.float32)
    y = x32 * lax.rsqrt(jnp.mean(x32 * x32, axis=-1, keepdims=True) + RMS_EPS)
    return y.astype(x.dtype) * g


def _swiglu(h, wg, wu, wd):
    return (jax.nn.silu(h @ wg) * (h @ wu)) @ wd


def _partial_rope(t, pos):
    half = ROPE_DIM // 2
    inv_freq = ROPE_THETA ** (-jnp.arange(0, ROPE_DIM, 2, dtype=jnp.float32) / ROPE_DIM)
    ang = pos[:, None] * inv_freq[None, :]
    cos = jnp.cos(ang)[None, :, None, :]
    sin = jnp.sin(ang)[None, :, None, :]
    t32 = t[..., :ROPE_DIM].astype(jnp.float32)
    x1, x2 = t32[..., :half], t32[..., half:]
    rot = jnp.concatenate([x1 * cos - x2 * sin, x2 * cos + x1 * sin], axis=-1).astype(t.dtype)
    return jnp.concatenate([rot, t[..., ROPE_DIM:]], axis=-1)


def _ssd_chunked(xdt, dA, Bm, Cm):
    b, S, H, P = xdt.shape
    G = SSD_GROUPS
    HG = H // G
    N = Bm.shape[-1]
    L = SSD_CHUNK
    nc = S // L
    x = xdt.reshape(b, nc, L, G, HG, P)
    a = dA.reshape(b, nc, L, G, HG).transpose(0, 3, 4, 1, 2)
    Bc = Bm.reshape(b, nc, L, G, N)
    Cc = Cm.reshape(b, nc, L, G, N)
    a_cs = jnp.cumsum(a, axis=-1)
    causal = jnp.tril(jnp.ones((L, L), dtype=bool))
    seg = jnp.exp(jnp.where(causal, a_cs[..., :, None] - a_cs[..., None, :], -jnp.inf))
    cb = jnp.einsum('bclgn,bcsgn->bgcls', Cc, Bc)
    y_diag = jnp.einsum('bgcls,bghcls,bcsghp->bclghp', cb, seg, x)
    decay_in = jnp.exp(a_cs[..., -1:] - a_cs)
    states = jnp.einsum('bclgn,bghcl,bclghp->bcghpn', Bc, decay_in, x)
    chunk_decay = jnp.exp(a_cs[..., -1])

    def step(h, inp):
        st, dec = inp
        return h * dec[..., None, None] + st, h

    h0 = jnp.zeros(states.shape[:1] + states.shape[2:], states.dtype)
    _, prev = lax.scan(step, h0, (jnp.moveaxis(states, 1, 0), jnp.moveaxis(chunk_decay, -1, 0)))
    prev = jnp.moveaxis(prev, 0, 1)
    y_off = jnp.einsum('bclgn,bcghpn,bghcl->bclghp', Cc, prev, jnp.exp(a_cs))
    return (y_diag + y_off).reshape(b, S, H, P)


def _ssd_mixer(z, xbc, dt_raw, conv_w, conv_b, dt_bias, a_log, d_skip, norm_g):
    b, S, C = xbc.shape
    xbc = lax.conv_general_dilated(
        xbc, conv_w[:, None, :], window_strides=(1,), padding=[(SSD_CONV - 1, 0)],
        dimension_numbers=('NWC', 'WIO', 'NWC'), feature_group_count=C)
    xbc = jax.nn.silu(xbc + conv_b)
    gn = SSD_GROUPS * SSD_STATE
    xs = xbc[..., :SSD_INNER].reshape(b, S, SSD_HEADS, SSD_HEAD_DIM)
    Bm = xbc[..., SSD_INNER:SSD_INNER + gn].reshape(b, S, SSD_GROUPS, SSD_STATE)
    Cm = xbc[..., SSD_INNER + gn:].reshape(b, S, SSD_GROUPS, SSD_STATE)
    dt = jax.nn.softplus((dt_raw + dt_bias).astype(jnp.float32))
    A = -jnp.exp(a_log.astype(jnp.float32))
    y = _ssd_chunked(xs * dt[..., None], dt * A, Bm, Cm)
    y = y + d_skip[:, None] * xs
    y = y.reshape(b, S, SSD_INNER).astype(z.dtype)
    return _rmsnorm(y * jax.nn.silu(z), norm_g)


def _pool_mixer(u, pool_w, pool_scale):
    S = u.shape[1]
    c = jnp.cumsum(u.astype(jnp.float32), axis=1)
    t = jnp.arange(1, S + 1, dtype=jnp.float32)
    outs = []
    for g in range(POOL_GROUPS):
        w = 2 ** (g + 1)
        sl = slice(g * POOL_GROUP_DIM, (g + 1) * POOL_GROUP_DIM)
        cg = c[..., sl]
        shifted = jnp.pad(cg, ((0, 0), (w, 0), (0, 0)))[:, :S]
        mean = (cg - shifted) / jnp.minimum(t, float(w))[None, :, None]
        outs.append(jnp.einsum('bsc,cd->bsd', (mean - u[..., sl]).astype(u.dtype), pool_w[g]))
    return jnp.concatenate(outs, axis=-1) * pool_scale


def _moba_attention(q, k, v):
    b, S, H, Dh = q.shape
    q = q.transpose(0, 2, 1, 3)
    k = k.transpose(0, 2, 1, 3)
    v = v.transpose(0, 2, 1, 3)
    nb = -(-S // MOBA_BLOCK)
    sp = nb * MOBA_BLOCK
    topk = min(MOBA_TOPK, nb)
    pad = ((0, 0), (0, 0), (0, sp - S), (0, 0))
    k_pad = jnp.pad(k, pad)
    v_pad = jnp.pad(v, pad)
    kb = k_pad.reshape(b, H, nb, MOBA_BLOCK, Dh)
    vb = v_pad.reshape(b, H, nb, MOBA_BLOCK, Dh)
    k_mean = jnp.mean(kb.astype(jnp.float32), axis=3)
    gate = jnp.einsum('bhsd,bhnd->bhsn', q.astype(jnp.float32), k_mean)
    q_blk = jnp.arange(S) // MOBA_BLOCK
    past = jnp.arange(nb)[None, :] < q_blk[:, None]
    gate = jnp.where(past, gate, -jnp.inf)
    _, sel = lax.top_k(gate, topk)
    sel_valid = sel < q_blk[None, None, :, None]
    nq = S // MOBA_QCHUNK

    def chunks(t):
        return jnp.moveaxis(t.reshape((b, H, nq, MOBA_QCHUNK) + t.shape[3:]), 2, 0)

    bidx = jnp.arange(b)[:, None, None, None]
    hidx = jnp.arange(H)[None, :, None, None]
    scale = Dh ** -0.5

    def attend(args):
        qi, si, vi, ci = args
        start = ci * MOBA_QCHUNK
        q_pos = start + jnp.arange(MOBA_QCHUNK)
        kg = kb[bidx, hidx, si]
        vg = vb[bidx, hidx, si]
        s_sel = jnp.einsum('bhqd,bhqkjd->bhqkj', qi, kg).astype(jnp.float32) * scale
        s_sel = jnp.where(vi[..., None], s_sel, -jnp.inf).reshape(b, H, MOBA_QCHUNK, topk * MOBA_BLOCK)
        own = (start // MOBA_BLOCK) * MOBA_BLOCK
        ko = lax.dynamic_slice_in_dim(k_pad, own, MOBA_BLOCK, axis=2)
        vo = lax.dynamic_slice_in_dim(v_pad, own, MOBA_BLOCK, axis=2)
        s_own = jnp.einsum('bhqd,bhjd->bhqj', qi, ko).astype(jnp.float32) * scale
        k_pos = own + jnp.arange(MOBA_BLOCK)
        s_own = jnp.where(k_pos[None, :] <= q_pos[:, None], s_own, -jnp.inf)
        p = jax.nn.softmax(jnp.concatenate([s_sel, s_own], axis=-1), axis=-1).astype(v.dtype)
        p_sel = p[..., :topk * MOBA_BLOCK].reshape(b, H, MOBA_QCHUNK, topk, MOBA_BLOCK)
        return (jnp.einsum('bhqkj,bhqkjd->bhqd', p_sel, vg)
                + jnp.einsum('bhqj,bhjd->bhqd', p[..., topk * MOBA_BLOCK:], vo))

    out = lax.map(attend, (chunks(q), chunks(sel), chunks(sel_valid), jnp.arange(nq)))
    out = jnp.moveaxis(out, 0, 2).reshape(b, H, S, Dh)
    return out.transpose(0, 2, 1, 3).reshape(b, S, H * Dh)


def _hybrid_mixer(h, w_in, conv_w, conv_b, dt_bias, a_log, d_skip, ssd_norm, pool_w, pool_scale, w_out):
    b, S, _ = h.shape
    proj = h @ w_in
    z = proj[..., OFF_Z:OFF_XBC]
    xbc = proj[..., OFF_XBC:OFF_DT]
    dt_raw = proj[..., OFF_DT:OFF_POOL]
    u = proj[..., OFF_POOL:OFF_Q]
    pos = jnp.arange(S, dtype=jnp.float32)
    q = _partial_rope(proj[..., OFF_Q:OFF_K].reshape(b, S, ATTN_HEADS, ATTN_HEAD_DIM), pos)
    k = _partial_rope(proj[..., OFF_K:OFF_V].reshape(b, S, ATTN_HEADS, ATTN_HEAD_DIM), pos)
    v = proj[..., OFF_V:IN_COLS].reshape(b, S, ATTN_HEADS, ATTN_HEAD_DIM)
    y_ssd = _ssd_mixer(z, xbc, dt_raw, conv_w, conv_b, dt_bias, a_log, d_skip, ssd_norm)
    y_pool = _pool_mixer(u, pool_w, pool_scale).astype(h.dtype)
    y_attn = _moba_attention(q, k, v)
    return jnp.concatenate([y_ssd, y_pool, y_attn], axis=-1) @ w_out


def setup_inputs(seed: int = 0) -> dict:
    key = jax.random.key(seed)
    ks = jax.random.split(key, 24)
    f32 = jnp.float32

    def nrm(k, shape, fan_in):
        return jax.random.normal(k, shape, f32) * fan_in ** -0.5

    def gain(k, shape):
        return 1.0 + 0.05 * jax.random.normal(k, shape, f32)

    u_dt = jax.random.uniform(ks[9], (DEPTH, SSD_HEADS), f32)
    dt0 = jnp.exp(u_dt * (math.log(0.1) - math.log(0.001)) + math.log(0.001))
    dt_bias = dt0 + jnp.log(-jnp.expm1(-dt0))
    a_log = jnp.log(jax.random.uniform(ks[10], (DEPTH, SSD_HEADS), f32, 1.0, 16.0))
    return {
        "x": jax.random.normal(ks[0], (BATCH, SEQ, D_MODEL), f32),
        "ff1_norm_pre": gain(ks[1], (DEPTH, D_MODEL)),
        "ff1_w_gate": nrm(ks[2], (DEPTH, D_MODEL, D_FF), D_MODEL),
        "ff1_w_up": nrm(ks[3], (DEPTH, D_MODEL, D_FF), D_MODEL),
        "ff1_w_down": nrm(ks[4], (DEPTH, D_FF, D_MODEL), D_FF),
        "ff1_norm_post": gain(ks[5], (DEPTH, D_MODEL)),
        "mix_norm_pre": gain(ks[6], (DEPTH, D_MODEL)),
        "w_in": nrm(ks[7], (DEPTH, D_MODEL, IN_COLS), D_MODEL),
        "conv_w": nrm(ks[8], (DEPTH, SSD_CONV, SSD_XBC), SSD_CONV),
        "conv_b": 0.01 * jax.random.normal(ks[11], (DEPTH, SSD_XBC), f32),
        "dt_bias": dt_bias,
        "a_log": a_log,
        "d_skip": gain(ks[12], (DEPTH, SSD_HEADS)),
        "ssd_norm": gain(ks[13], (DEPTH, SSD_INNER)),
        "pool_w": nrm(ks[14], (DEPTH, POOL_GROUPS, POOL_GROUP_DIM, POOL_GROUP_DIM), POOL_GROUP_DIM),
        "pool_scale": gain(ks[15], (DEPTH, POOL_WIDTH)),
        "w_out": nrm(ks[16], (DEPTH, D_MIX, D_MODEL), D_MIX),
        "mix_norm_post": gain(ks[17], (DEPTH, D_MODEL)),
        "ff2_norm_pre": gain(ks[18], (DEPTH, D_MODEL)),
        "ff2_w_gate": nrm(ks[19], (DEPTH, D_MODEL, D_FF), D_MODEL),
        "ff2_w_up": nrm(ks[20], (DEPTH, D_MODEL, D_FF), D_MODEL),
        "ff2_w_down": nrm(ks[21], (DEPTH, D_FF, D_MODEL), D_FF),
        "ff2_norm_post": gain(ks[22], (DEPTH, D_MODEL)),
    }


def reference(x, ff1_norm_pre, ff1_w_gate, ff1_w_up, ff1_w_down, ff1_norm_post,
              mix_norm_pre, w_in, conv_w, conv_b, dt_bias, a_log, d_skip, ssd_norm,
              pool_w, pool_scale, w_out, mix_norm_post,
              ff2_norm_pre, ff2_w_gate, ff2_w_up, ff2_w_down, ff2_norm_post):
    h = x
    for l in range(DEPTH):
        f = _swiglu(_rmsnorm(h, ff1_norm_pre[l]), ff1_w_gate[l], ff1_w_up[l], ff1_w_down[l])
        h = h + 0.5 * _rmsnorm(f, ff1_norm_post[l])
        m = _hybrid_mixer(_rmsnorm(h, mix_norm_pre[l]), w_in[l], conv_w[l], conv_b[l], dt_bias[l],
                          a_log[l], d_skip[l], ssd_norm[l], pool_w[l], pool_scale[l], w_out[l])
        h = h + _rmsnorm(m, mix_norm_post[l])
        f = _swiglu(_rmsnorm(h, ff2_norm_pre[l]), ff2_w_gate[l], ff2_w_up[l], ff2_w_down[l])
        h = h + 0.5 * _rmsnorm(f, ff2_norm_post[l])
    return h
```

```python
import numpy as np
import concourse.bass as bass
import concourse.mybir as mybir
from concourse.bass_utils import run_bass_kernel_spmd

F32 = mybir.dt.float32
BF16 = mybir.dt.bfloat16
AF = mybir.ActivationFunctionType
ALU = mybir.AluOpType
AX = mybir.AxisListType

S = 2048
D = 1024
DFF = 2816
NJ = DFF // 128
NC8 = D // 128
DEPTH = 2
EPS = 1e-6
TG = 1024
NTG = S // TG
NSG = TG // 512


class Op:
    __slots__ = ("eng", "fn", "deps", "signal", "count", "dkey", "ndma", "name", "epoch")


class Sched:
    ENGS = ("pe", "act", "dve", "pool", "sp")

    def __init__(self, nc):
        self.nc = nc
        self.ops = {e: [] for e in self.ENGS}
        self.last_writer = {}
        self.readers = {}
        self.dcount = {}
        self.nops = 0
        self.epoch = 0

    def add(self, eng, fn, reads=(), writes=(), dma=None, ndma=1, name=""):
        op = Op()
        op.eng = eng
        op.fn = fn
        op.signal = False
        op.count = None
        op.dkey = dma
        op.ndma = ndma
        op.name = name
        op.epoch = self.epoch
        if dma is not None:
            self.dcount[dma] = self.dcount.get(dma, 0) + 16 * ndma
            op.count = self.dcount[dma]
        deps = {}
        raw = set()
        for b in reads:
            w = self.last_writer.get(b)
            if w is not None:
                deps[id(w)] = w
                raw.add(id(w))
        for b in writes:
            w = self.last_writer.get(b)
            if w is not None:
                deps[id(w)] = w
            rd = self.readers.get(b)
            if rd:
                for r in rd.values():
                    if isinstance(r, list):
                        for rr in r:
                            deps[id(rr)] = rr
                    else:
                        deps[id(r)] = r
        fdeps = []
        for k, d in deps.items():
            if d.dkey is not None:
                fdeps.append(d)
                continue
            if d.eng == eng:
                if eng == "pe":
                    continue
                fdeps.append(d)
                continue
            fdeps.append(d)
        for d in fdeps:
            d.signal = True
        op.deps = fdeps
        for b in reads:
            rd = self.readers.setdefault(b, {})
            if dma is not None:
                rd.setdefault("dma", []).append(op)
            else:
                rd[eng] = op
        for b in writes:
            self.last_writer[b] = op
            self.readers[b] = {}
        self.ops[eng].append(op)
        self.nops += 1
        return op

    def emit(self):
        nc = self.nc
        esem = {(e, ep): nc.alloc_semaphore("sem_%s_%d" % (e, ep)) for e in self.ENGS for ep in range(self.epoch + 1)}
        dsem = {k: nc.alloc_semaphore("dsem_%d" % i) for i, k in enumerate(self.dcount)}
        for e in self.ENGS:
            c = {}
            for op in self.ops[e]:
                if op.dkey is None and op.signal:
                    c[op.epoch] = c.get(op.epoch, 0) + 1
                    op.count = c[op.epoch]
        ops = self.ops

        def run(eng_name, eng):
            waited = {}
            for op in ops[eng_name]:
                for d in op.deps:
                    if d.dkey is not None:
                        sem = dsem[d.dkey]
                    else:
                        sem = esem[(d.eng, d.epoch)]
                    if waited.get(sem.num, 0) < d.count:
                        eng.wait_ge(sem, d.count)
                        waited[sem.num] = d.count
                r = op.fn(eng)
                if op.name:
                    rr_ = r[-1] if isinstance(r, (list, tuple)) else r
                    rr_.annotate(op.name)
                if op.dkey is not None:
                    if not isinstance(r, (list, tuple)):
                        r = [r]
                    assert len(r) == op.ndma, (op.name, len(r), op.ndma)
                    for ins in r:
                        ins.then_inc(dsem[op.dkey], 16)
                elif op.signal:
                    if isinstance(r, (list, tuple)):
                        r = r[-1]
                    r.then_inc(esem[(eng_name, op.epoch)], 1)

        with nc.Block() as block:
            @block.tensor
            def _(e):
                run("pe", e)

            @block.scalar
            def _(e):
                run("act", e)

            @block.vector
            def _(e):
                run("dve", e)

            @block.gpsimd
            def _(e):
                run("pool", e)

            @block.sync
            def _(e):
                run("sp", e)


def _colmajor(v):
    return np.ascontiguousarray(v.reshape(-1, 128).T)


SM_OFF = {}


def pack_small(inp):
    cols = []
    off = 0

    def put(name, arr):
        nonlocal off
        SM_OFF[name] = (off, arr.shape[1])
        cols.append(arr.astype(np.float32))
        off += arr.shape[1]

    for l in range(DEPTH):
        for nm in ("ff1_norm_pre", "ff1_norm_post", "mix_norm_pre", "mix_norm_post", "ff2_norm_pre", "ff2_norm_post"):
            put("%s%d" % (nm, l), _colmajor(inp[nm][l]))
    for l in range(DEPTH):
        put("conv_b%d" % l, _colmajor(inp["conv_b"][l]))
        for j in range(4):
            put("conv_w%d_%d" % (j, l), _colmajor(inp["conv_w"][l, j]))
        put("ssd_norm%d" % l, _colmajor(inp["ssd_norm"][l]))
        put("pool_scale%d" % l, _colmajor(inp["pool_scale"][l]))
        for nm in ("dt_bias", "a_log", "d_skip"):
            put("%s%d" % (nm, l), np.broadcast_to(inp[nm][l][None, :], (128, 8)))
    return np.ascontiguousarray(np.concatenate(cols, axis=1))


NSMALL = 6 * DEPTH * 8 + DEPTH * (8 + 32 + 4 + 2 + 24)
WIN_PIECES = 10
TCOLS = list(range(0, 512)) + list(range(1800, 2312)) + list(range(2312, 2568)) + list(range(1544, 1800))


def make_consts():
    c = {}
    k = np.arange(128)
    c["tri"] = (k[:, None] <= k[None, :]).astype(np.float32)
    c["negmask"] = np.where(k[None, :] >= k[:, None], 0.0, -30000.0).astype(np.float32)
    inv_freq = 500000.0 ** (-np.arange(0, 16, 2, dtype=np.float32) / 16.0)
    pos = (np.arange(16)[None, :, None] * 128 + k[:, None, None]).astype(np.float32)
    ang = pos * inv_freq[None, None, :]
    c["rope"] = np.concatenate([np.cos(ang), np.sin(ang)], axis=-1).reshape(128, 16 * 16).astype(np.float32)
    bands = []
    for w in (2, 4, 8, 16):
        s_ = k[:, None]
        t_ = k[None, :]
        cur = ((s_ <= t_) & (t_ - s_ < w)).astype(np.float32) / w - (s_ == t_)
        cur0 = ((s_ <= t_) & (t_ - s_ < w)).astype(np.float32) / np.minimum(t_ + 1, w) - (s_ == t_)
        prev = (((t_ + 128 - s_) < w)).astype(np.float32) / w
        bands += [cur, cur0, prev]
    c["bands"] = np.concatenate(bands, axis=1).astype(np.float32)
    es = np.zeros((8, 8, 128), np.float32)
    for r in range(8):
        es[r, r, :] = 1.0
    c["esel"] = es.reshape(8, 1024)
    return c


def build(stage="all"):
    nc = bass.Bass("TRN2", target_bir_lowering=False)
    sc = Sched(nc)

    x_d = nc.dram_tensor("x", [S, D], F32, kind="ExternalInput").ap()
    out_d = nc.dram_tensor("out", [S, D], F32, kind="ExternalOutput").ap()
    small_d = nc.dram_tensor("small", [128, NSMALL], F32, kind="ExternalInput").ap()
    ident_d = nc.dram_tensor("ident", [128, 128], F32, kind="ExternalInput").ap()
    tri_d = nc.dram_tensor("tri", [128, 128], F32, kind="ExternalInput").ap()
    negmask_d = nc.dram_tensor("negmask", [128, 128], F32, kind="ExternalInput").ap()
    rope_d = nc.dram_tensor("rope", [128, 256], F32, kind="ExternalInput").ap()
    bands_d = nc.dram_tensor("bands", [128, 12 * 128], F32, kind="ExternalInput").ap()
    esel_d = nc.dram_tensor("esel", [8, 1024], F32, kind="ExternalInput").ap()
    wgu_d = nc.dram_tensor("wgu", [DEPTH * 2 * NJ * 128, 2 * 8 * 128], F32, kind="ExternalInput").ap()
    wd_d = nc.dram_tensor("wd", [DEPTH * 2 * NC8 * 128, NJ * 128], F32, kind="ExternalInput").ap()
    win_d = nc.dram_tensor("win", [DEPTH * WIN_PIECES * 128, 2048], F32, kind="ExternalInput").ap()
    wdt_d = nc.dram_tensor("wdt", [DEPTH * 128, 64], F32, kind="ExternalInput").ap()
    wout_d = nc.dram_tensor("wout", [DEPTH * 4 * 128, 2048], F32, kind="ExternalInput").ap()
    wp_d = nc.dram_tensor("wp", [DEPTH * 128, 256], F32, kind="ExternalInput").ap()

    def sb(name, shape, dt):
        return nc.alloc_sbuf_tensor(name, shape, dt)

    xT = sb("xT", [128, NC8, S], F32)
    small = sb("small_sb", [128, NSMALL], F32)
    g32 = sb("g32", [128, 6 * DEPTH * 8], F32)
    ident = sb("ident_sb", [128, 128], F32)
    ones_bf = sb("ones_bf", [128, 128], BF16)
    ones_f = sb("ones_f", [128, 128], F32)
    neghalf = sb("neghalf", [128, 512], F32)
    sq = [sb("sq%d" % i, [128, 512], BF16) for i in range(2)]
    rstd = [sb("rstd%d" % i, [128, 512], F32) for i in range(2)]
    tmp = [sb("tmp%d" % i, [128, 512], F32) for i in range(2)]
    tri = sb("tri_sb", [128, 128], F32)
    negmask = sb("negmask_sb", [128, 128], F32)
    cmask_bf = sb("cmask_bf", [128, 128], BF16)
    ident_bf = sb("ident_bf", [128, 128], BF16)
    rope = sb("rope_sb", [128, 256], F32)
    bands = sb("bands_sb", [128, 12 * 128], BF16)
    esel = sb("esel_sb", [8, 1024], BF16)
    ptmp = tmp
    ps = [nc.alloc_psum_tensor("ps%d" % i, [128, 512], F32) for i in range(8)]

    with nc.reset_on_exit():
        hnT = sb("hnT", [128, NC8, TG], BF16)
        aT = sb("aT", [128, NJ, TG], BF16)
        fT = sb("fT", [128, NC8, TG], F32)
        sil = [sb("sil%d" % i, [128, 512], F32) for i in range(2)]
        wgu = [sb("wgu%d" % i, [128, 2 * 8 * 128], BF16) for i in range(2)]
        wd = [sb("wd%d" % i, [128, NJ * 128], BF16) for i in range(2)]
        xio = [fT[:, i, :] for i in range(2)]
    hnTm = sb("hnTm", [128, NC8, 512], BF16)
    KT = sb("KT", [128, 2, S], BF16)
    vp = sb("vp", [128, 16, 4, 65], BF16)
    wsl = [sb("wsl%d" % i, [128, 2048], BF16) for i in range(2)]
    wdt = sb("wdt_sb", [128, 64], BF16)
    wp = sb("wp_sb", [128, 256], BF16)
    pre = [sb("pre%d" % i, [128, 515], F32) for i in range(2)]
    carry = sb("carry", [128, 8, 3], F32)
    cacc = [sb("cacc%d" % i, [128, 512], F32) for i in range(2)]
    xbcf = sb("xbcf", [128, 8, 512], F32)
    bcT = sb("bcT", [128, 4, 512], BF16)
    x_tm = sb("x_tm", [128, 4, 512], BF16)
    B_tm = sb("B_tm", [128, 4, 256], BF16)
    zs = sb("zs", [128, 4, 512], F32)
    dtt = sb("dtt", [128, 4, 8], F32)
    dAt = sb("dAt", [128, 4, 8], F32)
    Aneg = sb("Aneg", [128, 8], F32)
    qkf = sb("qkf", [128, 256], F32)
    ropet = [sb("ropet%d" % i, [128, 8, 8], F32) for i in range(4)]
    QT = sb("QT", [128, 2, 512], BF16)
    u_tm = sb("u_tm", [128, 5, 256], BF16)
    kmf = sb("kmf", [128, 2, 8], F32)
    kmT = sb("kmT", [128, 2, 8], BF16)
    state_f = sb("state_f", [128, 512], F32)
    state_b = sb("state_b", [128, 512], BF16)
    Dall = sb("Dall", [128, 1024], F32)
    seg = sb("seg", [128, 8, 128], BF16)
    MT = sb("MT", [128, 8, 128], BF16)
    Ein = sb("Ein", [128, 24], F32)
    Eout = sb("Eout", [128, 24], F32)
    negacs = sb("negacs", [128, 8], F32)
    xdt = sb("xdt", [128, 512], BF16)
    xdtd = sb("xdtd", [128, 512], BF16)
    y_sb = sb("y_sb", [128, 512], F32)
    y2 = sb("y2", [128, 512], F32)
    ssq = sb("ssq", [128, 4], F32)
    d_tm = sb("d_tm", [128, 256], F32)
    dT = sb("dT", [128, 2, 512], BF16)
    PT = [sb("PT%d" % i, [128, 512], BF16) for i in range(2)]
    gate = sb("gate", [128, 4, 8], F32)
    cmpb = sb("cmpb", [128, 4, 8, 8], F32)
    cnt = sb("cnt", [128, 4, 8], F32)
    bias_tm = sb("bias_tm", [128, 4, 4, 8], F32)
    biasT = [sb("biasT%d" % i, [8, 512], BF16) for i in range(2)]
    att_tm = sb("att_tm", [128, 4, 256], F32)
    rden = sb("rden", [128, 4], F32)
    ycatT = sb("ycatT", [128, NC8, 512], BF16)
    mT = xbcf

    def gsl(name):
        o, n = SM_OFF[name]
        return slice(o, o + n)

    def scol(name, i=0):
        o, n = SM_OFF[name]
        return small[:, o + i:o + i + 1]

    sc.add("sp", lambda e: e.dma_start(out=small[:], in_=small_d), writes=["small"], dma="small")
    sc.add("sp", lambda e: e.dma_start(out=ident[:], in_=ident_d), writes=["ident"], dma="ident")
    sc.add("sp", lambda e: e.dma_start(out=tri[:], in_=tri_d), writes=["tri"], dma="tri")
    sc.add("sp", lambda e: e.dma_start(out=negmask[:], in_=negmask_d), writes=["negmask"], dma="negmask")
    sc.add("sp", lambda e: e.dma_start(out=rope[:], in_=rope_d), writes=["rope"], dma="rope")
    sc.add("pool", lambda e: e.dma_start(out=bands[:], in_=bands_d), writes=["bands"], dma="bands")
    sc.add("pool", lambda e: e.dma_start(out=esel[:], in_=esel_d), writes=["esel"], dma="esel")
    sc.add("dve", lambda e: e.memset(ones_bf[:], 1.0), writes=["ones"])
    sc.add("dve", lambda e: e.memset(ones_f[:], 1.0), writes=["ones_f"])
    sc.add("pool", lambda e: e.memset(neghalf[:], -0.5), writes=["neghalf"])
    sc.add("dve", lambda e: e.tensor_copy(out=cmask_bf[:], in_=negmask[:]), reads=["negmask"], writes=["cmask"])
    sc.add("dve", lambda e: e.tensor_copy(out=ident_bf[:], in_=ident[:]), reads=["ident"], writes=["ident_bf"])
    for l in range(DEPTH):
        for i, (nm, coef) in enumerate((("ff1_norm_pre", 32.0), ("ff1_norm_post", 16.0), ("mix_norm_pre", 32.0),
                                        ("mix_norm_post", 32.0), ("ff2_norm_pre", 32.0), ("ff2_norm_post", 16.0))):
            s_ = gsl("%s%d" % (nm, l))
            sc.add("dve", lambda e, s_=s_, coef=coef: e.tensor_scalar(
                out=g32[:, s_], in0=small[:, s_], scalar1=coef, scalar2=0.0, op0=ALU.mult, op1=ALU.add),
                reads=["small"], writes=[("g32", s_.start)])

    def barrier():
        lasts = []
        for e_ in sc.ENGS:
            for o_ in reversed(sc.ops[e_]):
                if o_.dkey is None:
                    lasts.append(o_)
                    break
        sc.epoch += 1
        for e in ("pe", "act", "dve", "pool", "sp"):
            op = sc.add(e, lambda eng: eng.nop(), name="barrier")
            for d in lasts:
                if d is not op:
                    d.signal = True
                    op.deps.append(d)

    def xkey(c, t512):
        return ("xT", c, t512)

    for t in range(S // 128):
        slot = t % 2
        sc.add("sp", lambda e, t=t, slot=slot: e.dma_start(out=xio[slot], in_=x_d[t * 128:(t + 1) * 128, :]),
               writes=[("xio", slot)], dma=("xio", slot))
        for q in range(2):
            pb = 6 + q

            def tr(e, q=q, slot=slot, pb=pb):
                r = None
                for i in range(4):
                    c = 4 * q + i
                    r = e.transpose(out=ps[pb][:, i * 128:(i + 1) * 128], in_=xio[slot][:, c * 128:(c + 1) * 128],
                                    identity=ident[:])
                return r
            sc.add("pe", tr, reads=[("xio", slot), "ident"], writes=[("ps", pb)])
            sc.add("dve", lambda e, q=q, t=t, pb=pb: e.tensor_copy(
                out=xT[:, 4 * q:4 * q + 4, t * 128:(t + 1) * 128],
                in_=ps[pb][:].rearrange("p (c t) -> p c t", c=4)),
                reads=[("ps", pb)], writes=[xkey(4 * q + i, t // 4) for i in range(4)])

    barrier()

    def rstd_op(sg, pb):
        sc.add("dve", lambda e, sg=sg, pb=pb: e.tensor_scalar(
            out=tmp[sg][:], in0=ps[pb][:], scalar1=1024.0 * EPS, scalar2=1.0,
            op0=ALU.add, op1=ALU.mult),
            reads=[("ps", pb)], writes=[("tmp", sg)])
        sc.add("pool", lambda e, sg=sg: e.tensor_tensor(
            out=rstd[sg][:], in0=tmp[sg][:], in1=neghalf[:], op=ALU.pow),
            reads=[("tmp", sg), "neghalf"], writes=[("rstd", sg)])

    def prenorm(t512, sg, gpre, dst, dkey):
        cols = slice(t512 * 512, (t512 + 1) * 512)
        for c in range(NC8):
            s2 = c % 2
            sc.add("act", lambda e, c=c, cols=cols, s2=s2: e.activation(
                out=sq[s2][:], in_=xT[:, c, cols], func=AF.Square),
                reads=[xkey(c, t512)], writes=[("sq", s2)])
            sc.add("pe", lambda e, c=c, s2=s2, sg=sg: e.matmul(
                ps[6 + sg][:], lhsT=ones_bf[:], rhs=sq[s2][:], start=(c == 0), stop=(c == NC8 - 1)),
                reads=[("sq", s2), "ones"], writes=[("ps", 6 + sg)])
        rstd_op(sg, 6 + sg)
        for c in range(NC8):
            sc.add("dve", lambda e, c=c, sg=sg, cols=cols: e.scalar_tensor_tensor(
                out=dst(c), in0=xT[:, c, cols],
                scalar=g32[:, gpre.start + c:gpre.start + c + 1], in1=rstd[sg][:],
                op0=ALU.mult, op1=ALU.mult),
                reads=[xkey(c, t512), ("rstd", sg), ("g32", gpre.start)], writes=[dkey(c)])

    def postnorm_add(t512, sg, gpost, src, skey):
        cols = slice(t512 * 512, (t512 + 1) * 512)
        for m in range(NC8):
            tb = m % 2
            sc.add("dve", lambda e, m=m, sg=sg, tb=tb: e.scalar_tensor_tensor(
                out=cacc_or_tmp(tb), in0=src(m),
                scalar=g32[:, gpost.start + m:gpost.start + m + 1], in1=rstd[sg][:],
                op0=ALU.mult, op1=ALU.mult),
                reads=[skey(m), ("rstd", sg), ("g32", gpost.start)], writes=[("tmp", tb)])
            sc.add("pool", lambda e, m=m, cols=cols, tb=tb: e.tensor_tensor(
                out=xT[:, m, cols], in0=xT[:, m, cols], in1=cacc_or_tmp(tb), op=ALU.add),
                reads=[("tmp", tb), xkey(m, t512)], writes=[xkey(m, t512)])

    def cacc_or_tmp(tb):
        return ptmp[tb][:]

    def ffn(l, f, pre_name, post_name):
        gpre = gsl("%s%d" % (pre_name, l))
        gpost = gsl("%s%d" % (post_name, l))
        wrow = (l * 2 + f)
        for tg in range(NTG):
            for sg in range(NSG):
                prenorm(tg * NSG + sg, sg, gpre, lambda c, sg=sg: hnT[:, c, sg * 512:(sg + 1) * 512],
                        lambda c, sg=sg: ("hnT", c, sg))
            import os
            KCUT = int(os.environ.get("KCUT", "9")) if f == 1 else 9
            if KCUT < 1:
                continue
            it = 0
            for j in range(NJ):
                ws = j % 2
                r0 = (wrow * NJ + j) * 128
                sc.add("pool", lambda e, ws=ws, r0=r0: e.dma_start(out=wgu[ws][:], in_=wgu_d[r0:r0 + 128, :]),
                       writes=[("wgu", ws)], dma=("wgu", ws))
                for sg in range(NSG):
                    b = it % 2
                    it += 1

                    def mm(e, ws=ws, sg=sg, b=b):
                        r = None
                        for g in range(2):
                            for k in range(NC8):
                                r = e.matmul(ps[2 * g + b][:], lhsT=wgu[ws][:, (g * 8 + k) * 128:(g * 8 + k + 1) * 128],
                                             rhs=hnT[:, k, sg * 512:(sg + 1) * 512], start=(k == 0), stop=(k == NC8 - 1))
                        return r
                    sc.add("pe", mm, reads=[("wgu", ws)] + [("hnT", k, sg) for k in range(NC8)],
                           writes=[("ps", b), ("ps", 2 + b)])
                    sc.add("act", lambda e, b=b: e.activation(out=sil[b][:], in_=ps[b][:], func=AF.Silu),
                           reads=[("ps", b)], writes=[("sil", b)])
                    sc.add("dve", lambda e, b=b, j=j, sg=sg: e.tensor_tensor(
                        out=aT[:, j, sg * 512:(sg + 1) * 512], in0=sil[b][:], in1=ps[2 + b][:], op=ALU.mult),
                        reads=[("sil", b), ("ps", 2 + b)], writes=[("aT", j, sg)])
            if KCUT < 2:
                continue
            it = 0
            for m in range(NC8):
                ws = m % 2
                r0 = (wrow * NC8 + m) * 128
                hw_ = NJ * 64
                sc.add("pool", lambda e, ws=ws, r0=r0, hw_=hw_: [
                    e.dma_start(out=wd[ws][:, h * hw_:(h + 1) * hw_], in_=wd_d[r0:r0 + 128, h * hw_:(h + 1) * hw_])
                    for h in range(2)],
                       writes=[("wd", ws)], dma=("wd", ws), ndma=2)
                for sg in range(NSG):
                    b = 4 + (it % 2)
                    it += 1

                    def mm2(e, ws=ws, sg=sg, b=b):
                        r = None
                        for j in range(NJ):
                            r = e.matmul(ps[b][:], lhsT=wd[ws][:, j * 128:(j + 1) * 128],
                                         rhs=aT[:, j, sg * 512:(sg + 1) * 512], start=(j == 0), stop=(j == NJ - 1))
                        return r
                    sc.add("pe", mm2, reads=[("wd", ws)] + [("aT", j, sg) for j in range(NJ)], writes=[("ps", b)])
                    s2 = it % 2
                    sc.add("dve", lambda e, b=b, m=m, sg=sg: e.tensor_copy(
                        out=fT[:, m, sg * 512:(sg + 1) * 512], in_=ps[b][:]),
                        reads=[("ps", b)], writes=[("fT", m, sg)])
                    sc.add("act", lambda e, m=m, sg=sg, s2=s2: e.activation(
                        out=sq[s2][:], in_=fT[:, m, sg * 512:(sg + 1) * 512], func=AF.Square),
                        reads=[("fT", m, sg)], writes=[("sq", s2)])
                    sc.add("pe", lambda e, s2=s2, sg=sg, m=m: e.matmul(
                        ps[6 + sg][:], lhsT=ones_bf[:], rhs=sq[s2][:], start=(m == 0), stop=(m == NC8 - 1)),
                        reads=[("sq", s2), "ones"], writes=[("ps", 6 + sg)])
            for sg in range(NSG):
                rstd_op(sg, 6 + sg)
                postnorm_add(tg * NSG + sg, sg, gpost, lambda m, sg=sg: fT[:, m, sg * 512:(sg + 1) * 512],
                             lambda m, sg=sg: ("fT", m, sg))

    def mixer(l):
        gpre = gsl("mix_norm_pre%d" % l)
        gpost = gsl("mix_norm_post%d" % l)
        psb = [0]

        def nb2(pair):
            psb[0] += 1
            return pair * 2 + (psb[0] % 2)

        wcnt = [0]

        def load_w(src_ap):
            ws = wcnt[0] % 2
            wcnt[0] += 1
            sc.add("pool", lambda e, ws=ws, src_ap=src_ap: e.dma_start(out=wsl[ws][:], in_=src_ap),
                   writes=[("wsl", ws)], dma=("wsl", ws))
            return ws

        sc.add("pool", lambda e: e.dma_start(out=wdt[:], in_=wdt_d[l * 128:(l + 1) * 128, :]), writes=["wdt"], dma="wdt")
        sc.add("pool", lambda e: e.dma_start(out=wp[:], in_=wp_d[l * 128:(l + 1) * 128, :]), writes=["wp"], dma="wp")
        sc.add("dve", lambda e: e.memset(vp[:], 1.0), writes=["vp_init"] + [("vp", t_) for t_ in range(16)])
        sc.add("dve", lambda e: e.memset(carry[:], 0.0), writes=[("carry", c) for c in range(8)])
        sc.add("dve", lambda e: e.memset(state_f[:], 0.0), writes=["state_f"])
        sc.add("dve", lambda e: e.memset(state_b[:], 0.0), writes=["state_b"])
        sc.add("dve", lambda e: e.memset(u_tm[:, 4, :], 0.0), writes=[("u_tm", 4)])
        sc.add("act", lambda e: e.activation(out=Aneg[:], in_=small[:, gsl("a_log%d" % l)], func=AF.Exp),
               reads=["small"], writes=["Aneg0"])
        sc.add("dve", lambda e: e.tensor_scalar(out=Aneg[:], in0=Aneg[:], scalar1=-1.0, scalar2=0.0,
                                                op0=ALU.mult, op1=ALU.add), reads=["Aneg0"], writes=["Aneg"])

        for grp in range(4):
            t0 = grp * 4
            prenorm(grp, 0, gpre, lambda c: hnTm[:, c, :], lambda c: ("hnTm", c))
            if grp > 0:
                sc.add("pool", lambda e: e.tensor_copy(out=u_tm[:, 4, :], in_=u_tm[:, 3, :]),
                       reads=[("u_tm", 3)], writes=[("u_tm", 4)])
            for fp in range(4):
                ws = load_w(win_d[(l * WIN_PIECES + fp) * 128:(l * WIN_PIECES + fp + 1) * 128, :])
                for cc in range(2):
                    ch = fp * 2 + cc
                    pb = nb2(0)

                    def mmf(e, ws=ws, cc=cc, pb=pb):
                        r = None
                        for k in range(NC8):
                            r = e.matmul(ps[pb][:], lhsT=wsl[ws][:, (cc * 8 + k) * 128:(cc * 8 + k + 1) * 128],
                                         rhs=hnTm[:, k, :], start=(k == 0), stop=(k == NC8 - 1))
                        return r
                    sc.add("pe", mmf, reads=[("wsl", ws)] + [("hnTm", k) for k in range(NC8)], writes=[("ps", pb)])
                    pr = ch % 2
                    sc.add("dve", lambda e, pr=pr, ch=ch: e.tensor_copy(out=pre[pr][:, 0:3], in_=carry[:, ch, :]),
                           reads=[("carry", ch)], writes=[("pre", pr, 0)])
                    sc.add("act", lambda e, pr=pr, pb=pb: e.copy(out=pre[pr][:, 3:515], in_=ps[pb][:]),
                           reads=[("ps", pb)], writes=[("pre", pr, 1)])
                    sc.add("dve", lambda e, pr=pr, ch=ch: e.tensor_copy(out=carry[:, ch, :], in_=pre[pr][:, 512:515]),
                           reads=[("pre", pr, 1)], writes=[("carry", ch)])
                    sc.add("dve", lambda e, pr=pr, ch=ch: e.tensor_scalar(
                        out=cacc[pr][:], in0=pre[pr][:, 0:512], scalar1=scol("conv_w0_%d" % l, ch), scalar2=0.0,
                        op0=ALU.mult, op1=ALU.add),
                        reads=[("pre", pr, 0), ("pre", pr, 1), "small"], writes=[("cacc", pr)])
                    for j in range(1, 4):
                        sc.add("dve", lambda e, pr=pr, ch=ch, j=j: e.scalar_tensor_tensor(
                            out=cacc[pr][:], in0=pre[pr][:, j:j + 512], scalar=scol("conv_w%d_%d" % (j, l), ch),
                            in1=cacc[pr][:], op0=ALU.mult, op1=ALU.add),
                            reads=[("pre", pr, 0), ("pre", pr, 1), ("cacc", pr), "small"], writes=[("cacc", pr)])
                    sc.add("act", lambda e, pr=pr, ch=ch: e.activation(
                        out=xbcf[:, ch, :], in_=cacc[pr][:], func=AF.Silu, bias=scol("conv_b%d" % l, ch), scale=1.0),
                        reads=[("cacc", pr), "small"], writes=[("xbcf", ch)])
                    if ch >= 4:
                        sc.add("pool", lambda e, ch=ch: e.tensor_copy(out=bcT[:, ch - 4, :], in_=xbcf[:, ch, :]),
                               reads=[("xbcf", ch)], writes=[("bcT", ch - 4)])
            for ti in range(4):
                for half in range(2):
                    pb = nb2(1)
                    chs = [0, 1, 2, 3] if half == 0 else [4, 5]

                    def trx(e, ti=ti, chs=chs, pb=pb):
                        r = None
                        for i, ch in enumerate(chs):
                            r = e.transpose(out=ps[pb][:, i * 128:(i + 1) * 128],
                                            in_=xbcf[:, ch, ti * 128:(ti + 1) * 128], identity=ident[:])
                        return r
                    sc.add("pe", trx, reads=[("xbcf", ch) for ch in chs] + ["ident"], writes=[("ps", pb)])
                    if half == 0:
                        sc.add("act", lambda e, ti=ti, pb=pb: e.copy(out=x_tm[:, ti, :], in_=ps[pb][:]),
                               reads=[("ps", pb)], writes=[("x_tm", ti)])
                    else:
                        sc.add("act", lambda e, ti=ti, pb=pb: e.copy(out=B_tm[:, ti, :], in_=ps[pb][:, 0:256]),
                               reads=[("ps", pb)], writes=[("B_tm", ti)])
            for tp in range(6):
                ws = load_w(win_d[(l * WIN_PIECES + 4 + tp) * 128:(l * WIN_PIECES + 5 + tp) * 128, :])
                for ti in range(4):
                    pb = nb2(0)

                    def mmt(e, ws=ws, ti=ti, pb=pb):
                        r = None
                        for k in range(NC8):
                            r = e.matmul(ps[pb][:, 0:256], lhsT=hnTm[:, k, ti * 128:(ti + 1) * 128],
                                         rhs=wsl[ws][:, k * 256:(k + 1) * 256], start=(k == 0), stop=(k == NC8 - 1))
                        return r
                    sc.add("pe", mmt, reads=[("wsl", ws)] + [("hnTm", k) for k in range(NC8)], writes=[("ps", pb)])
                    if tp < 2:
                        sc.add("act", lambda e, ti=ti, tp=tp, pb=pb: e.activation(
                            out=zs[:, ti, tp * 256:(tp + 1) * 256], in_=ps[pb][:, 0:256], func=AF.Silu),
                            reads=[("ps", pb)], writes=[("zs", ti, tp)])
                    elif tp < 4:
                        isk = tp - 2
                        sc.add("act", lambda e, pb=pb: e.copy(out=qkf[:, 0:256], in_=ps[pb][:, 0:256]),
                               reads=[("ps", pb)], writes=["qkf"])
                        tile = t0 + ti
                        qv = qkf[:, 0:256].rearrange("p (h d) -> p h d", h=4)
                        cosb = rope[:, tile * 16:tile * 16 + 8].unsqueeze(1).broadcast_to([128, 4, 8])
                        sinb = rope[:, tile * 16 + 8:tile * 16 + 16].unsqueeze(1).broadcast_to([128, 4, 8])
                        x1 = qv[:, :, 0:8]
                        x2 = qv[:, :, 8:16]
                        r_ = [ropet[i][:, 0:4, :] for i in range(4)]
                        sc.add("dve", lambda e, x1=x1, cosb=cosb, r_=r_: e.tensor_tensor(out=r_[0], in0=x1, in1=cosb, op=ALU.mult),
                               reads=["qkf", "rope"], writes=[("ropet", 0)])
                        sc.add("dve", lambda e, x2=x2, sinb=sinb, r_=r_: e.tensor_tensor(out=r_[1], in0=x2, in1=sinb, op=ALU.mult),
                               reads=["qkf", "rope"], writes=[("ropet", 1)])
                        sc.add("dve", lambda e, x2=x2, cosb=cosb, r_=r_: e.tensor_tensor(out=r_[2], in0=x2, in1=cosb, op=ALU.mult),
                               reads=["qkf", "rope"], writes=[("ropet", 2)])
                        sc.add("dve", lambda e, x1=x1, sinb=sinb, r_=r_: e.tensor_tensor(out=r_[3], in0=x1, in1=sinb, op=ALU.mult),
                               reads=["qkf", "rope"], writes=[("ropet", 3)])
                        sc.add("dve", lambda e, x1=x1, r_=r_: e.tensor_tensor(out=x1, in0=r_[0], in1=r_[1], op=ALU.subtract),
                               reads=[("ropet", 0), ("ropet", 1)], writes=["qkf"])
                        sc.add("dve", lambda e, x2=x2, r_=r_: e.tensor_tensor(out=x2, in0=r_[2], in1=r_[3], op=ALU.add),
                               reads=[("ropet", 2), ("ropet", 3), "qkf"], writes=["qkf"])
                        pb2 = nb2(1)

                        def trq(e, pb2=pb2):
                            r = None
                            for hp in range(2):
                                r = e.transpose(out=ps[pb2][:, hp * 128:(hp + 1) * 128],
                                                in_=qkf[:, hp * 128:(hp + 1) * 128], identity=ident[:])
                            return r
                        sc.add("pe", trq, reads=["qkf", "ident"], writes=[("ps", pb2)])
                        if isk:
                            sc.add("act", lambda e, pb2=pb2, tile=tile: e.copy(
                                out=KT[:, :, tile * 128:(tile + 1) * 128],
                                in_=ps[pb2][:, 0:256].rearrange("p (a t) -> p a t", a=2)),
                                reads=[("ps", pb2)], writes=[("KT", tile)])
                        else:
                            sc.add("act", lambda e, pb2=pb2, ti=ti: e.copy(
                                out=QT[:, :, ti * 128:(ti + 1) * 128],
                                in_=ps[pb2][:, 0:256].rearrange("p (a t) -> p a t", a=2)),
                                reads=[("ps", pb2)], writes=[("QT", ti)])
                    elif tp == 4:
                        tile = t0 + ti
                        sc.add("act", lambda e, pb=pb, tile=tile: e.copy(
                            out=vp[:, tile, :, 0:64], in_=ps[pb][:, 0:256].rearrange("p (h d) -> p h d", h=4)),
                            reads=[("ps", pb), "vp_init"], writes=[("vp", tile)])
                    else:
                        sc.add("act", lambda e, pb=pb, ti=ti: e.copy(out=u_tm[:, ti, :], in_=ps[pb][:, 0:256]),
                               reads=[("ps", pb)], writes=[("u_tm", ti)])
            for ti in range(4):
                pb = nb2(0)

                def mmd(e, ti=ti, pb=pb):
                    r = None
                    for k in range(NC8):
                        r = e.matmul(ps[pb][:, 0:8], lhsT=hnTm[:, k, ti * 128:(ti + 1) * 128],
                                     rhs=wdt[:, k * 8:(k + 1) * 8], start=(k == 0), stop=(k == NC8 - 1))
                    return r
                sc.add("pe", mmd, reads=["wdt"] + [("hnTm", k) for k in range(NC8)], writes=[("ps", pb)])
                sc.add("dve", lambda e, ti=ti, pb=pb: e.tensor_tensor(
                    out=dtt[:, ti, :], in0=ps[pb][:, 0:8], in1=small[:, gsl("dt_bias%d" % l)], op=ALU.add),
                    reads=[("ps", pb), "small"], writes=[("dtt", ti)])
            sc.add("act", lambda e: e.activation(out=dtt[:], in_=dtt[:], func=AF.Exp),
                   reads=[("dtt", ti) for ti in range(4)], writes=["dtt_e"])
            sc.add("act", lambda e: e.activation(out=dtt[:], in_=dtt[:], func=AF.Ln, bias=1.0, scale=1.0),
                   reads=["dtt_e"], writes=["dtt_f"])
            sc.add("dve", lambda e: e.tensor_tensor(
                out=dAt[:], in0=dtt[:], in1=Aneg[:].unsqueeze(1).broadcast_to([128, 4, 8]), op=ALU.mult),
                reads=["dtt_f", "Aneg"], writes=["dAt"])

            for bi in range(2):
                blk = grp * 2 + bi
                sc.add("dve", lambda e, blk=blk: e.tensor_reduce(
                    out=kmf[:, :, blk], in_=KT[:, :, blk * 256:(blk + 1) * 256], axis=AX.X, op=ALU.add),
                    reads=[("KT", 2 * blk), ("KT", 2 * blk + 1)], writes=[("kmf", blk)])
                sc.add("dve", lambda e, blk=blk: e.tensor_scalar(
                    out=kmT[:, :, blk:blk + 1], in0=kmf[:, :, blk:blk + 1], scalar1=1.0 / 256, scalar2=0.0,
                    op0=ALU.mult, op1=ALU.add),
                    reads=[("kmf", blk)], writes=[("kmT", blk)])

            for ci in range(4):
                sc.add("dve", lambda e, ci=ci: e.tensor_tensor(
                    out=Dall[:].rearrange("p (h l) -> p h l", h=8), in0=dAt[:, ci, :].unsqueeze(2).broadcast_to([128, 8, 128]),
                    in1=tri[:].unsqueeze(1).broadcast_to([128, 8, 128]), op=ALU.mult),
                    reads=["dAt", "tri"], writes=["Dall"])

                def mm_acs(e, ci=ci):
                    e.matmul(ps[2][:, 0:8], lhsT=tri[:], rhs=dAt[:, ci, :], start=True, stop=True)
                    return e.matmul(ps[2][:, 8:16], lhsT=ones_f[:], rhs=dAt[:, ci, :], start=True, stop=True)
                sc.add("pe", mm_acs, reads=["dAt", "tri", "ones_f"], writes=[("ps", 2)])

                def mm_row(e):
                    r = None
                    for h in range(8):
                        reg = ps[4 + h // 4][:, (h % 4) * 128:(h % 4 + 1) * 128]
                        e.matmul(reg, lhsT=ones_f[:], rhs=Dall[:, h * 128:(h + 1) * 128], start=True, stop=False)
                        r = e.matmul(reg, lhsT=ident[:], rhs=negmask[:], start=False, stop=True)
                    return r
                sc.add("pe", mm_row, reads=["Dall", "ones_f", "ident", "negmask"], writes=[("ps", 4), ("ps", 5)])
                pbc = 3

                def mm_cb(e, ci=ci):
                    r = None
                    for g in range(2):
                        r = e.matmul(ps[pbc][:, g * 128:(g + 1) * 128], lhsT=bcT[:, g, ci * 128:(ci + 1) * 128],
                                     rhs=bcT[:, 2 + g, ci * 128:(ci + 1) * 128], start=True, stop=True)
                    return r
                sc.add("pe", mm_cb, reads=[("bcT", i) for i in range(4)], writes=[("ps", 3)])
                sc.add("dve", lambda e: e.tensor_copy(out=Ein[:, 0:16], in_=ps[2][:, 0:16]),
                       reads=[("ps", 2)], writes=["Ein0"])
                sc.add("dve", lambda e: e.tensor_scalar(out=negacs[:], in0=Ein[:, 0:8], scalar1=-1.0, scalar2=0.0,
                                                        op0=ALU.mult, op1=ALU.add), reads=["Ein0"], writes=["negacs"])
                sc.add("dve", lambda e: e.tensor_tensor(out=Ein[:, 16:24], in0=Ein[:, 8:16], in1=Ein[:, 0:8], op=ALU.subtract),
                       reads=["Ein0"], writes=["Ein1"])
                sc.add("act", lambda e: e.activation(out=Eout[:], in_=Ein[:], func=AF.Exp),
                       reads=["Ein0", "Ein1"], writes=["Eout"])
                for h in range(8):
                    sc.add("act", lambda e, h=h: e.activation(
                        out=seg[:, h, :], in_=ps[4 + h // 4][:, (h % 4) * 128:(h % 4 + 1) * 128], func=AF.Exp,
                        bias=negacs[:, h:h + 1], scale=1.0),
                        reads=[("ps", 4 + h // 4), "negacs"], writes=[("seg", h)])
                for h in range(8):
                    sc.add("dve", lambda e, h=h: e.tensor_tensor(
                        out=MT[:, h, :], in0=seg[:, h, :], in1=ps[pbc][:, (h // 4) * 128:(h // 4 + 1) * 128], op=ALU.mult),
                        reads=[("seg", h), ("ps", 3)], writes=[("MT", h)])
                xv = x_tm[:, ci, :].rearrange("p (h d) -> p h d", h=8)
                sc.add("dve", lambda e, ci=ci, xv=xv: e.tensor_tensor(
                    out=xdt[:].rearrange("p (h d) -> p h d", h=8), in0=xv,
                    in1=dtt[:, ci, :].unsqueeze(2).broadcast_to([128, 8, 64]), op=ALU.mult),
                    reads=[("x_tm", ci), "dtt_f"], writes=["xdt"])
                sc.add("dve", lambda e: e.tensor_tensor(
                    out=xdtd[:].rearrange("p (h d) -> p h d", h=8), in0=xdt[:].rearrange("p (h d) -> p h d", h=8),
                    in1=Eout[:, 16:24].unsqueeze(2).broadcast_to([128, 8, 64]), op=ALU.mult),
                    reads=["xdt", "Eout"], writes=["xdtd"])

                def mm_y(e):
                    r = None
                    for h in range(8):
                        r = e.matmul(ps[6][:, h * 64:(h + 1) * 64], lhsT=MT[:, h, :], rhs=xdt[:, h * 64:(h + 1) * 64],
                                     start=True, stop=True)
                    return r
                sc.add("pe", mm_y, reads=[("MT", h) for h in range(8)] + ["xdt"], writes=[("ps", 6)])

                def mm_off(e, ci=ci):
                    r = None
                    for g in range(2):
                        r = e.matmul(ps[7][:, g * 256:(g + 1) * 256], lhsT=bcT[:, 2 + g, ci * 128:(ci + 1) * 128],
                                     rhs=state_b[:, g * 256:(g + 1) * 256], start=True, stop=True)
                    return r
                sc.add("pe", mm_off, reads=[("bcT", 2), ("bcT", 3), "state_b"], writes=[("ps", 7)])
                sc.add("dve", lambda e: e.tensor_tensor(
                    out=y_sb[:].rearrange("p (h d) -> p h d", h=8), in0=ps[7][:].rearrange("p (h d) -> p h d", h=8),
                    in1=Eout[:, 0:8].unsqueeze(2).broadcast_to([128, 8, 64]), op=ALU.mult),
                    reads=[("ps", 7), "Eout"], writes=["y_sb"])
                sc.add("dve", lambda e: e.tensor_tensor(out=y_sb[:], in0=y_sb[:], in1=ps[6][:], op=ALU.add),
                       reads=[("ps", 6), "y_sb"], writes=["y_sb"])
                sc.add("dve", lambda e, xv=xv: e.tensor_tensor(
                    out=y2[:].rearrange("p (h d) -> p h d", h=8), in0=xv,
                    in1=small[:, gsl("d_skip%d" % l)].unsqueeze(2).broadcast_to([128, 8, 64]), op=ALU.mult),
                    reads=[("x_tm", ci), "small"], writes=["y2"])
                sc.add("dve", lambda e: e.tensor_tensor(out=y_sb[:], in0=y_sb[:], in1=y2[:], op=ALU.add),
                       reads=["y_sb", "y2"], writes=["y_sb"])
                sc.add("dve", lambda e, ci=ci: e.tensor_tensor(out=y_sb[:], in0=y_sb[:], in1=zs[:, ci, :], op=ALU.mult),
                       reads=["y_sb", ("zs", ci, 0), ("zs", ci, 1)], writes=["y_sb"])
                sc.add("dve", lambda e: e.tensor_tensor(out=y2[:], in0=y_sb[:], in1=y_sb[:], op=ALU.mult),
                       reads=["y_sb", "y2"], writes=["y2"])
                sc.add("dve", lambda e: e.tensor_reduce(out=ssq[:, 0:1], in_=y2[:], axis=AX.X, op=ALU.add),
                       reads=["y2"], writes=["ssq0"])
                sc.add("dve", lambda e: e.tensor_scalar(out=ssq[:, 1:2], in0=ssq[:, 0:1], scalar1=1.0 / 512, scalar2=EPS,
                                                        op0=ALU.mult, op1=ALU.add), reads=["ssq0"], writes=["ssq1"])
                sc.add("pool", lambda e: e.tensor_tensor(out=ssq[:, 2:3], in0=ssq[:, 1:2], in1=neghalf[:, 0:1], op=ALU.pow),
                       reads=["ssq1", "neghalf"], writes=["ssq2"])
                sc.add("dve", lambda e: e.tensor_scalar(out=y_sb[:], in0=y_sb[:], scalar1=ssq[:, 2:3], scalar2=0.0,
                                                        op0=ALU.mult, op1=ALU.add), reads=["y_sb", "ssq2"], writes=["y_sb"])
                pb = nb2(0)

                def tr_y(e, pb=pb):
                    r = None
                    for i in range(4):
                        r = e.transpose(out=ps[pb][:, i * 128:(i + 1) * 128], in_=y_sb[:, i * 128:(i + 1) * 128],
                                        identity=ident[:])
                    return r
                sc.add("pe", tr_y, reads=["y_sb", "ident"], writes=[("ps", pb)])
                for i in range(4):
                    sc.add("act", lambda e, i=i, ci=ci, pb=pb: e.activation(
                        out=ycatT[:, i, ci * 128:(ci + 1) * 128], in_=ps[pb][:, i * 128:(i + 1) * 128], func=AF.Copy,
                        scale=scol("ssd_norm%d" % l, i)),
                        reads=[("ps", pb), "small"], writes=[("ycatT", i, ci)])
                pbs = 2

                def mm_st(e, ci=ci):
                    r = None
                    for g in range(2):
                        r = e.matmul(ps[1][:, g * 256:(g + 1) * 256], lhsT=B_tm[:, ci, g * 128:(g + 1) * 128],
                                     rhs=xdtd[:, g * 256:(g + 1) * 256], start=True, stop=True)
                    return r
                sc.add("pe", mm_st, reads=[("B_tm", ci), "xdtd"], writes=[("ps", 1)])
                sc.add("dve", lambda e: e.tensor_tensor(
                    out=state_f[:].rearrange("p (h d) -> p h d", h=8), in0=state_f[:].rearrange("p (h d) -> p h d", h=8),
                    in1=Eout[:, 8:16].unsqueeze(2).broadcast_to([128, 8, 64]), op=ALU.mult),
                    reads=["state_f", "Eout", "state_b"], writes=["state_f"])
                sc.add("dve", lambda e: e.tensor_tensor(out=state_f[:], in0=state_f[:], in1=ps[1][:], op=ALU.add),
                       reads=["state_f", ("ps", 1)], writes=["state_f"])
                sc.add("pool", lambda e: e.tensor_copy(out=state_b[:], in_=state_f[:]),
                       reads=["state_f"], writes=["state_b"])

            for ti in range(4):
                tile = t0 + ti
                pb = nb2(1)

                def mm_pool(e, ti=ti, tile=tile, pb=pb):
                    r = None
                    for g in range(4):
                        bc = (3 * g + (1 if tile == 0 else 0)) * 128
                        bp = (3 * g + 2) * 128
                        first = tile == 0
                        r = e.matmul(ps[pb][:, g * 64:(g + 1) * 64], lhsT=bands[:, bc:bc + 128],
                                     rhs=u_tm[:, ti, g * 64:(g + 1) * 64], start=True, stop=first)
                        if not first:
                            prev = ti - 1 if ti > 0 else 4
                            r = e.matmul(ps[pb][:, g * 64:(g + 1) * 64], lhsT=bands[:, bp:bp + 128],
                                         rhs=u_tm[:, prev, g * 64:(g + 1) * 64], start=False, stop=True)
                    return r
                sc.add("pe", mm_pool, reads=["bands", ("u_tm", ti), ("u_tm", ti - 1 if ti > 0 else 4)], writes=[("ps", pb)])
                sc.add("act", lambda e, pb=pb: e.copy(out=d_tm[:], in_=ps[pb][:, 0:256]),
                       reads=[("ps", pb)], writes=["d_tm"])
                pb2 = nb2(1)

                def tr_d(e, pb2=pb2):
                    r = None
                    for i in range(2):
                        r = e.transpose(out=ps[pb2][:, i * 128:(i + 1) * 128], in_=d_tm[:, i * 128:(i + 1) * 128],
                                        identity=ident[:])
                    return r
                sc.add("pe", tr_d, reads=["d_tm", "ident"], writes=[("ps", pb2)])
                sc.add("act", lambda e, pb2=pb2, ti=ti: e.copy(
                    out=dT[:, :, ti * 128:(ti + 1) * 128], in_=ps[pb2][:, 0:256].rearrange("p (a t) -> p a t", a=2)),
                    reads=[("ps", pb2)], writes=[("dT", ti)])
            for pair in range(2):
                pb = nb2(0)
                sc.add("pe", lambda e, pair=pair, pb=pb: e.matmul(
                    ps[pb][:], lhsT=wp[:, pair * 128:(pair + 1) * 128], rhs=dT[:, pair, :], start=True, stop=True),
                    reads=["wp"] + [("dT", ti) for ti in range(4)], writes=[("ps", pb)])
                sc.add("act", lambda e, pair=pair, pb=pb: e.activation(
                    out=ycatT[:, 4 + pair, :], in_=ps[pb][:], func=AF.Copy, scale=scol("pool_scale%d" % l, pair)),
                    reads=[("ps", pb), "small"], writes=[("ycatT", 4 + pair, ci) for ci in range(4)])

            need_sel = (grp * 2) >= 4
            if need_sel:
                for ti in range(4):
                    qb = (t0 + ti) // 2
                    pb = nb2(1)

                    def mm_gate(e, ti=ti, qb=qb, pb=pb):
                        r = None
                        for h in range(4):
                            rows = slice((h % 2) * 64, (h % 2) * 64 + 64)
                            r = e.matmul(ps[pb][:, h * 8:h * 8 + qb], lhsT=QT[rows, h // 2, ti * 128:(ti + 1) * 128],
                                         rhs=kmT[rows, h // 2, 0:qb], start=True, stop=True)
                        return r
                    sc.add("pe", mm_gate, reads=[("QT", ti)] + [("kmT", b) for b in range(qb)], writes=[("ps", pb)])
                    sc.add("dve", lambda e, pb=pb, qb=qb: e.tensor_copy(
                        out=gate[:, :, 0:qb], in_=ps[pb][:, 0:32].rearrange("p (h n) -> p h n", h=4)[:, :, 0:qb]),
                        reads=[("ps", pb)], writes=["gate"])
                    sc.add("dve", lambda e, qb=qb: e.tensor_tensor(
                        out=cmpb[:, :, 0:qb, 0:qb],
                        in0=gate[:, :, 0:qb].unsqueeze(2).broadcast_to([128, 4, qb, qb]),
                        in1=gate[:, :, 0:qb].unsqueeze(3).broadcast_to([128, 4, qb, qb]), op=ALU.is_gt),
                        reads=["gate"], writes=["cmpb"])
                    sc.add("dve", lambda e, qb=qb: e.tensor_reduce(
                        out=cnt[:, :, 0:qb], in_=cmpb[:, :, 0:qb, 0:qb], axis=AX.X, op=ALU.add),
                        reads=["cmpb"], writes=["cnt"])
                    sc.add("dve", lambda e, ti=ti: e.memset(bias_tm[:, ti, :, :], 0.0), writes=[("bias_tm", ti)])
                    sc.add("dve", lambda e, ti=ti, qb=qb: e.tensor_scalar(
                        out=bias_tm[:, ti, :, 0:qb], in0=cnt[:, :, 0:qb], scalar1=2.5, scalar2=-30000.0,
                        op0=ALU.is_gt, op1=ALU.mult),
                        reads=["cnt", ("bias_tm", ti)], writes=[("bias_tm", ti)])
            for h in range(4):
                hp = h // 2
                rows = slice((h % 2) * 64, (h % 2) * 64 + 64)
                bs = h % 2
                if need_sel:
                    pb = nb2(1)

                    def tr_b(e, h=h, pb=pb):
                        r = None
                        for ti in range(4):
                            r = e.transpose(out=ps[pb][0:8, ti * 128:(ti + 1) * 128], in_=bias_tm[:, ti, h, :],
                                            identity=ident[:])
                        return r
                    sc.add("pe", tr_b, reads=[("bias_tm", ti) for ti in range(4)] + ["ident"], writes=[("ps", pb)])
                    sc.add("act", lambda e, pb=pb, bs=bs: e.copy(out=biasT[bs][:], in_=ps[pb][0:8, :]),
                           reads=[("ps", pb)], writes=[("biasT", bs)])
                acc = 6 + (h % 2)
                for kt in range(t0 + 4):
                    kb = kt // 2
                    if kt < t0:
                        c0 = 0
                    else:
                        c0 = (kt - t0) * 128
                    pb = 4 + (kt % 2)
                    past_cols = None
                    if need_sel:
                        if kb < grp * 2:
                            past_cols = (0, 512)
                        elif kb == grp * 2:
                            past_cols = (256, 512)

                    if kt < t0:
                        segs = [(0, 512, need_sel, False)]
                    else:
                        d0 = (kt - t0) * 128
                        segs = [(d0, d0 + 128, False, True)]
                        blk_end = 256 if (kt - t0) < 2 else 512
                        if d0 + 128 < blk_end:
                            segs.append((d0 + 128, blk_end, False, False))
                        if blk_end < 512:
                            if need_sel:
                                segs.append((blk_end, 512, True, False))
                            elif len(segs) > 1:
                                segs[-1] = (segs[-1][0], 512, False, False)
                            else:
                                segs.append((blk_end, 512, False, False))

                    def mm_s(e, kt=kt, pb=pb, segs=segs, rows=rows, hp=hp, kb=kb, bs=bs):
                        r = None
                        for (a0, a1, hb, hc) in segs:
                            reg = ps[pb][:, a0:a1]
                            r = e.matmul(reg, lhsT=KT[rows, hp, kt * 128:(kt + 1) * 128],
                                         rhs=QT[rows, hp, a0:a1], start=True, stop=not (hb or hc))
                            if hb:
                                r = e.matmul(reg, lhsT=esel[:, kb * 128:(kb + 1) * 128],
                                             rhs=biasT[bs][:, a0:a1], start=False, stop=True)
                            if hc:
                                r = e.matmul(reg, lhsT=ident_bf[:], rhs=cmask_bf[:], start=False, stop=True)
                        return r
                    rd = [("KT", kt)] + [("QT", ti) for ti in range(4)] + ["esel", ("biasT", bs), "ident_bf", "cmask"]
                    sc.add("pe", mm_s, reads=rd, writes=[("ps", pb)])
                    pt = kt % 2
                    sc.add("act", lambda e, pb=pb, c0=c0, pt=pt: e.activation(
                        out=PT[pt][:, c0:512], in_=ps[pb][:, c0:512], func=AF.Exp, scale=0.125),
                        reads=[("ps", pb)], writes=[("PT", pt)])

                    def mm_pv(e, kt=kt, c0=c0, pt=pt, h=h, t0=t0):
                        r = None
                        for ti in range(c0 // 128, 4):
                            tq = t0 + ti
                            r = e.matmul(ps[ti][:, 0:65], lhsT=PT[pt][:, ti * 128:(ti + 1) * 128],
                                         rhs=vp[:, kt, h, :], start=(kt == 0), stop=(kt == tq))
                        return r
                    sc.add("pe", mm_pv, reads=[("PT", pt), ("vp", kt)], writes=[("ps", ti) for ti in range(c0 // 128, 4)])
                for ti in range(4):
                    sc.add("dve", lambda e, ti=ti: e.reciprocal(out=rden[:, ti:ti + 1], in_=ps[ti][:, 64:65]),
                           reads=[("ps", ti)], writes=[("rden", ti)], name="recip g%d h%d ti%d" % (grp, h, ti))
                    sc.add("dve", lambda e, ti=ti, h=h: e.tensor_scalar(
                        out=att_tm[:, ti, h * 64:(h + 1) * 64], in0=ps[ti][:, 0:64], scalar1=rden[:, ti:ti + 1], scalar2=0.0,
                        op0=ALU.mult, op1=ALU.add),
                        reads=[("ps", ti), ("rden", ti)], writes=[("att_tm", h, ti)])
            for ti in range(4):
                pb = nb2(1)

                def tr_a(e, ti=ti, pb=pb):
                    r = None
                    for i in range(2):
                        r = e.transpose(out=ps[pb][:, i * 128:(i + 1) * 128], in_=att_tm[:, ti, i * 128:(i + 1) * 128],
                                        identity=ident[:])
                    return r
                sc.add("pe", tr_a, reads=[("att_tm", h, ti) for h in range(4)] + ["ident"], writes=[("ps", pb)])
                sc.add("act", lambda e, pb=pb, ti=ti: e.copy(
                    out=ycatT[:, 6:8, ti * 128:(ti + 1) * 128], in_=ps[pb][:, 0:256].rearrange("p (a t) -> p a t", a=2)),
                    reads=[("ps", pb)], writes=[("ycatT", 6, ti), ("ycatT", 7, ti)])

            ycr = [("ycatT", c, ci) for c in range(8) for ci in range(4)]
            for half in range(4):
                ws = load_w(wout_d[(l * 4 + half) * 128:(l * 4 + half + 1) * 128, :])
                for mm_ in range(2):
                    m = half * 2 + mm_
                    pb = nb2(0)

                    def mmo(e, ws=ws, mm_=mm_, pb=pb):
                        r = None
                        for k in range(NC8):
                            r = e.matmul(ps[pb][:], lhsT=wsl[ws][:, (mm_ * 8 + k) * 128:(mm_ * 8 + k + 1) * 128],
                                         rhs=ycatT[:, k, :], start=(k == 0), stop=(k == NC8 - 1))
                        return r
                    sc.add("pe", mmo, reads=[("wsl", ws)] + ycr, writes=[("ps", pb)])
                    s2 = m % 2
                    sc.add("dve", lambda e, pb=pb, m=m: e.tensor_copy(out=mT[:, m, :], in_=ps[pb][:]),
                           reads=[("ps", pb)], writes=[("xbcf", m)])
                    sc.add("act", lambda e, m=m, s2=s2: e.activation(out=sq[s2][:], in_=mT[:, m, :], func=AF.Square),
                           reads=[("xbcf", m)], writes=[("sq", s2)])
                    sc.add("pe", lambda e, s2=s2, m=m: e.matmul(
                        ps[7][:], lhsT=ones_bf[:], rhs=sq[s2][:], start=(m == 0), stop=(m == NC8 - 1)),
                        reads=[("sq", s2), "ones"], writes=[("ps", 7)])
            rstd_op(1, 7)
            postnorm_add(grp, 1, gpost, lambda m: mT[:, m, :], lambda m: ("xbcf", m))

    if stage == "all":
        plan = [(l, ["f1", "m", "f2"]) for l in range(DEPTH)]
    elif stage == "none":
        plan = []
    else:
        lay, parts = stage.split(":")
        plan = [(int(lay), parts.split("+"))]
    for l, stages in plan:
        if "f1" in stages:
            ffn(l, 0, "ff1_norm_pre", "ff1_norm_post")
        if "m" in stages:
            barrier()
            mixer(l)
            barrier()
        if "f2" in stages:
            ffn(l, 1, "ff2_norm_pre", "ff2_norm_post")

    barrier()
    for t in range(S // 128):
        slot = t % 2
        for q in range(2):
            pb = 6 + q

            def tr2(e, q=q, t=t, pb=pb):
                r = None
                for i in range(4):
                    c = 4 * q + i
                    r = e.transpose(out=ps[pb][:, i * 128:(i + 1) * 128], in_=xT[:, c, t * 128:(t + 1) * 128],
                                    identity=ident[:])
                return r
            sc.add("pe", tr2, reads=[xkey(4 * q + i, t // 4) for i in range(4)] + ["ident"], writes=[("ps", pb)])
            sc.add("act", lambda e, q=q, slot=slot, pb=pb: e.copy(
                out=xio[slot][:, q * 512:(q + 1) * 512], in_=ps[pb][:]),
                reads=[("ps", pb)], writes=[("xio", slot, q)])
        sc.add("sp", lambda e, t=t, slot=slot: e.dma_start(out=out_d[t * 128:(t + 1) * 128, :], in_=xio[slot]),
               reads=[("xio", slot, 0), ("xio", slot, 1)], writes=[("out", t)], dma=("out", slot))
    sc.add("sp", lambda e: e.nop(), reads=[("out", t) for t in range(S // 128)], writes=[])
    sc.emit()
    return nc


def prep_weights(inp):
    wgu = np.empty((DEPTH, 2, NJ, 128, 2, 8, 128), np.float32)
    wd = np.empty((DEPTH, 2, NC8, 128, NJ, 128), np.float32)
    win = np.empty((DEPTH, WIN_PIECES, 128, 2048), np.float32)
    wdt = np.empty((DEPTH, 128, 64), np.float32)
    wout = np.empty((DEPTH, 4, 128, 2, 8, 128), np.float32)
    wp = np.zeros((DEPTH, 128, 2, 128), np.float32)
    for l in range(DEPTH):
        for f, pfx in enumerate(("ff1", "ff2")):
            for g, nm in enumerate(("w_gate", "w_up")):
                w = np.asarray(inp["%s_%s" % (pfx, nm)][l])
                wgu[l, f, :, :, g] = w.reshape(8, 128, NJ, 128).transpose(2, 1, 0, 3)
            w = np.asarray(inp["%s_w_down" % pfx][l])
            wd[l, f] = w.reshape(NJ, 128, NC8, 128).transpose(2, 1, 0, 3)
        w = np.asarray(inp["w_in"][l])
        wk = w.reshape(8, 128, 2568)
        for fp in range(4):
            blk = wk[:, :, 512 + fp * 256:512 + (fp + 1) * 256].reshape(8, 128, 2, 128)
            win[l, fp] = blk.transpose(1, 2, 0, 3).reshape(128, 2048)
        wt = wk[:, :, TCOLS]
        for tp in range(6):
            blk = wt[:, :, tp * 256:(tp + 1) * 256]
            win[l, 4 + tp] = blk.transpose(1, 0, 2).reshape(128, 2048)
        wdt[l] = wk[:, :, 1536:1544].transpose(1, 0, 2).reshape(128, 64)
        wo = np.asarray(inp["w_out"][l]).reshape(8, 128, 4, 2, 128)
        wout[l] = wo.transpose(2, 1, 3, 0, 4)
        pw = np.asarray(inp["pool_w"][l])
        for g in range(4):
            pair, i = g // 2, g % 2
            wp[l, i * 64:(i + 1) * 64, pair, i * 64:(i + 1) * 64] = pw[g]
    return dict(
        wgu=wgu.reshape(DEPTH * 2 * NJ * 128, 2 * 8 * 128), wd=wd.reshape(DEPTH * 2 * NC8 * 128, NJ * 128),
        win=win.reshape(DEPTH * WIN_PIECES * 128, 2048), wdt=wdt.reshape(DEPTH * 128, 64),
        wout=wout.reshape(DEPTH * 4 * 128, 2048), wp=wp.reshape(DEPTH * 128, 256))


LAUNCH_PLAN = ["0:f1+m", "0:f2", "1:f1+m", "1:f2"]


def kernel(stage=None, **inp):
    inp = {k: np.asarray(v) for k, v in inp.items()}
    small = pack_small(inp)
    assert small.shape[1] == NSMALL
    shared = prep_weights(inp)
    shared.update(make_consts())
    shared["small"] = small
    shared["ident"] = np.eye(128, dtype=np.float32)
    x = inp["x"]
    plan = LAUNCH_PLAN if stage is None else [stage]
    progs = {}
    for st in plan:
        if st not in progs:
            progs[st] = build(st)
        nc = progs[st]
        in_maps = [dict(shared, x=np.ascontiguousarray(x[b])) for b in range(8)]
        res = run_bass_kernel_spmd(nc, in_maps, core_ids=list(range(8)))
        x = np.stack([r["out"] for r in res.results], axis=0)
    return x
```

```python
import numpy as np
import concourse.bass as bass
import concourse.mybir as mybir
from concourse.bass_utils import run_bass_kernel_spmd

F32 = mybir.dt.float32
BF16 = mybir.dt.bfloat16
AF = mybir.ActivationFunctionType
ALU = mybir.AluOpType
AX = mybir.AxisListType

S = 2048
D = 1024
DFF = 2816
NJ = DFF // 128
NC8 = D // 128
DEPTH = 2
EPS = 1e-6
TG = 1024
NTG = S // TG
NSG = TG // 512


class Op:
    __slots__ = ("eng", "fn", "deps", "signal", "count", "dkey", "ndma", "name", "epoch")


class Sched:
    ENGS = ("pe", "act", "dve", "pool", "sp")

    def __init__(self, nc):
        self.nc = nc
        self.ops = {e: [] for e in self.ENGS}
        self.last_writer = {}
        self.readers = {}
        self.dcount = {}
        self.nops = 0
        self.epoch = 0

    def add(self, eng, fn, reads=(), writes=(), dma=None, ndma=1, name=""):
        op = Op()
        op.eng = eng
        op.fn = fn
        op.signal = False
        op.count = None
        op.dkey = dma
        op.ndma = ndma
        op.name = name
        op.epoch = self.epoch
        if dma is not None:
            self.dcount[dma] = self.dcount.get(dma, 0) + 16 * ndma
            op.count = self.dcount[dma]
        deps = {}
        raw = set()
        for b in reads:
            w = self.last_writer.get(b)
            if w is not None:
                deps[id(w)] = w
                raw.add(id(w))
        for b in writes:
            w = self.last_writer.get(b)
            if w is not None:
                deps[id(w)] = w
            rd = self.readers.get(b)
            if rd:
                for r in rd.values():
                    if isinstance(r, list):
                        for rr in r:
                            deps[id(rr)] = rr
                    else:
                        deps[id(r)] = r
        fdeps = []
        for k, d in deps.items():
            if d.dkey is not None:
                fdeps.append(d)
                continue
            if d.eng == eng:
                if eng == "pe":
                    continue
                fdeps.append(d)
                continue
            fdeps.append(d)
        for d in fdeps:
            d.signal = True
        op.deps = fdeps
        for b in reads:
            rd = self.readers.setdefault(b, {})
            if dma is not None:
                rd.setdefault("dma", []).append(op)
            else:
                rd[eng] = op
        for b in writes:
            self.last_writer[b] = op
            self.readers[b] = {}
        self.ops[eng].append(op)
        self.nops += 1
        return op

    def emit(self):
        nc = self.nc
        esem = {(e, ep): nc.alloc_semaphore("sem_%s_%d" % (e, ep)) for e in self.ENGS for ep in range(self.epoch + 1)}
        dsem = {k: nc.alloc_semaphore("dsem_%d" % i) for i, k in enumerate(self.dcount)}
        for e in self.ENGS:
            c = {}
            for op in self.ops[e]:
                if op.dkey is None and op.signal:
                    c[op.epoch] = c.get(op.epoch, 0) + 1
                    op.count = c[op.epoch]
        ops = self.ops

        def run(eng_name, eng):
            waited = {}
            for op in ops[eng_name]:
                for d in op.deps:
                    if d.dkey is not None:
                        sem = dsem[d.dkey]
                    else:
                        sem = esem[(d.eng, d.epoch)]
                    if waited.get(sem.num, 0) < d.count:
                        eng.wait_ge(sem, d.count)
                        waited[sem.num] = d.count
                r = op.fn(eng)
                if op.name:
                    rr_ = r[-1] if isinstance(r, (list, tuple)) else r
                    rr_.annotate(op.name)
                if op.dkey is not None:
                    if not isinstance(r, (list, tuple)):
                        r = [r]
                    assert len(r) == op.ndma, (op.name, len(r), op.ndma)
                    for ins in r:
                        ins.then_inc(dsem[op.dkey], 16)
                elif op.signal:
                    if isinstance(r, (list, tuple)):
                        r = r[-1]
                    r.then_inc(esem[(eng_name, op.epoch)], 1)

        with nc.Block() as block:
            @block.tensor
            def _(e):
                run("pe", e)

            @block.scalar
            def _(e):
                run("act", e)

            @block.vector
            def _(e):
                run("dve", e)

            @block.gpsimd
            def _(e):
                run("pool", e)

            @block.sync
            def _(e):
                run("sp", e)


def _colmajor(v):
    return np.ascontiguousarray(v.reshape(-1, 128).T)


SM_OFF = {}


def pack_small(inp):
    cols = []
    off = 0

    def put(name, arr):
        nonlocal off
        SM_OFF[name] = (off, arr.shape[1])
        cols.append(arr.astype(np.float32))
        off += arr.shape[1]

    for l in range(DEPTH):
        for nm in ("ff1_norm_pre", "ff1_norm_post", "mix_norm_pre", "mix_norm_post", "ff2_norm_pre", "ff2_norm_post"):
            put("%s%d" % (nm, l), _colmajor(inp[nm][l]))
    for l in range(DEPTH):
        put("conv_b%d" % l, _colmajor(inp["conv_b"][l]))
        for j in range(4):
            put("conv_w%d_%d" % (j, l), _colmajor(inp["conv_w"][l, j]))
        put("ssd_norm%d" % l, _colmajor(inp["ssd_norm"][l]))
        put("pool_scale%d" % l, _colmajor(inp["pool_scale"][l]))
        for nm in ("dt_bias", "a_log", "d_skip"):
            put("%s%d" % (nm, l), np.broadcast_to(inp[nm][l][None, :], (128, 8)))
    return np.ascontiguousarray(np.concatenate(cols, axis=1))


NSMALL = 6 * DEPTH * 8 + DEPTH * (8 + 32 + 4 + 2 + 24)
WIN_PIECES = 10
TCOLS = list(range(0, 512)) + list(range(1800, 2312)) + list(range(2312, 2568)) + list(range(1544, 1800))


def make_consts():
    c = {}
    k = np.arange(128)
    c["tri"] = (k[:, None] <= k[None, :]).astype(np.float32)
    c["negmask"] = np.where(k[None, :] >= k[:, None], 0.0, -30000.0).astype(np.float32)
    inv_freq = 500000.0 ** (-np.arange(0, 16, 2, dtype=np.float32) / 16.0)
    pos = (np.arange(16)[None, :, None] * 128 + k[:, None, None]).astype(np.float32)
    ang = pos * inv_freq[None, None, :]
    c["rope"] = np.concatenate([np.cos(ang), np.sin(ang)], axis=-1).reshape(128, 16 * 16).astype(np.float32)
    bands = []
    for w in (2, 4, 8, 16):
        s_ = k[:, None]
        t_ = k[None, :]
        cur = ((s_ <= t_) & (t_ - s_ < w)).astype(np.float32) / w - (s_ == t_)
        cur0 = ((s_ <= t_) & (t_ - s_ < w)).astype(np.float32) / np.minimum(t_ + 1, w) - (s_ == t_)
        prev = (((t_ + 128 - s_) < w)).astype(np.float32) / w
        bands += [cur, cur0, prev]
    c["bands"] = np.concatenate(bands, axis=1).astype(np.float32)
    es = np.zeros((8, 8, 128), np.float32)
    for r in range(8):
        es[r, r, :] = 1.0
    c["esel"] = es.reshape(8, 1024)
    return c


import os
ADD_ENG = os.environ.get("KADD", "dve")


def build(stage="all"):
    nc = bass.Bass("TRN2", target_bir_lowering=False)
    sc = Sched(nc)

    x_d = nc.dram_tensor("x", [S, D], F32, kind="ExternalInput").ap()
    out_d = nc.dram_tensor("out", [S, D], F32, kind="ExternalOutput").ap()
    small_d = nc.dram_tensor("small", [128, NSMALL], F32, kind="ExternalInput").ap()
    ident_d = nc.dram_tensor("ident", [128, 128], F32, kind="ExternalInput").ap()
    tri_d = nc.dram_tensor("tri", [128, 128], F32, kind="ExternalInput").ap()
    negmask_d = nc.dram_tensor("negmask", [128, 128], F32, kind="ExternalInput").ap()
    rope_d = nc.dram_tensor("rope", [128, 256], F32, kind="ExternalInput").ap()
    bands_d = nc.dram_tensor("bands", [128, 12 * 128], F32, kind="ExternalInput").ap()
    esel_d = nc.dram_tensor("esel", [8, 1024], F32, kind="ExternalInput").ap()
    wgu_d = nc.dram_tensor("wgu", [DEPTH * 2 * NJ * 128, 2 * 8 * 128], F32, kind="ExternalInput").ap()
    wd_d = nc.dram_tensor("wd", [DEPTH * 2 * NC8 * 128, NJ * 128], F32, kind="ExternalInput").ap()
    win_d = nc.dram_tensor("win", [DEPTH * WIN_PIECES * 128, 2048], F32, kind="ExternalInput").ap()
    wdt_d = nc.dram_tensor("wdt", [DEPTH * 128, 64], F32, kind="ExternalInput").ap()
    wout_d = nc.dram_tensor("wout", [DEPTH * 4 * 128, 2048], F32, kind="ExternalInput").ap()
    wp_d = nc.dram_tensor("wp", [DEPTH * 128, 256], F32, kind="ExternalInput").ap()

    def sb(name, shape, dt):
        return nc.alloc_sbuf_tensor(name, shape, dt)

    xT = sb("xT", [128, NC8, S], F32)
    small = sb("small_sb", [128, NSMALL], F32)
    g32 = sb("g32", [128, 6 * DEPTH * 8], F32)
    ident = sb("ident_sb", [128, 128], F32)
    ones_bf = sb("ones_bf", [128, 128], BF16)
    ones_f = sb("ones_f", [128, 128], F32)
    neghalf = sb("neghalf", [128, 512], F32)
    sq = [sb("sq%d" % i, [128, 512], BF16) for i in range(2)]
    rstd = [sb("rstd%d" % i, [128, 512], F32) for i in range(2)]
    tmp = [sb("tmp%d" % i, [128, 512], F32) for i in range(2)]
    tri = sb("tri_sb", [128, 128], F32)
    negmask = sb("negmask_sb", [128, 128], F32)
    cmask_bf = sb("cmask_bf", [128, 128], BF16)
    ident_bf = sb("ident_bf", [128, 128], BF16)
    rope = sb("rope_sb", [128, 256], F32)
    bands = sb("bands_sb", [128, 12 * 128], BF16)
    esel = sb("esel_sb", [8, 1024], BF16)
    ptmp = tmp
    ps = [nc.alloc_psum_tensor("ps%d" % i, [128, 512], F32) for i in range(8)]

    with nc.reset_on_exit():
        hnT = sb("hnT", [128, NC8, TG], BF16)
        aT = sb("aT", [128, NJ, TG], BF16)
        fT = sb("fT", [128, NC8, TG], F32)
        sil = [sb("sil%d" % i, [128, 512], F32) for i in range(2)]
        wgu = [sb("wgu%d" % i, [128, 2 * 8 * 128], BF16) for i in range(2)]
        wd = [sb("wd%d" % i, [128, NJ * 128], BF16) for i in range(2)]
        xio = [fT[:, i, :] for i in range(2)]
    hnTm = sb("hnTm", [128, NC8, 512], BF16)
    KT = sb("KT", [128, 2, S], BF16)
    vp = sb("vp", [128, 16, 4, 65], BF16)
    wsl = [sb("wsl%d" % i, [128, 2048], BF16) for i in range(2)]
    wdt = sb("wdt_sb", [128, 64], BF16)
    wp = sb("wp_sb", [128, 256], BF16)
    pre = [sb("pre%d" % i, [128, 515], F32) for i in range(2)]
    carry = sb("carry", [128, 8, 3], F32)
    cacc = [sb("cacc%d" % i, [128, 512], F32) for i in range(2)]
    xbcf = sb("xbcf", [128, 8, 512], F32)
    bcT = sb("bcT", [128, 4, 512], BF16)
    x_tm = sb("x_tm", [128, 4, 512], BF16)
    B_tm = sb("B_tm", [128, 4, 256], BF16)
    zs = sb("zs", [128, 4, 512], F32)
    dtt = sb("dtt", [128, 4, 8], F32)
    dAt = sb("dAt", [128, 4, 8], F32)
    Aneg = sb("Aneg", [128, 8], F32)
    qkf = sb("qkf", [128, 256], F32)
    ropet = [sb("ropet%d" % i, [128, 8, 8], F32) for i in range(4)]
    QT = sb("QT", [128, 2, 512], BF16)
    u_tm = sb("u_tm", [128, 5, 256], BF16)
    kmf = sb("kmf", [128, 2, 8], F32)
    kmT = sb("kmT", [128, 2, 8], BF16)
    state_f = sb("state_f", [128, 512], F32)
    state_b = sb("state_b", [128, 512], BF16)
    Dall = sb("Dall", [128, 1024], F32)
    seg = sb("seg", [128, 8, 128], BF16)
    MT = sb("MT", [128, 8, 128], BF16)
    Ein = sb("Ein", [128, 24], F32)
    Eout = sb("Eout", [128, 24], F32)
    negacs = sb("negacs", [128, 8], F32)
    xdt = sb("xdt", [128, 512], BF16)
    xdtd = sb("xdtd", [128, 512], BF16)
    y_sb = sb("y_sb", [128, 512], F32)
    y2 = sb("y2", [128, 512], F32)
    ssq = sb("ssq", [128, 4], F32)
    d_tm = sb("d_tm", [128, 256], F32)
    dT = sb("dT", [128, 2, 512], BF16)
    PT = [sb("PT%d" % i, [128, 512], BF16) for i in range(2)]
    gate = sb("gate", [128, 4, 8], F32)
    cmpb = sb("cmpb", [128, 4, 8, 8], F32)
    cnt = sb("cnt", [128, 4, 8], F32)
    bias_tm = sb("bias_tm", [128, 4, 4, 8], F32)
    biasT = [sb("biasT%d" % i, [8, 512], BF16) for i in range(2)]
    att_tm = sb("att_tm", [128, 4, 256], F32)
    rden = sb("rden", [128, 4], F32)
    ycatT = sb("ycatT", [128, NC8, 512], BF16)
    mT = xbcf

    def gsl(name):
        o, n = SM_OFF[name]
        return slice(o, o + n)

    def scol(name, i=0):
        o, n = SM_OFF[name]
        return small[:, o + i:o + i + 1]

    sc.add("sp", lambda e: e.dma_start(out=small[:], in_=small_d), writes=["small"], dma="small")
    sc.add("sp", lambda e: e.dma_start(out=ident[:], in_=ident_d), writes=["ident"], dma="ident")
    sc.add("sp", lambda e: e.dma_start(out=tri[:], in_=tri_d), writes=["tri"], dma="tri")
    sc.add("sp", lambda e: e.dma_start(out=negmask[:], in_=negmask_d), writes=["negmask"], dma="negmask")
    sc.add("sp", lambda e: e.dma_start(out=rope[:], in_=rope_d), writes=["rope"], dma="rope")
    sc.add("pool", lambda e: e.dma_start(out=bands[:], in_=bands_d), writes=["bands"], dma="bands")
    sc.add("pool", lambda e: e.dma_start(out=esel[:], in_=esel_d), writes=["esel"], dma="esel")
    sc.add("dve", lambda e: e.memset(ones_bf[:], 1.0), writes=["ones"])
    sc.add("dve", lambda e: e.memset(ones_f[:], 1.0), writes=["ones_f"])
    sc.add("pool", lambda e: e.memset(neghalf[:], -0.5), writes=["neghalf"])
    sc.add("dve", lambda e: e.tensor_copy(out=cmask_bf[:], in_=negmask[:]), reads=["negmask"], writes=["cmask"])
    sc.add("dve", lambda e: e.tensor_copy(out=ident_bf[:], in_=ident[:]), reads=["ident"], writes=["ident_bf"])
    for l in range(DEPTH):
        for i, (nm, coef) in enumerate((("ff1_norm_pre", 32.0), ("ff1_norm_post", 16.0), ("mix_norm_pre", 32.0),
                                        ("mix_norm_post", 32.0), ("ff2_norm_pre", 32.0), ("ff2_norm_post", 16.0))):
            s_ = gsl("%s%d" % (nm, l))
            sc.add("dve", lambda e, s_=s_, coef=coef: e.tensor_scalar(
                out=g32[:, s_], in0=small[:, s_], scalar1=coef, scalar2=0.0, op0=ALU.mult, op1=ALU.add),
                reads=["small"], writes=[("g32", s_.start)])

    def barrier():
        lasts = []
        for e_ in sc.ENGS:
            for o_ in reversed(sc.ops[e_]):
                if o_.dkey is None:
                    lasts.append(o_)
                    break
        sc.epoch += 1
        for e in ("pe", "act", "dve", "pool", "sp"):
            op = sc.add(e, lambda eng: eng.nop(), name="barrier")
            for d in lasts:
                if d is not op:
                    d.signal = True
                    op.deps.append(d)

    def xkey(c, t512):
        return ("xT", c, t512)

    for t in range(S // 128):
        slot = t % 2
        sc.add("sp", lambda e, t=t, slot=slot: e.dma_start(out=xio[slot], in_=x_d[t * 128:(t + 1) * 128, :]),
               writes=[("xio", slot)], dma=("xio", slot))
        for q in range(2):
            pb = 6 + q

            def tr(e, q=q, slot=slot, pb=pb):
                r = None
                for i in range(4):
                    c = 4 * q + i
                    r = e.transpose(out=ps[pb][:, i * 128:(i + 1) * 128], in_=xio[slot][:, c * 128:(c + 1) * 128],
                                    identity=ident[:])
                return r
            sc.add("pe", tr, reads=[("xio", slot), "ident"], writes=[("ps", pb)])
            sc.add("dve", lambda e, q=q, t=t, pb=pb: e.tensor_copy(
                out=xT[:, 4 * q:4 * q + 4, t * 128:(t + 1) * 128],
                in_=ps[pb][:].rearrange("p (c t) -> p c t", c=4)),
                reads=[("ps", pb)], writes=[xkey(4 * q + i, t // 4) for i in range(4)])

    barrier()

    def rstd_op(sg, pb):
        sc.add("dve", lambda e, sg=sg, pb=pb: e.tensor_scalar(
            out=tmp[sg][:], in0=ps[pb][:], scalar1=1024.0 * EPS, scalar2=1.0,
            op0=ALU.add, op1=ALU.mult),
            reads=[("ps", pb)], writes=[("tmp", sg)])
        sc.add("pool", lambda e, sg=sg: e.tensor_tensor(
            out=rstd[sg][:], in0=tmp[sg][:], in1=neghalf[:], op=ALU.pow),
            reads=[("tmp", sg), "neghalf"], writes=[("rstd", sg)])

    def prenorm(t512, sg, gpre, dst, dkey):
        cols = slice(t512 * 512, (t512 + 1) * 512)
        for c in range(NC8):
            s2 = c % 2
            sc.add("act", lambda e, c=c, cols=cols, s2=s2: e.activation(
                out=sq[s2][:], in_=xT[:, c, cols], func=AF.Square),
                reads=[xkey(c, t512)], writes=[("sq", s2)])
            sc.add("pe", lambda e, c=c, s2=s2, sg=sg: e.matmul(
                ps[6 + sg][:], lhsT=ones_bf[:], rhs=sq[s2][:], start=(c == 0), stop=(c == NC8 - 1)),
                reads=[("sq", s2), "ones"], writes=[("ps", 6 + sg)])
        rstd_op(sg, 6 + sg)
        for c in range(NC8):
            sc.add("dve", lambda e, c=c, sg=sg, cols=cols: e.scalar_tensor_tensor(
                out=dst(c), in0=xT[:, c, cols],
                scalar=g32[:, gpre.start + c:gpre.start + c + 1], in1=rstd[sg][:],
                op0=ALU.mult, op1=ALU.mult),
                reads=[xkey(c, t512), ("rstd", sg), ("g32", gpre.start)], writes=[dkey(c)])

    def postnorm_add(t512, sg, gpost, src, skey):
        cols = slice(t512 * 512, (t512 + 1) * 512)
        for m in range(NC8):
            tb = m % 2
            sc.add("dve", lambda e, m=m, sg=sg, tb=tb: e.scalar_tensor_tensor(
                out=cacc_or_tmp(tb), in0=src(m),
                scalar=g32[:, gpost.start + m:gpost.start + m + 1], in1=rstd[sg][:],
                op0=ALU.mult, op1=ALU.mult),
                reads=[skey(m), ("rstd", sg), ("g32", gpost.start)], writes=[("tmp", tb)])
            sc.add(ADD_ENG, lambda e, m=m, cols=cols, tb=tb: e.tensor_tensor(
                out=xT[:, m, cols], in0=xT[:, m, cols], in1=cacc_or_tmp(tb), op=ALU.add),
                reads=[("tmp", tb), xkey(m, t512)], writes=[xkey(m, t512)])

    def cacc_or_tmp(tb):
        return ptmp[tb][:]

    def ffn(l, f, pre_name, post_name):
        gpre = gsl("%s%d" % (pre_name, l))
        gpost = gsl("%s%d" % (post_name, l))
        wrow = (l * 2 + f)
        for tg in range(NTG):
            for sg in range(NSG):
                prenorm(tg * NSG + sg, sg, gpre, lambda c, sg=sg: hnT[:, c, sg * 512:(sg + 1) * 512],
                        lambda c, sg=sg: ("hnT", c, sg))
            import os
            KCUT = int(os.environ.get("KCUT", "9")) if f == 1 else 9
            if KCUT < 1:
                continue
            it = 0
            for j in range(NJ):
                ws = j % 2
                r0 = (wrow * NJ + j) * 128
                sc.add("pool", lambda e, ws=ws, r0=r0: e.dma_start(out=wgu[ws][:], in_=wgu_d[r0:r0 + 128, :]),
                       writes=[("wgu", ws)], dma=("wgu", ws))
                for sg in range(NSG):
                    b = it % 2
                    it += 1

                    def mm(e, ws=ws, sg=sg, b=b):
                        r = None
                        for g in range(2):
                            for k in range(NC8):
                                r = e.matmul(ps[2 * g + b][:], lhsT=wgu[ws][:, (g * 8 + k) * 128:(g * 8 + k + 1) * 128],
                                             rhs=hnT[:, k, sg * 512:(sg + 1) * 512], start=(k == 0), stop=(k == NC8 - 1))
                        return r
                    sc.add("pe", mm, reads=[("wgu", ws)] + [("hnT", k, sg) for k in range(NC8)],
                           writes=[("ps", b), ("ps", 2 + b)])
                    sc.add("act", lambda e, b=b: e.activation(out=sil[b][:], in_=ps[b][:], func=AF.Silu),
                           reads=[("ps", b)], writes=[("sil", b)])
                    sc.add("dve", lambda e, b=b, j=j, sg=sg: e.tensor_tensor(
                        out=aT[:, j, sg * 512:(sg + 1) * 512], in0=sil[b][:], in1=ps[2 + b][:], op=ALU.mult),
                        reads=[("sil", b), ("ps", 2 + b)], writes=[("aT", j, sg)])
            if KCUT < 2:
                continue
            it = 0
            for m in range(NC8):
                ws = m % 2
                r0 = (wrow * NC8 + m) * 128
                hw_ = NJ * 64
                sc.add("pool", lambda e, ws=ws, r0=r0, hw_=hw_: [
                    e.dma_start(out=wd[ws][:, h * hw_:(h + 1) * hw_], in_=wd_d[r0:r0 + 128, h * hw_:(h + 1) * hw_])
                    for h in range(2)],
                       writes=[("wd", ws)], dma=("wd", ws), ndma=2)
                for sg in range(NSG):
                    b = 4 + (it % 2)
                    it += 1

                    def mm2(e, ws=ws, sg=sg, b=b):
                        r = None
                        for j in range(NJ):
                            r = e.matmul(ps[b][:], lhsT=wd[ws][:, j * 128:(j + 1) * 128],
                                         rhs=aT[:, j, sg * 512:(sg + 1) * 512], start=(j == 0), stop=(j == NJ - 1))
                        return r
                    sc.add("pe", mm2, reads=[("wd", ws)] + [("aT", j, sg) for j in range(NJ)], writes=[("ps", b)])
                    s2 = it % 2
                    sc.add("dve", lambda e, b=b, m=m, sg=sg: e.tensor_copy(
                        out=fT[:, m, sg * 512:(sg + 1) * 512], in_=ps[b][:]),
                        reads=[("ps", b)], writes=[("fT", m, sg)])
                    sc.add("act", lambda e, m=m, sg=sg, s2=s2: e.activation(
                        out=sq[s2][:], in_=fT[:, m, sg * 512:(sg + 1) * 512], func=AF.Square),
                        reads=[("fT", m, sg)], writes=[("sq", s2)])
                    sc.add("pe", lambda e, s2=s2, sg=sg, m=m: e.matmul(
                        ps[6 + sg][:], lhsT=ones_bf[:], rhs=sq[s2][:], start=(m == 0), stop=(m == NC8 - 1)),
                        reads=[("sq", s2), "ones"], writes=[("ps", 6 + sg)])
            if KCUT < 3:
                continue
            for sg in range(NSG):
                rstd_op(sg, 6 + sg)
                postnorm_add(tg * NSG + sg, sg, gpost, lambda m, sg=sg: fT[:, m, sg * 512:(sg + 1) * 512],
                             lambda m, sg=sg: ("fT", m, sg))

    def mixer(l):
        gpre = gsl("mix_norm_pre%d" % l)
        gpost = gsl("mix_norm_post%d" % l)
        psb = [0]

        def nb2(pair):
            psb[0] += 1
            return pair * 2 + (psb[0] % 2)

        wcnt = [0]

        def load_w(src_ap):
            ws = wcnt[0] % 2
            wcnt[0] += 1
            sc.add("pool", lambda e, ws=ws, src_ap=src_ap: e.dma_start(out=wsl[ws][:], in_=src_ap),
                   writes=[("wsl", ws)], dma=("wsl", ws))
            return ws

        sc.add("pool", lambda e: e.dma_start(out=wdt[:], in_=wdt_d[l * 128:(l + 1) * 128, :]), writes=["wdt"], dma="wdt")
        sc.add("pool", lambda e: e.dma_start(out=wp[:], in_=wp_d[l * 128:(l + 1) * 128, :]), writes=["wp"], dma="wp")
        sc.add("dve", lambda e: e.memset(vp[:], 1.0), writes=["vp_init"] + [("vp", t_) for t_ in range(16)])
        sc.add("dve", lambda e: e.memset(carry[:], 0.0), writes=[("carry", c) for c in range(8)])
        sc.add("dve", lambda e: e.memset(state_f[:], 0.0), writes=["state_f"])
        sc.add("dve", lambda e: e.memset(state_b[:], 0.0), writes=["state_b"])
        sc.add("dve", lambda e: e.memset(u_tm[:, 4, :], 0.0), writes=[("u_tm", 4)])
        sc.add("act", lambda e: e.activation(out=Aneg[:], in_=small[:, gsl("a_log%d" % l)], func=AF.Exp),
               reads=["small"], writes=["Aneg0"])
        sc.add("dve", lambda e: e.tensor_scalar(out=Aneg[:], in0=Aneg[:], scalar1=-1.0, scalar2=0.0,
                                                op0=ALU.mult, op1=ALU.add), reads=["Aneg0"], writes=["Aneg"])

        for grp in range(4):
            t0 = grp * 4
            prenorm(grp, 0, gpre, lambda c: hnTm[:, c, :], lambda c: ("hnTm", c))
            if grp > 0:
                sc.add("pool", lambda e: e.tensor_copy(out=u_tm[:, 4, :], in_=u_tm[:, 3, :]),
                       reads=[("u_tm", 3)], writes=[("u_tm", 4)])
            for fp in range(4):
                ws = load_w(win_d[(l * WIN_PIECES + fp) * 128:(l * WIN_PIECES + fp + 1) * 128, :])
                for cc in range(2):
                    ch = fp * 2 + cc
                    pb = nb2(0)

                    def mmf(e, ws=ws, cc=cc, pb=pb):
                        r = None
                        for k in range(NC8):
                            r = e.matmul(ps[pb][:], lhsT=wsl[ws][:, (cc * 8 + k) * 128:(cc * 8 + k + 1) * 128],
                                         rhs=hnTm[:, k, :], start=(k == 0), stop=(k == NC8 - 1))
                        return r
                    sc.add("pe", mmf, reads=[("wsl", ws)] + [("hnTm", k) for k in range(NC8)], writes=[("ps", pb)])
                    pr = ch % 2
                    sc.add("dve", lambda e, pr=pr, ch=ch: e.tensor_copy(out=pre[pr][:, 0:3], in_=carry[:, ch, :]),
                           reads=[("carry", ch)], writes=[("pre", pr, 0)])
                    sc.add("act", lambda e, pr=pr, pb=pb: e.copy(out=pre[pr][:, 3:515], in_=ps[pb][:]),
                           reads=[("ps", pb)], writes=[("pre", pr, 1)])
                    sc.add("dve", lambda e, pr=pr, ch=ch: e.tensor_copy(out=carry[:, ch, :], in_=pre[pr][:, 512:515]),
                           reads=[("pre", pr, 1)], writes=[("carry", ch)])
                    sc.add("dve", lambda e, pr=pr, ch=ch: e.tensor_scalar(
                        out=cacc[pr][:], in0=pre[pr][:, 0:512], scalar1=scol("conv_w0_%d" % l, ch), scalar2=0.0,
                        op0=ALU.mult, op1=ALU.add),
                        reads=[("pre", pr, 0), ("pre", pr, 1), "small"], writes=[("cacc", pr)])
                    for j in range(1, 4):
                        sc.add("dve", lambda e, pr=pr, ch=ch, j=j: e.scalar_tensor_tensor(
                            out=cacc[pr][:], in0=pre[pr][:, j:j + 512], scalar=scol("conv_w%d_%d" % (j, l), ch),
                            in1=cacc[pr][:], op0=ALU.mult, op1=ALU.add),
                            reads=[("pre", pr, 0), ("pre", pr, 1), ("cacc", pr), "small"], writes=[("cacc", pr)])
                    sc.add("act", lambda e, pr=pr, ch=ch: e.activation(
                        out=xbcf[:, ch, :], in_=cacc[pr][:], func=AF.Silu, bias=scol("conv_b%d" % l, ch), scale=1.0),
                        reads=[("cacc", pr), "small"], writes=[("xbcf", ch)])
                    if ch >= 4:
                        sc.add("pool", lambda e, ch=ch: e.tensor_copy(out=bcT[:, ch - 4, :], in_=xbcf[:, ch, :]),
                               reads=[("xbcf", ch)], writes=[("bcT", ch - 4)])
            for ti in range(4):
                for half in range(2):
                    pb = nb2(1)
                    chs = [0, 1, 2, 3] if half == 0 else [4, 5]

                    def trx(e, ti=ti, chs=chs, pb=pb):
                        r = None
                        for i, ch in enumerate(chs):
                            r = e.transpose(out=ps[pb][:, i * 128:(i + 1) * 128],
                                            in_=xbcf[:, ch, ti * 128:(ti + 1) * 128], identity=ident[:])
                        return r
                    sc.add("pe", trx, reads=[("xbcf", ch) for ch in chs] + ["ident"], writes=[("ps", pb)])
                    if half == 0:
                        sc.add("act", lambda e, ti=ti, pb=pb: e.copy(out=x_tm[:, ti, :], in_=ps[pb][:]),
                               reads=[("ps", pb)], writes=[("x_tm", ti)])
                    else:
                        sc.add("act", lambda e, ti=ti, pb=pb: e.copy(out=B_tm[:, ti, :], in_=ps[pb][:, 0:256]),
                               reads=[("ps", pb)], writes=[("B_tm", ti)])
            for tp in range(6):
                ws = load_w(win_d[(l * WIN_PIECES + 4 + tp) * 128:(l * WIN_PIECES + 5 + tp) * 128, :])
                for ti in range(4):
                    pb = nb2(0)

                    def mmt(e, ws=ws, ti=ti, pb=pb):
                        r = None
                        for k in range(NC8):
                            r = e.matmul(ps[pb][:, 0:256], lhsT=hnTm[:, k, ti * 128:(ti + 1) * 128],
                                         rhs=wsl[ws][:, k * 256:(k + 1) * 256], start=(k == 0), stop=(k == NC8 - 1))
                        return r
                    sc.add("pe", mmt, reads=[("wsl", ws)] + [("hnTm", k) for k in range(NC8)], writes=[("ps", pb)])
                    if tp < 2:
                        sc.add("act", lambda e, ti=ti, tp=tp, pb=pb: e.activation(
                            out=zs[:, ti, tp * 256:(tp + 1) * 256], in_=ps[pb][:, 0:256], func=AF.Silu),
                            reads=[("ps", pb)], writes=[("zs", ti, tp)])
                    elif tp < 4:
                        isk = tp - 2
                        sc.add("act", lambda e, pb=pb: e.copy(out=qkf[:, 0:256], in_=ps[pb][:, 0:256]),
                               reads=[("ps", pb)], writes=["qkf"])
                        tile = t0 + ti
                        qv = qkf[:, 0:256].rearrange("p (h d) -> p h d", h=4)
                        cosb = rope[:, tile * 16:tile * 16 + 8].unsqueeze(1).broadcast_to([128, 4, 8])
                        sinb = rope[:, tile * 16 + 8:tile * 16 + 16].unsqueeze(1).broadcast_to([128, 4, 8])
                        x1 = qv[:, :, 0:8]
                        x2 = qv[:, :, 8:16]
                        r_ = [ropet[i][:, 0:4, :] for i in range(4)]
                        sc.add("dve", lambda e, x1=x1, cosb=cosb, r_=r_: e.tensor_tensor(out=r_[0], in0=x1, in1=cosb, op=ALU.mult),
                               reads=["qkf", "rope"], writes=[("ropet", 0)])
                        sc.add("dve", lambda e, x2=x2, sinb=sinb, r_=r_: e.tensor_tensor(out=r_[1], in0=x2, in1=sinb, op=ALU.mult),
                               reads=["qkf", "rope"], writes=[("ropet", 1)])
                        sc.add("dve", lambda e, x2=x2, cosb=cosb, r_=r_: e.tensor_tensor(out=r_[2], in0=x2, in1=cosb, op=ALU.mult),
                               reads=["qkf", "rope"], writes=[("ropet", 2)])
                        sc.add("dve", lambda e, x1=x1, sinb=sinb, r_=r_: e.tensor_tensor(out=r_[3], in0=x1, in1=sinb, op=ALU.mult),
                               reads=["qkf", "rope"], writes=[("ropet", 3)])
                        sc.add("dve", lambda e, x1=x1, r_=r_: e.tensor_tensor(out=x1, in0=r_[0], in1=r_[1], op=ALU.subtract),
                               reads=[("ropet", 0), ("ropet", 1)], writes=["qkf"])
                        sc.add("dve", lambda e, x2=x2, r_=r_: e.tensor_tensor(out=x2, in0=r_[2], in1=r_[3], op=ALU.add),
                               reads=[("ropet", 2), ("ropet", 3), "qkf"], writes=["qkf"])
                        pb2 = nb2(1)

                        def trq(e, pb2=pb2):
                            r = None
                            for hp in range(2):
                                r = e.transpose(out=ps[pb2][:, hp * 128:(hp + 1) * 128],
                                                in_=qkf[:, hp * 128:(hp + 1) * 128], identity=ident[:])
                            return r
                        sc.add("pe", trq, reads=["qkf", "ident"], writes=[("ps", pb2)])
                        if isk:
                            sc.add("act", lambda e, pb2=pb2, tile=tile: e.copy(
                                out=KT[:, :, tile * 128:(tile + 1) * 128],
                                in_=ps[pb2][:, 0:256].rearrange("p (a t) -> p a t", a=2)),
                                reads=[("ps", pb2)], writes=[("KT", tile)])
                        else:
                            sc.add("act", lambda e, pb2=pb2, ti=ti: e.copy(
                                out=QT[:, :, ti * 128:(ti + 1) * 128],
                                in_=ps[pb2][:, 0:256].rearrange("p (a t) -> p a t", a=2)),
                                reads=[("ps", pb2)], writes=[("QT", ti)])
                    elif tp == 4:
                        tile = t0 + ti
                        sc.add("act", lambda e, pb=pb, tile=tile: e.copy(
                            out=vp[:, tile, :, 0:64], in_=ps[pb][:, 0:256].rearrange("p (h d) -> p h d", h=4)),
                            reads=[("ps", pb), "vp_init"], writes=[("vp", tile)])
                    else:
                        sc.add("act", lambda e, pb=pb, ti=ti: e.copy(out=u_tm[:, ti, :], in_=ps[pb][:, 0:256]),
                               reads=[("ps", pb)], writes=[("u_tm", ti)])
            for ti in range(4):
                pb = nb2(0)

                def mmd(e, ti=ti, pb=pb):
                    r = None
                    for k in range(NC8):
                        r = e.matmul(ps[pb][:, 0:8], lhsT=hnTm[:, k, ti * 128:(ti + 1) * 128],
                                     rhs=wdt[:, k * 8:(k + 1) * 8], start=(k == 0), stop=(k == NC8 - 1))
                    return r
                sc.add("pe", mmd, reads=["wdt"] + [("hnTm", k) for k in range(NC8)], writes=[("ps", pb)])
                sc.add("dve", lambda e, ti=ti, pb=pb: e.tensor_tensor(
                    out=dtt[:, ti, :], in0=ps[pb][:, 0:8], in1=small[:, gsl("dt_bias%d" % l)], op=ALU.add),
                    reads=[("ps", pb), "small"], writes=[("dtt", ti)])
            sc.add("act", lambda e: e.activation(out=dtt[:], in_=dtt[:], func=AF.Exp),
                   reads=[("dtt", ti) for ti in range(4)], writes=["dtt_e"])
            sc.add("act", lambda e: e.activation(out=dtt[:], in_=dtt[:], func=AF.Ln, bias=1.0, scale=1.0),
                   reads=["dtt_e"], writes=["dtt_f"])
            sc.add("dve", lambda e: e.tensor_tensor(
                out=dAt[:], in0=dtt[:], in1=Aneg[:].unsqueeze(1).broadcast_to([128, 4, 8]), op=ALU.mult),
                reads=["dtt_f", "Aneg"], writes=["dAt"])

            for bi in range(2):
                blk = grp * 2 + bi
                sc.add("dve", lambda e, blk=blk: e.tensor_reduce(
                    out=kmf[:, :, blk], in_=KT[:, :, blk * 256:(blk + 1) * 256], axis=AX.X, op=ALU.add),
                    reads=[("KT", 2 * blk), ("KT", 2 * blk + 1)], writes=[("kmf", blk)])
                sc.add("dve", lambda e, blk=blk: e.tensor_scalar(
                    out=kmT[:, :, blk:blk + 1], in0=kmf[:, :, blk:blk + 1], scalar1=1.0 / 256, scalar2=0.0,
                    op0=ALU.mult, op1=ALU.add),
                    reads=[("kmf", blk)], writes=[("kmT", blk)])

            for ci in range(4):
                sc.add("dve", lambda e, ci=ci: e.tensor_tensor(
                    out=Dall[:].rearrange("p (h l) -> p h l", h=8), in0=dAt[:, ci, :].unsqueeze(2).broadcast_to([128, 8, 128]),
                    in1=tri[:].unsqueeze(1).broadcast_to([128, 8, 128]), op=ALU.mult),
                    reads=["dAt", "tri"], writes=["Dall"])

                def mm_acs(e, ci=ci):
                    e.matmul(ps[2][:, 0:8], lhsT=tri[:], rhs=dAt[:, ci, :], start=True, stop=True)
                    return e.matmul(ps[2][:, 8:16], lhsT=ones_f[:], rhs=dAt[:, ci, :], start=True, stop=True)
                sc.add("pe", mm_acs, reads=["dAt", "tri", "ones_f"], writes=[("ps", 2)])

                def mm_row(e):
                    r = None
                    for h in range(8):
                        reg = ps[4 + h // 4][:, (h % 4) * 128:(h % 4 + 1) * 128]
                        e.matmul(reg, lhsT=ones_f[:], rhs=Dall[:, h * 128:(h + 1) * 128], start=True, stop=False)
                        r = e.matmul(reg, lhsT=ident[:], rhs=negmask[:], start=False, stop=True)
                    return r
                sc.add("pe", mm_row, reads=["Dall", "ones_f", "ident", "negmask"], writes=[("ps", 4), ("ps", 5)])
                pbc = 3

                def mm_cb(e, ci=ci):
                    r = None
                    for g in range(2):
                        r = e.matmul(ps[pbc][:, g * 128:(g + 1) * 128], lhsT=bcT[:, g, ci * 128:(ci + 1) * 128],
                                     rhs=bcT[:, 2 + g, ci * 128:(ci + 1) * 128], start=True, stop=True)
                    return r
                sc.add("pe", mm_cb, reads=[("bcT", i) for i in range(4)], writes=[("ps", 3)])
                sc.add("dve", lambda e: e.tensor_copy(out=Ein[:, 0:16], in_=ps[2][:, 0:16]),
                       reads=[("ps", 2)], writes=["Ein0"])
                sc.add("dve", lambda e: e.tensor_scalar(out=negacs[:], in0=Ein[:, 0:8], scalar1=-1.0, scalar2=0.0,
                                                        op0=ALU.mult, op1=ALU.add), reads=["Ein0"], writes=["negacs"])
                sc.add("dve", lambda e: e.tensor_tensor(out=Ein[:, 16:24], in0=Ein[:, 8:16], in1=Ein[:, 0:8], op=ALU.subtract),
                       reads=["Ein0"], writes=["Ein1"])
                sc.add("act", lambda e: e.activation(out=Eout[:], in_=Ein[:], func=AF.Exp),
                       reads=["Ein0", "Ein1"], writes=["Eout"])
                for h in range(8):
                    sc.add("act", lambda e, h=h: e.activation(
                        out=seg[:, h, :], in_=ps[4 + h // 4][:, (h % 4) * 128:(h % 4 + 1) * 128], func=AF.Exp,
                        bias=negacs[:, h:h + 1], scale=1.0),
                        reads=[("ps", 4 + h // 4), "negacs"], writes=[("seg", h)])
                for h in range(8):
                    sc.add("dve", lambda e, h=h: e.tensor_tensor(
                        out=MT[:, h, :], in0=seg[:, h, :], in1=ps[pbc][:, (h // 4) * 128:(h // 4 + 1) * 128], op=ALU.mult),
                        reads=[("seg", h), ("ps", 3)], writes=[("MT", h)])
                xv = x_tm[:, ci, :].rearrange("p (h d) -> p h d", h=8)
                sc.add("dve", lambda e, ci=ci, xv=xv: e.tensor_tensor(
                    out=xdt[:].rearrange("p (h d) -> p h d", h=8), in0=xv,
                    in1=dtt[:, ci, :].unsqueeze(2).broadcast_to([128, 8, 64]), op=ALU.mult),
                    reads=[("x_tm", ci), "dtt_f"], writes=["xdt"])
                sc.add("dve", lambda e: e.tensor_tensor(
                    out=xdtd[:].rearrange("p (h d) -> p h d", h=8), in0=xdt[:].rearrange("p (h d) -> p h d", h=8),
                    in1=Eout[:, 16:24].unsqueeze(2).broadcast_to([128, 8, 64]), op=ALU.mult),
                    reads=["xdt", "Eout"], writes=["xdtd"])

                def mm_y(e):
                    r = None
                    for h in range(8):
                        r = e.matmul(ps[6][:, h * 64:(h + 1) * 64], lhsT=MT[:, h, :], rhs=xdt[:, h * 64:(h + 1) * 64],
                                     start=True, stop=True)
                    return r
                sc.add("pe", mm_y, reads=[("MT", h) for h in range(8)] + ["xdt"], writes=[("ps", 6)])

                def mm_off(e, ci=ci):
                    r = None
                    for g in range(2):
                        r = e.matmul(ps[7][:, g * 256:(g + 1) * 256], lhsT=bcT[:, 2 + g, ci * 128:(ci + 1) * 128],
                                     rhs=state_b[:, g * 256:(g + 1) * 256], start=True, stop=True)
                    return r
                sc.add("pe", mm_off, reads=[("bcT", 2), ("bcT", 3), "state_b"], writes=[("ps", 7)])
                sc.add("dve", lambda e: e.tensor_tensor(
                    out=y_sb[:].rearrange("p (h d) -> p h d", h=8), in0=ps[7][:].rearrange("p (h d) -> p h d", h=8),
                    in1=Eout[:, 0:8].unsqueeze(2).broadcast_to([128, 8, 64]), op=ALU.mult),
                    reads=[("ps", 7), "Eout"], writes=["y_sb"])
                sc.add("dve", lambda e: e.tensor_tensor(out=y_sb[:], in0=y_sb[:], in1=ps[6][:], op=ALU.add),
                       reads=[("ps", 6), "y_sb"], writes=["y_sb"])
                sc.add("dve", lambda e, xv=xv: e.tensor_tensor(
                    out=y2[:].rearrange("p (h d) -> p h d", h=8), in0=xv,
                    in1=small[:, gsl("d_skip%d" % l)].unsqueeze(2).broadcast_to([128, 8, 64]), op=ALU.mult),
                    reads=[("x_tm", ci), "small"], writes=["y2"])
                sc.add("dve", lambda e: e.tensor_tensor(out=y_sb[:], in0=y_sb[:], in1=y2[:], op=ALU.add),
                       reads=["y_sb", "y2"], writes=["y_sb"])
                sc.add("dve", lambda e, ci=ci: e.tensor_tensor(out=y_sb[:], in0=y_sb[:], in1=zs[:, ci, :], op=ALU.mult),
                       reads=["y_sb", ("zs", ci, 0), ("zs", ci, 1)], writes=["y_sb"])
                sc.add("dve", lambda e: e.tensor_tensor(out=y2[:], in0=y_sb[:], in1=y_sb[:], op=ALU.mult),
                       reads=["y_sb", "y2"], writes=["y2"])
                sc.add("dve", lambda e: e.tensor_reduce(out=ssq[:, 0:1], in_=y2[:], axis=AX.X, op=ALU.add),
                       reads=["y2"], writes=["ssq0"])
                sc.add("dve", lambda e: e.tensor_scalar(out=ssq[:, 1:2], in0=ssq[:, 0:1], scalar1=1.0 / 512, scalar2=EPS,
                                                        op0=ALU.mult, op1=ALU.add), reads=["ssq0"], writes=["ssq1"])
                sc.add("pool", lambda e: e.tensor_tensor(out=ssq[:, 2:3], in0=ssq[:, 1:2], in1=neghalf[:, 0:1], op=ALU.pow),
                       reads=["ssq1", "neghalf"], writes=["ssq2"])
                sc.add("dve", lambda e: e.tensor_scalar(out=y_sb[:], in0=y_sb[:], scalar1=ssq[:, 2:3], scalar2=0.0,
                                                        op0=ALU.mult, op1=ALU.add), reads=["y_sb", "ssq2"], writes=["y_sb"])
                pb = nb2(0)

                def tr_y(e, pb=pb):
                    r = None
                    for i in range(4):
                        r = e.transpose(out=ps[pb][:, i * 128:(i + 1) * 128], in_=y_sb[:, i * 128:(i + 1) * 128],
                                        identity=ident[:])
                    return r
                sc.add("pe", tr_y, reads=["y_sb", "ident"], writes=[("ps", pb)])
                for i in range(4):
                    sc.add("act", lambda e, i=i, ci=ci, pb=pb: e.activation(
                        out=ycatT[:, i, ci * 128:(ci + 1) * 128], in_=ps[pb][:, i * 128:(i + 1) * 128], func=AF.Copy,
                        scale=scol("ssd_norm%d" % l, i)),
                        reads=[("ps", pb), "small"], writes=[("ycatT", i, ci)])
                pbs = 2

                def mm_st(e, ci=ci):
                    r = None
                    for g in range(2):
                        r = e.matmul(ps[1][:, g * 256:(g + 1) * 256], lhsT=B_tm[:, ci, g * 128:(g + 1) * 128],
                                     rhs=xdtd[:, g * 256:(g + 1) * 256], start=True, stop=True)
                    return r
                sc.add("pe", mm_st, reads=[("B_tm", ci), "xdtd"], writes=[("ps", 1)])
                sc.add("dve", lambda e: e.tensor_tensor(
                    out=state_f[:].rearrange("p (h d) -> p h d", h=8), in0=state_f[:].rearrange("p (h d) -> p h d", h=8),
                    in1=Eout[:, 8:16].unsqueeze(2).broadcast_to([128, 8, 64]), op=ALU.mult),
                    reads=["state_f", "Eout", "state_b"], writes=["state_f"])
                sc.add("dve", lambda e: e.tensor_tensor(out=state_f[:], in0=state_f[:], in1=ps[1][:], op=ALU.add),
                       reads=["state_f", ("ps", 1)], writes=["state_f"])
                sc.add("pool", lambda e: e.tensor_copy(out=state_b[:], in_=state_f[:]),
                       reads=["state_f"], writes=["state_b"])

            for ti in range(4):
                tile = t0 + ti
                pb = nb2(1)

                def mm_pool(e, ti=ti, tile=tile, pb=pb):
                    r = None
                    for g in range(4):
                        bc = (3 * g + (1 if tile == 0 else 0)) * 128
                        bp = (3 * g + 2) * 128
                        first = tile == 0
                        r = e.matmul(ps[pb][:, g * 64:(g + 1) * 64], lhsT=bands[:, bc:bc + 128],
                                     rhs=u_tm[:, ti, g * 64:(g + 1) * 64], start=True, stop=first)
                        if not first:
                            prev = ti - 1 if ti > 0 else 4
                            r = e.matmul(ps[pb][:, g * 64:(g + 1) * 64], lhsT=bands[:, bp:bp + 128],
                                         rhs=u_tm[:, prev, g * 64:(g + 1) * 64], start=False, stop=True)
                    return r
                sc.add("pe", mm_pool, reads=["bands", ("u_tm", ti), ("u_tm", ti - 1 if ti > 0 else 4)], writes=[("ps", pb)])
                sc.add("act", lambda e, pb=pb: e.copy(out=d_tm[:], in_=ps[pb][:, 0:256]),
                       reads=[("ps", pb)], writes=["d_tm"])
                pb2 = nb2(1)

                def tr_d(e, pb2=pb2):
                    r = None
                    for i in range(2):
                        r = e.transpose(out=ps[pb2][:, i * 128:(i + 1) * 128], in_=d_tm[:, i * 128:(i + 1) * 128],
                                        identity=ident[:])
                    return r
                sc.add("pe", tr_d, reads=["d_tm", "ident"], writes=[("ps", pb2)])
                sc.add("act", lambda e, pb2=pb2, ti=ti: e.copy(
                    out=dT[:, :, ti * 128:(ti + 1) * 128], in_=ps[pb2][:, 0:256].rearrange("p (a t) -> p a t", a=2)),
                    reads=[("ps", pb2)], writes=[("dT", ti)])
            for pair in range(2):
                pb = nb2(0)
                sc.add("pe", lambda e, pair=pair, pb=pb: e.matmul(
                    ps[pb][:], lhsT=wp[:, pair * 128:(pair + 1) * 128], rhs=dT[:, pair, :], start=True, stop=True),
                    reads=["wp"] + [("dT", ti) for ti in range(4)], writes=[("ps", pb)])
                sc.add("act", lambda e, pair=pair, pb=pb: e.activation(
                    out=ycatT[:, 4 + pair, :], in_=ps[pb][:], func=AF.Copy, scale=scol("pool_scale%d" % l, pair)),
                    reads=[("ps", pb), "small"], writes=[("ycatT", 4 + pair, ci) for ci in range(4)])

            need_sel = (grp * 2) >= 4
            if need_sel:
                for ti in range(4):
                    qb = (t0 + ti) // 2
                    pb = nb2(1)

                    def mm_gate(e, ti=ti, qb=qb, pb=pb):
                        r = None
                        for h in range(4):
                            rows = slice((h % 2) * 64, (h % 2) * 64 + 64)
                            r = e.matmul(ps[pb][:, h * 8:h * 8 + qb], lhsT=QT[rows, h // 2, ti * 128:(ti + 1) * 128],
                                         rhs=kmT[rows, h // 2, 0:qb], start=True, stop=True)
                        return r
                    sc.add("pe", mm_gate, reads=[("QT", ti)] + [("kmT", b) for b in range(qb)], writes=[("ps", pb)])
                    sc.add("dve", lambda e, pb=pb, qb=qb: e.tensor_copy(
                        out=gate[:, :, 0:qb], in_=ps[pb][:, 0:32].rearrange("p (h n) -> p h n", h=4)[:, :, 0:qb]),
                        reads=[("ps", pb)], writes=["gate"])
                    sc.add("dve", lambda e, qb=qb: e.tensor_tensor(
                        out=cmpb[:, :, 0:qb, 0:qb],
                        in0=gate[:, :, 0:qb].unsqueeze(2).broadcast_to([128, 4, qb, qb]),
                        in1=gate[:, :, 0:qb].unsqueeze(3).broadcast_to([128, 4, qb, qb]), op=ALU.is_gt),
                        reads=["gate"], writes=["cmpb"])
                    sc.add("dve", lambda e, qb=qb: e.tensor_reduce(
                        out=cnt[:, :, 0:qb], in_=cmpb[:, :, 0:qb, 0:qb], axis=AX.X, op=ALU.add),
                        reads=["cmpb"], writes=["cnt"])
                    sc.add("dve", lambda e, ti=ti: e.memset(bias_tm[:, ti, :, :], 0.0), writes=[("bias_tm", ti)])
                    sc.add("dve", lambda e, ti=ti, qb=qb: e.tensor_scalar(
                        out=bias_tm[:, ti, :, 0:qb], in0=cnt[:, :, 0:qb], scalar1=2.5, scalar2=-30000.0,
                        op0=ALU.is_gt, op1=ALU.mult),
                        reads=["cnt", ("bias_tm", ti)], writes=[("bias_tm", ti)])
            for h in range(4):
                hp = h // 2
                rows = slice((h % 2) * 64, (h % 2) * 64 + 64)
                bs = h % 2
                if need_sel:
                    pb = nb2(1)

                    def tr_b(e, h=h, pb=pb):
                        r = None
                        for ti in range(4):
                            r = e.transpose(out=ps[pb][0:8, ti * 128:(ti + 1) * 128], in_=bias_tm[:, ti, h, :],
                                            identity=ident[:])
                        return r
                    sc.add("pe", tr_b, reads=[("bias_tm", ti) for ti in range(4)] + ["ident"], writes=[("ps", pb)])
                    sc.add("act", lambda e, pb=pb, bs=bs: e.copy(out=biasT[bs][:], in_=ps[pb][0:8, :]),
                           reads=[("ps", pb)], writes=[("biasT", bs)])
                acc = 6 + (h % 2)
                for kt in range(t0 + 4):
                    kb = kt // 2
                    if kt < t0:
                        c0 = 0
                    else:
                        c0 = (kt - t0) * 128
                    pb = 4 + (kt % 2)
                    past_cols = None
                    if need_sel:
                        if kb < grp * 2:
                            past_cols = (0, 512)
                        elif kb == grp * 2:
                            past_cols = (256, 512)

                    if kt < t0:
                        segs = [(0, 512, need_sel, False)]
                    else:
                        d0 = (kt - t0) * 128
                        segs = [(d0, d0 + 128, False, True)]
                        blk_end = 256 if (kt - t0) < 2 else 512
                        if d0 + 128 < blk_end:
                            segs.append((d0 + 128, blk_end, False, False))
                        if blk_end < 512:
                            if need_sel:
                                segs.append((blk_end, 512, True, False))
                            elif len(segs) > 1:
                                segs[-1] = (segs[-1][0], 512, False, False)
                            else:
                                segs.append((blk_end, 512, False, False))

                    def mm_s(e, kt=kt, pb=pb, segs=segs, rows=rows, hp=hp, kb=kb, bs=bs):
                        r = None
                        for (a0, a1, hb, hc) in segs:
                            reg = ps[pb][:, a0:a1]
                            r = e.matmul(reg, lhsT=KT[rows, hp, kt * 128:(kt + 1) * 128],
                                         rhs=QT[rows, hp, a0:a1], start=True, stop=not (hb or hc))
                            if hb:
                                r = e.matmul(reg, lhsT=esel[:, kb * 128:(kb + 1) * 128],
                                             rhs=biasT[bs][:, a0:a1], start=False, stop=True)
                            if hc:
                                r = e.matmul(reg, lhsT=ident_bf[:], rhs=cmask_bf[:], start=False, stop=True)
                        return r
                    rd = [("KT", kt)] + [("QT", ti) for ti in range(4)] + ["esel", ("biasT", bs), "ident_bf", "cmask"]
                    sc.add("pe", mm_s, reads=rd, writes=[("ps", pb)])
                    pt = kt % 2
                    sc.add("act", lambda e, pb=pb, c0=c0, pt=pt: e.activation(
                        out=PT[pt][:, c0:512], in_=ps[pb][:, c0:512], func=AF.Exp, scale=0.125),
                        reads=[("ps", pb)], writes=[("PT", pt)])

                    def mm_pv(e, kt=kt, c0=c0, pt=pt, h=h, t0=t0):
                        r = None
                        for ti in range(c0 // 128, 4):
                            tq = t0 + ti
                            r = e.matmul(ps[ti][:, 0:65], lhsT=PT[pt][:, ti * 128:(ti + 1) * 128],
                                         rhs=vp[:, kt, h, :], start=(kt == 0), stop=(kt == tq))
                        return r
                    sc.add("pe", mm_pv, reads=[("PT", pt), ("vp", kt)], writes=[("ps", ti) for ti in range(c0 // 128, 4)])
                for ti in range(4):
                    sc.add("dve", lambda e, ti=ti: e.reciprocal(out=rden[:, ti:ti + 1], in_=ps[ti][:, 64:65]),
                           reads=[("ps", ti)], writes=[("rden", ti)], name="recip g%d h%d ti%d" % (grp, h, ti))
                    sc.add("dve", lambda e, ti=ti, h=h: e.tensor_scalar(
                        out=att_tm[:, ti, h * 64:(h + 1) * 64], in0=ps[ti][:, 0:64], scalar1=rden[:, ti:ti + 1], scalar2=0.0,
                        op0=ALU.mult, op1=ALU.add),
                        reads=[("ps", ti), ("rden", ti)], writes=[("att_tm", h, ti)])
            for ti in range(4):
                pb = nb2(1)

                def tr_a(e, ti=ti, pb=pb):
                    r = None
                    for i in range(2):
                        r = e.transpose(out=ps[pb][:, i * 128:(i + 1) * 128], in_=att_tm[:, ti, i * 128:(i + 1) * 128],
                                        identity=ident[:])
                    return r
                sc.add("pe", tr_a, reads=[("att_tm", h, ti) for h in range(4)] + ["ident"], writes=[("ps", pb)])
                sc.add("act", lambda e, pb=pb, ti=ti: e.copy(
                    out=ycatT[:, 6:8, ti * 128:(ti + 1) * 128], in_=ps[pb][:, 0:256].rearrange("p (a t) -> p a t", a=2)),
                    reads=[("ps", pb)], writes=[("ycatT", 6, ti), ("ycatT", 7, ti)])

            ycr = [("ycatT", c, ci) for c in range(8) for ci in range(4)]
            for half in range(4):
                ws = load_w(wout_d[(l * 4 + half) * 128:(l * 4 + half + 1) * 128, :])
                for mm_ in range(2):
                    m = half * 2 + mm_
                    pb = nb2(0)

                    def mmo(e, ws=ws, mm_=mm_, pb=pb):
                        r = None
                        for k in range(NC8):
                            r = e.matmul(ps[pb][:], lhsT=wsl[ws][:, (mm_ * 8 + k) * 128:(mm_ * 8 + k + 1) * 128],
                                         rhs=ycatT[:, k, :], start=(k == 0), stop=(k == NC8 - 1))
                        return r
                    sc.add("pe", mmo, reads=[("wsl", ws)] + ycr, writes=[("ps", pb)])
                    s2 = m % 2
                    sc.add("dve", lambda e, pb=pb, m=m: e.tensor_copy(out=mT[:, m, :], in_=ps[pb][:]),
                           reads=[("ps", pb)], writes=[("xbcf", m)])
                    sc.add("act", lambda e, m=m, s2=s2: e.activation(out=sq[s2][:], in_=mT[:, m, :], func=AF.Square),
                           reads=[("xbcf", m)], writes=[("sq", s2)])
                    sc.add("pe", lambda e, s2=s2, m=m: e.matmul(
                        ps[7][:], lhsT=ones_bf[:], rhs=sq[s2][:], start=(m == 0), stop=(m == NC8 - 1)),
                        reads=[("sq", s2), "ones"], writes=[("ps", 7)])
            rstd_op(1, 7)
            postnorm_add(grp, 1, gpost, lambda m: mT[:, m, :], lambda m: ("xbcf", m))

    if stage == "all":
        plan = [(l, ["f1", "m", "f2"]) for l in range(DEPTH)]
    elif stage == "none":
        plan = []
    else:
        lay, parts = stage.split(":")
        plan = [(int(lay), parts.split("+"))]
    for l, stages in plan:
        if "f1" in stages:
            ffn(l, 0, "ff1_norm_pre", "ff1_norm_post")
        if "m" in stages:
            barrier()
            mixer(l)
            barrier()
        if "f2" in stages:
            ffn(l, 1, "ff2_norm_pre", "ff2_norm_post")

    barrier()
    for t in range(S // 128):
        slot = t % 2
        for q in range(2):
            pb = 6 + q

            def tr2(e, q=q, t=t, pb=pb):
                r = None
                for i in range(4):
                    c = 4 * q + i
                    r = e.transpose(out=ps[pb][:, i * 128:(i + 1) * 128], in_=xT[:, c, t * 128:(t + 1) * 128],
                                    identity=ident[:])
                return r
            sc.add("pe", tr2, reads=[xkey(4 * q + i, t // 4) for i in range(4)] + ["ident"], writes=[("ps", pb)])
            sc.add("act", lambda e, q=q, slot=slot, pb=pb: e.copy(
                out=xio[slot][:, q * 512:(q + 1) * 512], in_=ps[pb][:]),
                reads=[("ps", pb)], writes=[("xio", slot, q)])
        sc.add("sp", lambda e, t=t, slot=slot: e.dma_start(out=out_d[t * 128:(t + 1) * 128, :], in_=xio[slot]),
               reads=[("xio", slot, 0), ("xio", slot, 1)], writes=[("out", t)], dma=("out", slot))
    sc.add("sp", lambda e: e.nop(), reads=[("out", t) for t in range(S // 128)], writes=[])
    sc.emit()
    return nc


def prep_weights(inp):
    wgu = np.empty((DEPTH, 2, NJ, 128, 2, 8, 128), np.float32)
    wd = np.empty((DEPTH, 2, NC8, 128, NJ, 128), np.float32)
    win = np.empty((DEPTH, WIN_PIECES, 128, 2048), np.float32)
    wdt = np.empty((DEPTH, 128, 64), np.float32)
    wout = np.empty((DEPTH, 4, 128, 2, 8, 128), np.float32)
    wp = np.zeros((DEPTH, 128, 2, 128), np.float32)
    for l in range(DEPTH):
        for f, pfx in enumerate(("ff1", "ff2")):
            gate_up = {"ff1": (inp["ff1_w_gate"], inp["ff1_w_up"]), "ff2": (inp["ff2_w_gate"], inp["ff2_w_up"])}[pfx]
            for g in range(2):
                w = np.asarray(gate_up[g][l])
                wgu[l, f, :, :, g] = w.reshape(8, 128, NJ, 128).transpose(2, 1, 0, 3)
            w = np.asarray({"ff1": inp["ff1_w_down"], "ff2": inp["ff2_w_down"]}[pfx][l])
            wd[l, f] = w.reshape(NJ, 128, NC8, 128).transpose(2, 1, 0, 3)
        w = np.asarray(inp["w_in"][l])
        wk = w.reshape(8, 128, 2568)
        for fp in range(4):
            blk = wk[:, :, 512 + fp * 256:512 + (fp + 1) * 256].reshape(8, 128, 2, 128)
            win[l, fp] = blk.transpose(1, 2, 0, 3).reshape(128, 2048)
        wt = wk[:, :, TCOLS]
        for tp in range(6):
            blk = wt[:, :, tp * 256:(tp + 1) * 256]
            win[l, 4 + tp] = blk.transpose(1, 0, 2).reshape(128, 2048)
        wdt[l] = wk[:, :, 1536:1544].transpose(1, 0, 2).reshape(128, 64)
        wo = np.asarray(inp["w_out"][l]).reshape(8, 128, 4, 2, 128)
        wout[l] = wo.transpose(2, 1, 3, 0, 4)
        pw = np.asarray(inp["pool_w"][l])
        for g in range(4):
            pair, i = g // 2, g % 2
            wp[l, i * 64:(i + 1) * 64, pair, i * 64:(i + 1) * 64] = pw[g]
    return dict(
        wgu=wgu.reshape(DEPTH * 2 * NJ * 128, 2 * 8 * 128), wd=wd.reshape(DEPTH * 2 * NC8 * 128, NJ * 128),
        win=win.reshape(DEPTH * WIN_PIECES * 128, 2048), wdt=wdt.reshape(DEPTH * 128, 64),
        wout=wout.reshape(DEPTH * 4 * 128, 2048), wp=wp.reshape(DEPTH * 128, 256))


LAUNCH_PLAN = ["all"]


def kernel(stage=None, **inp):
    inp = {k: np.asarray(v) for k, v in inp.items()}
    small = pack_small(inp)
    assert small.shape[1] == NSMALL
    shared = prep_weights(inp)
    shared.update(make_consts())
    shared["small"] = small
    shared["ident"] = np.eye(128, dtype=np.float32)
    x = inp["x"]
    plan = LAUNCH_PLAN if stage is None else [stage]
    progs = {}
    for st in plan:
        if st not in progs:
            progs[st] = build(st)
        nc = progs[st]
        in_maps = [dict(shared, x=np.ascontiguousarray(x[b])) for b in range(8)]
        res = run_bass_kernel_spmd(nc, in_maps, core_ids=list(range(8)))
        x = np.stack([r["out"] for r in res.results], axis=0)
    return x
```

```python
import numpy as np
import concourse.bass as bass
import concourse.mybir as mybir
from concourse.bass_utils import run_bass_kernel_spmd

F32 = mybir.dt.float32
BF16 = mybir.dt.bfloat16
AF = mybir.ActivationFunctionType
ALU = mybir.AluOpType
AX = mybir.AxisListType

S = 2048
D = 1024
DFF = 2816
NJ = DFF // 128
NC8 = D // 128
DEPTH = 2
EPS = 1e-6
TG = 1024
NTG = S // TG
NSG = TG // 512


class Op:
    __slots__ = ("eng", "fn", "deps", "signal", "count", "dkey", "ndma", "name", "epoch")


class Sched:
    ENGS = ("pe", "act", "dve", "pool", "sp")

    def __init__(self, nc):
        self.nc = nc
        self.ops = {e: [] for e in self.ENGS}
        self.last_writer = {}
        self.readers = {}
        self.dcount = {}
        self.nops = 0
        self.epoch = 0

    def add(self, eng, fn, reads=(), writes=(), dma=None, ndma=1, name=""):
        op = Op()
        op.eng = eng
        op.fn = fn
        op.signal = False
        op.count = None
        op.dkey = dma
        op.ndma = ndma
        op.name = name
        op.epoch = self.epoch
        if dma is not None:
            self.dcount[dma] = self.dcount.get(dma, 0) + 16 * ndma
            op.count = self.dcount[dma]
        deps = {}
        raw = set()
        for b in reads:
            w = self.last_writer.get(b)
            if w is not None:
                deps[id(w)] = w
                raw.add(id(w))
        for b in writes:
            w = self.last_writer.get(b)
            if w is not None:
                deps[id(w)] = w
            rd = self.readers.get(b)
            if rd:
                for r in rd.values():
                    if isinstance(r, list):
                        for rr in r:
                            deps[id(rr)] = rr
                    else:
                        deps[id(r)] = r
        fdeps = []
        for k, d in deps.items():
            if d.dkey is not None:
                fdeps.append(d)
                continue
            if d.eng == eng:
                if eng == "pe":
                    continue
                fdeps.append(d)
                continue
            fdeps.append(d)
        for d in fdeps:
            d.signal = True
        op.deps = fdeps
        for b in reads:
            rd = self.readers.setdefault(b, {})
            if dma is not None:
                rd.setdefault("dma", []).append(op)
            else:
                rd[eng] = op
        for b in writes:
            self.last_writer[b] = op
            self.readers[b] = {}
        self.ops[eng].append(op)
        self.nops += 1
        return op

    def emit(self):
        nc = self.nc
        esem = {(e, ep): nc.alloc_semaphore("sem_%s_%d" % (e, ep)) for e in self.ENGS for ep in range(self.epoch + 1)}
        dsem = {k: nc.alloc_semaphore("dsem_%d" % i) for i, k in enumerate(self.dcount)}
        for e in self.ENGS:
            c = {}
            for op in self.ops[e]:
                if op.dkey is None and op.signal:
                    c[op.epoch] = c.get(op.epoch, 0) + 1
                    op.count = c[op.epoch]
        ops = self.ops

        def run(eng_name, eng):
            waited = {}
            for op in ops[eng_name]:
                for d in op.deps:
                    if d.dkey is not None:
                        sem = dsem[d.dkey]
                    else:
                        sem = esem[(d.eng, d.epoch)]
                    if waited.get(sem.num, 0) < d.count:
                        eng.wait_ge(sem, d.count)
                        waited[sem.num] = d.count
                r = op.fn(eng)
                if op.name:
                    rr_ = r[-1] if isinstance(r, (list, tuple)) else r
                    rr_.annotate(op.name)
                if op.dkey is not None:
                    if not isinstance(r, (list, tuple)):
                        r = [r]
                    assert len(r) == op.ndma, (op.name, len(r), op.ndma)
                    for ins in r:
                        ins.then_inc(dsem[op.dkey], 16)
                elif op.signal:
                    if isinstance(r, (list, tuple)):
                        r = r[-1]
                    r.then_inc(esem[(eng_name, op.epoch)], 1)

        with nc.Block() as block:
            @block.tensor
            def _(e):
                run("pe", e)

            @block.scalar
            def _(e):
                run("act", e)

            @block.vector
            def _(e):
                run("dve", e)

            @block.gpsimd
            def _(e):
                run("pool", e)

            @block.sync
            def _(e):
                run("sp", e)


def _colmajor(v):
    return np.ascontiguousarray(v.reshape(-1, 128).T)


SM_OFF = {}


def pack_small(inp):
    cols = []
    off = 0

    def put(name, arr):
        nonlocal off
        SM_OFF[name] = (off, arr.shape[1])
        cols.append(arr.astype(np.float32))
        off += arr.shape[1]

    for l in range(DEPTH):
        for nm in ("ff1_norm_pre", "ff1_norm_post", "mix_norm_pre", "mix_norm_post", "ff2_norm_pre", "ff2_norm_post"):
            put("%s%d" % (nm, l), _colmajor(inp[nm][l]))
    for l in range(DEPTH):
        put("conv_b%d" % l, _colmajor(inp["conv_b"][l]))
        for j in range(4):
            put("conv_w%d_%d" % (j, l), _colmajor(inp["conv_w"][l, j]))
        put("ssd_norm%d" % l, _colmajor(inp["ssd_norm"][l]))
        put("pool_scale%d" % l, _colmajor(inp["pool_scale"][l]))
        for nm in ("dt_bias", "a_log", "d_skip"):
            put("%s%d" % (nm, l), np.broadcast_to(inp[nm][l][None, :], (128, 8)))
    return np.ascontiguousarray(np.concatenate(cols, axis=1))


NSMALL = 6 * DEPTH * 8 + DEPTH * (8 + 32 + 4 + 2 + 24)
WIN_PIECES = 10
TCOLS = list(range(0, 512)) + list(range(1800, 2312)) + list(range(2312, 2568)) + list(range(1544, 1800))


def make_consts():
    c = {}
    k = np.arange(128)
    c["tri"] = (k[:, None] <= k[None, :]).astype(np.float32)
    c["negmask"] = np.where(k[None, :] >= k[:, None], 0.0, -30000.0).astype(np.float32)
    inv_freq = 500000.0 ** (-np.arange(0, 16, 2, dtype=np.float32) / 16.0)
    pos = (np.arange(16)[None, :, None] * 128 + k[:, None, None]).astype(np.float32)
    ang = pos * inv_freq[None, None, :]
    c["rope"] = np.concatenate([np.cos(ang), np.sin(ang)], axis=-1).reshape(128, 16 * 16).astype(np.float32)
    bands = []
    for w in (2, 4, 8, 16):
        s_ = k[:, None]
        t_ = k[None, :]
        cur = ((s_ <= t_) & (t_ - s_ < w)).astype(np.float32) / w - (s_ == t_)
        cur0 = ((s_ <= t_) & (t_ - s_ < w)).astype(np.float32) / np.minimum(t_ + 1, w) - (s_ == t_)
        prev = (((t_ + 128 - s_) < w)).astype(np.float32) / w
        bands += [cur, cur0, prev]
    c["bands"] = np.concatenate(bands, axis=1).astype(np.float32)
    es = np.zeros((8, 8, 128), np.float32)
    for r in range(8):
        es[r, r, :] = 1.0
    c["esel"] = es.reshape(8, 1024)
    return c


import os
ADD_ENG = os.environ.get("KADD", "dve")


def build(stage="all"):
    nc = bass.Bass("TRN2", target_bir_lowering=False)
    sc = Sched(nc)

    x_d = nc.dram_tensor("x", [S, D], F32, kind="ExternalInput").ap()
    out_d = nc.dram_tensor("out", [S, D], F32, kind="ExternalOutput").ap()
    small_d = nc.dram_tensor("small", [128, NSMALL], F32, kind="ExternalInput").ap()
    ident_d = nc.dram_tensor("ident", [128, 128], F32, kind="ExternalInput").ap()
    tri_d = nc.dram_tensor("tri", [128, 128], F32, kind="ExternalInput").ap()
    negmask_d = nc.dram_tensor("negmask", [128, 128], F32, kind="ExternalInput").ap()
    rope_d = nc.dram_tensor("rope", [128, 256], F32, kind="ExternalInput").ap()
    bands_d = nc.dram_tensor("bands", [128, 12 * 128], F32, kind="ExternalInput").ap()
    esel_d = nc.dram_tensor("esel", [8, 1024], F32, kind="ExternalInput").ap()
    wgu_d = nc.dram_tensor("wgu", [DEPTH * 2 * NJ * 128, 2 * 8 * 128], F32, kind="ExternalInput").ap()
    wd_d = nc.dram_tensor("wd", [DEPTH * 2 * NC8 * 128, NJ * 128], F32, kind="ExternalInput").ap()
    win_d = nc.dram_tensor("win", [DEPTH * WIN_PIECES * 128, 2048], F32, kind="ExternalInput").ap()
    wdt_d = nc.dram_tensor("wdt", [DEPTH * 128, 64], F32, kind="ExternalInput").ap()
    wout_d = nc.dram_tensor("wout", [DEPTH * 4 * 128, 2048], F32, kind="ExternalInput").ap()
    wp_d = nc.dram_tensor("wp", [DEPTH * 128, 256], F32, kind="ExternalInput").ap()

    def sb(name, shape, dt):
        return nc.alloc_sbuf_tensor(name, shape, dt)

    xT = sb("xT", [128, NC8, S], F32)
    small = sb("small_sb", [128, NSMALL], F32)
    g32 = sb("g32", [128, 6 * DEPTH * 8], F32)
    ident = sb("ident_sb", [128, 128], F32)
    ones_bf = sb("ones_bf", [128, 128], BF16)
    ones_f = sb("ones_f", [128, 128], F32)
    neghalf = sb("neghalf", [128, 8], F32)
    epsc = sb("epsc", [128, 8], F32)
    sq = [sb("sq%d" % i, [128, 512], BF16) for i in range(2)]
    rstd = [sb("rstd%d" % i, [128, 512], F32) for i in range(2)]
    tmp = [sb("tmp%d" % i, [128, 512], F32) for i in range(2)]
    tri = sb("tri_sb", [128, 128], F32)
    negmask = sb("negmask_sb", [128, 128], F32)
    cmask_bf = sb("cmask_bf", [128, 128], BF16)
    ident_bf = sb("ident_bf", [128, 128], BF16)
    rope = sb("rope_sb", [128, 256], F32)
    bands = sb("bands_sb", [128, 12 * 128], BF16)
    esel = sb("esel_sb", [8, 1024], BF16)
    ptmp = tmp
    ps = [nc.alloc_psum_tensor("ps%d" % i, [128, 512], F32) for i in range(8)]

    with nc.reset_on_exit():
        hnT = sb("hnT", [128, NC8, TG], BF16)
        aT = sb("aT", [128, NJ, TG], BF16)
        fT = sb("fT", [128, NC8, TG], F32)
        sil = [sb("sil%d" % i, [128, 512], F32) for i in range(2)]
        wgu = [sb("wgu%d" % i, [128, 2 * 8 * 128], BF16) for i in range(2)]
        wd = [sb("wd%d" % i, [128, NJ * 128], BF16) for i in range(2)]
        xio = [fT[:, i, :] for i in range(2)]
    hnTm = sb("hnTm", [128, NC8, 512], BF16)
    KT = sb("KT", [128, 2, S], BF16)
    vp = sb("vp", [128, 16, 4, 65], BF16)
    wsl = [sb("wsl%d" % i, [128, 2048], BF16) for i in range(2)]
    wdt = sb("wdt_sb", [128, 64], BF16)
    wp = sb("wp_sb", [128, 256], BF16)
    pre = [sb("pre%d" % i, [128, 515], F32) for i in range(2)]
    carry = sb("carry", [128, 8, 3], F32)
    cacc = [sb("cacc%d" % i, [128, 512], F32) for i in range(2)]
    xbcf = sb("xbcf", [128, 8, 512], F32)
    bcT = sb("bcT", [128, 4, 512], BF16)
    x_tm = sb("x_tm", [128, 4, 512], BF16)
    B_tm = sb("B_tm", [128, 4, 256], BF16)
    zs = sb("zs", [128, 4, 512], F32)
    dtt = sb("dtt", [128, 4, 8], F32)
    dAt = sb("dAt", [128, 4, 8], F32)
    Aneg = sb("Aneg", [128, 8], F32)
    qkf = sb("qkf", [128, 256], F32)
    ropet = [sb("ropet%d" % i, [128, 8, 8], F32) for i in range(4)]
    QT = sb("QT", [128, 2, 512], BF16)
    u_tm = sb("u_tm", [128, 5, 256], BF16)
    kmf = sb("kmf", [128, 2, 8], F32)
    kmT = sb("kmT", [128, 2, 8], BF16)
    state_f = sb("state_f", [128, 512], F32)
    state_b = sb("state_b", [128, 512], BF16)
    Dall = sb("Dall", [128, 1024], F32)
    seg = sb("seg", [128, 8, 128], BF16)
    MT = sb("MT", [128, 8, 128], BF16)
    Ein = sb("Ein", [128, 24], F32)
    Eout = sb("Eout", [128, 24], F32)
    negacs = sb("negacs", [128, 8], F32)
    xdt = sb("xdt", [128, 512], BF16)
    xdtd = sb("xdtd", [128, 512], BF16)
    y_sb = sb("y_sb", [128, 512], F32)
    y2 = sb("y2", [128, 512], F32)
    ssq = sb("ssq", [128, 4], F32)
    d_tm = sb("d_tm", [128, 256], F32)
    dT = sb("dT", [128, 2, 512], BF16)
    PT = [sb("PT%d" % i, [128, 512], BF16) for i in range(2)]
    gate = sb("gate", [128, 4, 8], F32)
    cmpb = sb("cmpb", [128, 4, 8, 8], F32)
    cnt = sb("cnt", [128, 4, 8], F32)
    bias_tm = sb("bias_tm", [128, 4, 4, 8], F32)
    biasT = [sb("biasT%d" % i, [8, 512], BF16) for i in range(2)]
    att_tm = sb("att_tm", [128, 4, 256], F32)
    rden = sb("rden", [128, 4], F32)
    ycatT = sb("ycatT", [128, NC8, 512], BF16)
    mT = xbcf

    def gsl(name):
        o, n = SM_OFF[name]
        return slice(o, o + n)

    def scol(name, i=0):
        o, n = SM_OFF[name]
        return small[:, o + i:o + i + 1]

    sc.add("sp", lambda e: e.dma_start(out=small[:], in_=small_d), writes=["small"], dma="small")
    sc.add("sp", lambda e: e.dma_start(out=ident[:], in_=ident_d), writes=["ident"], dma="ident")
    sc.add("sp", lambda e: e.dma_start(out=tri[:], in_=tri_d), writes=["tri"], dma="tri")
    sc.add("sp", lambda e: e.dma_start(out=negmask[:], in_=negmask_d), writes=["negmask"], dma="negmask")
    sc.add("sp", lambda e: e.dma_start(out=rope[:], in_=rope_d), writes=["rope"], dma="rope")
    sc.add("pool", lambda e: e.dma_start(out=bands[:], in_=bands_d), writes=["bands"], dma="bands")
    sc.add("pool", lambda e: e.dma_start(out=esel[:], in_=esel_d), writes=["esel"], dma="esel")
    sc.add("dve", lambda e: e.memset(ones_bf[:], 1.0), writes=["ones"])
    sc.add("dve", lambda e: e.memset(ones_f[:], 1.0), writes=["ones_f"])
    sc.add("pool", lambda e: e.memset(neghalf[:], -0.5), writes=["neghalf"])
    sc.add("dve", lambda e: e.memset(epsc[:], EPS), writes=["epsc"])
    sc.add("dve", lambda e: e.tensor_copy(out=cmask_bf[:], in_=negmask[:]), reads=["negmask"], writes=["cmask"])
    sc.add("dve", lambda e: e.tensor_copy(out=ident_bf[:], in_=ident[:]), reads=["ident"], writes=["ident_bf"])
    for l in range(DEPTH):
        for i, (nm, coef) in enumerate((("ff1_norm_pre", 1.0), ("ff1_norm_post", 0.5), ("mix_norm_pre", 1.0),
                                        ("mix_norm_post", 1.0), ("ff2_norm_pre", 1.0), ("ff2_norm_post", 0.5))):
            s_ = gsl("%s%d" % (nm, l))
            sc.add("dve", lambda e, s_=s_, coef=coef: e.tensor_scalar(
                out=g32[:, s_], in0=small[:, s_], scalar1=coef, scalar2=0.0, op0=ALU.mult, op1=ALU.add),
                reads=["small"], writes=[("g32", s_.start)])

    def barrier():
        lasts = []
        for e_ in sc.ENGS:
            for o_ in reversed(sc.ops[e_]):
                if o_.dkey is None:
                    lasts.append(o_)
                    break
        sc.epoch += 1
        for e in ("pe", "act", "dve", "pool", "sp"):
            op = sc.add(e, lambda eng: eng.nop(), name="barrier")
            for d in lasts:
                if d is not op:
                    d.signal = True
                    op.deps.append(d)

    def xkey(c, t512):
        return ("xT", c, t512)

    for t in range(S // 128):
        slot = t % 2
        sc.add("sp", lambda e, t=t, slot=slot: e.dma_start(out=xio[slot], in_=x_d[t * 128:(t + 1) * 128, :]),
               writes=[("xio", slot)], dma=("xio", slot))
        for q in range(2):
            pb = 6 + q

            def tr(e, q=q, slot=slot, pb=pb):
                r = None
                for i in range(4):
                    c = 4 * q + i
                    r = e.transpose(out=ps[pb][:, i * 128:(i + 1) * 128], in_=xio[slot][:, c * 128:(c + 1) * 128],
                                    identity=ident[:])
                return r
            sc.add("pe", tr, reads=[("xio", slot), "ident"], writes=[("ps", pb)])
            sc.add("dve", lambda e, q=q, t=t, pb=pb: e.tensor_copy(
                out=xT[:, 4 * q:4 * q + 4, t * 128:(t + 1) * 128],
                in_=ps[pb][:].rearrange("p (c t) -> p c t", c=4)),
                reads=[("ps", pb)], writes=[xkey(4 * q + i, t // 4) for i in range(4)])

    barrier()

    def rstd_op(sg, pb):
        sc.add("act", lambda e, sg=sg, pb=pb: e.activation(
            out=tmp[sg][:], in_=ps[pb][:], func=AF.Sqrt, scale=1.0 / 1024, bias=epsc[:, 0:1]),
            reads=[("ps", pb), "epsc"], writes=[("tmp", sg)])
        sc.add("dve", lambda e, sg=sg: e.reciprocal(out=rstd[sg][:], in_=tmp[sg][:]),
               reads=[("tmp", sg)], writes=[("rstd", sg)])

    def prenorm(t512, sg, gpre, dst, dkey, pbs=None):
        pbs = 6 + sg if pbs is None else pbs
        cols = slice(t512 * 512, (t512 + 1) * 512)
        for c in range(NC8):
            s2 = c % 2
            sc.add("act", lambda e, c=c, cols=cols, s2=s2: e.activation(
                out=sq[s2][:], in_=xT[:, c, cols], func=AF.Square),
                reads=[xkey(c, t512)], writes=[("sq", s2)])
            sc.add("pe", lambda e, c=c, s2=s2, pbs=pbs: e.matmul(
                ps[pbs][:], lhsT=ones_bf[:], rhs=sq[s2][:], start=(c == 0), stop=(c == NC8 - 1)),
                reads=[("sq", s2), "ones"], writes=[("ps", pbs)])
        rstd_op(sg, pbs)
        for c in range(NC8):
            sc.add("dve", lambda e, c=c, sg=sg, cols=cols: e.scalar_tensor_tensor(
                out=dst(c), in0=xT[:, c, cols],
                scalar=g32[:, gpre.start + c:gpre.start + c + 1], in1=rstd[sg][:],
                op0=ALU.mult, op1=ALU.mult),
                reads=[xkey(c, t512), ("rstd", sg), ("g32", gpre.start)], writes=[dkey(c)])

    def postnorm_add(t512, sg, gpost, src, skey):
        cols = slice(t512 * 512, (t512 + 1) * 512)
        for m in range(NC8):
            tb = m % 2
            sc.add("dve", lambda e, m=m, sg=sg, tb=tb: e.scalar_tensor_tensor(
                out=cacc_or_tmp(tb), in0=src(m),
                scalar=g32[:, gpost.start + m:gpost.start + m + 1], in1=rstd[sg][:],
                op0=ALU.mult, op1=ALU.mult),
                reads=[skey(m), ("rstd", sg), ("g32", gpost.start)], writes=[("tmp", tb)])
            sc.add(ADD_ENG, lambda e, m=m, cols=cols, tb=tb: e.tensor_tensor(
                out=xT[:, m, cols], in0=xT[:, m, cols], in1=cacc_or_tmp(tb), op=ALU.add),
                reads=[("tmp", tb), xkey(m, t512)], writes=[xkey(m, t512)])

    def cacc_or_tmp(tb):
        return ptmp[tb][:]

    def ffn(l, f, pre_name, post_name):
        gpre = gsl("%s%d" % (pre_name, l))
        gpost = gsl("%s%d" % (post_name, l))
        wrow = (l * 2 + f)
        for tg in range(NTG):
            if tg == 0:
                for sg in range(NSG):
                    prenorm(tg * NSG + sg, sg, gpre, lambda c, sg=sg: hnT[:, c, sg * 512:(sg + 1) * 512],
                            lambda c, sg=sg: ("hnT", c, sg))
            import os
            KCUT = int(os.environ.get("KCUT", "9")) if f == 1 else 9
            if KCUT < 1:
                continue
            it = 0
            for j in range(NJ):
                ws = j % 2
                r0 = (wrow * NJ + j) * 128
                sc.add("pool", lambda e, ws=ws, r0=r0: e.dma_start(out=wgu[ws][:], in_=wgu_d[r0:r0 + 128, :]),
                       writes=[("wgu", ws)], dma=("wgu", ws))
                for sg in range(NSG):
                    b = it % 2
                    it += 1

                    def mm(e, ws=ws, sg=sg, b=b):
                        r = None
                        for g in range(2):
                            for k in range(NC8):
                                r = e.matmul(ps[2 * g + b][:], lhsT=wgu[ws][:, (g * 8 + k) * 128:(g * 8 + k + 1) * 128],
                                             rhs=hnT[:, k, sg * 512:(sg + 1) * 512], start=(k == 0), stop=(k == NC8 - 1))
                        return r
                    sc.add("pe", mm, reads=[("wgu", ws)] + [("hnT", k, sg) for k in range(NC8)],
                           writes=[("ps", b), ("ps", 2 + b)])
                    sc.add("act", lambda e, b=b: e.activation(out=sil[b][:], in_=ps[b][:], func=AF.Silu),
                           reads=[("ps", b)], writes=[("sil", b)])
                    sc.add("dve", lambda e, b=b, j=j, sg=sg: e.tensor_tensor(
                        out=aT[:, j, sg * 512:(sg + 1) * 512], in0=sil[b][:], in1=ps[2 + b][:], op=ALU.mult),
                        reads=[("sil", b), ("ps", 2 + b)], writes=[("aT", j, sg)])
            if KCUT < 2:
                continue
            it = 0
            for m in range(NC8):
                ws = m % 2
                r0 = (wrow * NC8 + m) * 128
                hw_ = NJ * 64
                sc.add("pool", lambda e, ws=ws, r0=r0, hw_=hw_: [
                    e.dma_start(out=wd[ws][:, h * hw_:(h + 1) * hw_], in_=wd_d[r0:r0 + 128, h * hw_:(h + 1) * hw_])
                    for h in range(2)],
                       writes=[("wd", ws)], dma=("wd", ws), ndma=2)
                for sg in range(NSG):
                    b = 4 + (it % 2)
                    it += 1

                    def mm2(e, ws=ws, sg=sg, b=b):
                        r = None
                        for j in range(NJ):
                            r = e.matmul(ps[b][:], lhsT=wd[ws][:, j * 128:(j + 1) * 128],
                                         rhs=aT[:, j, sg * 512:(sg + 1) * 512], start=(j == 0), stop=(j == NJ - 1))
                        return r
                    sc.add("pe", mm2, reads=[("wd", ws)] + [("aT", j, sg) for j in range(NJ)], writes=[("ps", b)])
                    s2 = it % 2
                    sc.add("dve", lambda e, b=b, m=m, sg=sg: e.tensor_copy(
                        out=fT[:, m, sg * 512:(sg + 1) * 512], in_=ps[b][:]),
                        reads=[("ps", b)], writes=[("fT", m, sg)])
                    sc.add("act", lambda e, m=m, sg=sg, s2=s2: e.activation(
                        out=sq[s2][:], in_=fT[:, m, sg * 512:(sg + 1) * 512], func=AF.Square),
                        reads=[("fT", m, sg)], writes=[("sq", s2)])
                    sc.add("pe", lambda e, s2=s2, sg=sg, m=m: e.matmul(
                        ps[6 + sg][:], lhsT=ones_bf[:], rhs=sq[s2][:], start=(m == 0), stop=(m == NC8 - 1)),
                        reads=[("sq", s2), "ones"], writes=[("ps", 6 + sg)])
                if m == 1 and tg + 1 < NTG:
                    for sg in range(NSG):
                        prenorm((tg + 1) * NSG + sg, sg, gpre, lambda c, sg=sg: hnT[:, c, sg * 512:(sg + 1) * 512],
                                lambda c, sg=sg: ("hnT", c, sg), pbs=sg)
            if KCUT < 3:
                continue
            for sg in range(NSG):
                rstd_op(sg, 6 + sg)
                postnorm_add(tg * NSG + sg, sg, gpost, lambda m, sg=sg: fT[:, m, sg * 512:(sg + 1) * 512],
                             lambda m, sg=sg: ("fT", m, sg))

    def mixer(l):
        gpre = gsl("mix_norm_pre%d" % l)
        gpost = gsl("mix_norm_post%d" % l)
        psb = [0]

        def nb2(pair):
            psb[0] += 1
            return pair * 2 + (psb[0] % 2)

        wcnt = [0]

        def load_w(src_ap):
            ws = wcnt[0] % 2
            wcnt[0] += 1
            sc.add("pool", lambda e, ws=ws, src_ap=src_ap: e.dma_start(out=wsl[ws][:], in_=src_ap),
                   writes=[("wsl", ws)], dma=("wsl", ws))
            return ws

        sc.add("pool", lambda e: e.dma_start(out=wdt[:], in_=wdt_d[l * 128:(l + 1) * 128, :]), writes=["wdt"], dma="wdt")
        sc.add("pool", lambda e: e.dma_start(out=wp[:], in_=wp_d[l * 128:(l + 1) * 128, :]), writes=["wp"], dma="wp")
        sc.add("dve", lambda e: e.memset(vp[:], 1.0), writes=["vp_init"] + [("vp", t_) for t_ in range(16)])
        sc.add("dve", lambda e: e.memset(carry[:], 0.0), writes=[("carry", c) for c in range(8)])
        sc.add("dve", lambda e: e.memset(state_f[:], 0.0), writes=["state_f"])
        sc.add("dve", lambda e: e.memset(state_b[:], 0.0), writes=["state_b"])
        sc.add("dve", lambda e: e.memset(u_tm[:, 4, :], 0.0), writes=[("u_tm", 4)])
        sc.add("act", lambda e: e.activation(out=Aneg[:], in_=small[:, gsl("a_log%d" % l)], func=AF.Exp),
               reads=["small"], writes=["Aneg0"])
        sc.add("dve", lambda e: e.tensor_scalar(out=Aneg[:], in0=Aneg[:], scalar1=-1.0, scalar2=0.0,
                                                op0=ALU.mult, op1=ALU.add), reads=["Aneg0"], writes=["Aneg"])

        for grp in range(4):
            t0 = grp * 4
            prenorm(grp, 0, gpre, lambda c: hnTm[:, c, :], lambda c: ("hnTm", c))
            if grp > 0:
                sc.add("pool", lambda e: e.tensor_copy(out=u_tm[:, 4, :], in_=u_tm[:, 3, :]),
                       reads=[("u_tm", 3)], writes=[("u_tm", 4)])
            for fp in range(4):
                ws = load_w(win_d[(l * WIN_PIECES + fp) * 128:(l * WIN_PIECES + fp + 1) * 128, :])
                for cc in range(2):
                    ch = fp * 2 + cc
                    pb = nb2(0)

                    def mmf(e, ws=ws, cc=cc, pb=pb):
                        r = None
                        for k in range(NC8):
                            r = e.matmul(ps[pb][:], lhsT=wsl[ws][:, (cc * 8 + k) * 128:(cc * 8 + k + 1) * 128],
                                         rhs=hnTm[:, k, :], start=(k == 0), stop=(k == NC8 - 1))
                        return r
                    sc.add("pe", mmf, reads=[("wsl", ws)] + [("hnTm", k) for k in range(NC8)], writes=[("ps", pb)])
                    pr = ch % 2
                    sc.add("dve", lambda e, pr=pr, ch=ch: e.tensor_copy(out=pre[pr][:, 0:3], in_=carry[:, ch, :]),
                           reads=[("carry", ch)], writes=[("pre", pr, 0)])
                    sc.add("act", lambda e, pr=pr, pb=pb: e.copy(out=pre[pr][:, 3:515], in_=ps[pb][:]),
                           reads=[("ps", pb)], writes=[("pre", pr, 1)])
                    sc.add("dve", lambda e, pr=pr, ch=ch: e.tensor_copy(out=carry[:, ch, :], in_=pre[pr][:, 512:515]),
                           reads=[("pre", pr, 1)], writes=[("carry", ch)])
                    sc.add("dve", lambda e, pr=pr, ch=ch: e.tensor_scalar(
                        out=cacc[pr][:], in0=pre[pr][:, 0:512], scalar1=scol("conv_w0_%d" % l, ch), scalar2=0.0,
                        op0=ALU.mult, op1=ALU.add),
                        reads=[("pre", pr, 0), ("pre", pr, 1), "small"], writes=[("cacc", pr)])
                    for j in range(1, 4):
                        sc.add("dve", lambda e, pr=pr, ch=ch, j=j: e.scalar_tensor_tensor(
                            out=cacc[pr][:], in0=pre[pr][:, j:j + 512], scalar=scol("conv_w%d_%d" % (j, l), ch),
                            in1=cacc[pr][:], op0=ALU.mult, op1=ALU.add),
                            reads=[("pre", pr, 0), ("pre", pr, 1), ("cacc", pr), "small"], writes=[("cacc", pr)])
                    sc.add("act", lambda e, pr=pr, ch=ch: e.activation(
                        out=xbcf[:, ch, :], in_=cacc[pr][:], func=AF.Silu, bias=scol("conv_b%d" % l, ch), scale=1.0),
                        reads=[("cacc", pr), "small"], writes=[("xbcf", ch)])
                    if ch >= 4:
                        sc.add("pool", lambda e, ch=ch: e.tensor_copy(out=bcT[:, ch - 4, :], in_=xbcf[:, ch, :]),
                               reads=[("xbcf", ch)], writes=[("bcT", ch - 4)])
            for ti in range(4):
                for half in range(2):
                    pb = nb2(1)
                    chs = [0, 1, 2, 3] if half == 0 else [4, 5]

                    def trx(e, ti=ti, chs=chs, pb=pb):
                        r = None
                        for i, ch in enumerate(chs):
                            r = e.transpose(out=ps[pb][:, i * 128:(i + 1) * 128],
                                            in_=xbcf[:, ch, ti * 128:(ti + 1) * 128], identity=ident[:])
                        return r
                    sc.add("pe", trx, reads=[("xbcf", ch) for ch in chs] + ["ident"], writes=[("ps", pb)])
                    if half == 0:
                        sc.add("act", lambda e, ti=ti, pb=pb: e.copy(out=x_tm[:, ti, :], in_=ps[pb][:]),
                               reads=[("ps", pb)], writes=[("x_tm", ti)])
                    else:
                        sc.add("act", lambda e, ti=ti, pb=pb: e.copy(out=B_tm[:, ti, :], in_=ps[pb][:, 0:256]),
                               reads=[("ps", pb)], writes=[("B_tm", ti)])
            for tp in range(6):
                ws = load_w(win_d[(l * WIN_PIECES + 4 + tp) * 128:(l * WIN_PIECES + 5 + tp) * 128, :])
                for ti in range(4):
                    pb = nb2(0)

                    def mmt(e, ws=ws, ti=ti, pb=pb):
                        r = None
                        for k in range(NC8):
                            r = e.matmul(ps[pb][:, 0:256], lhsT=hnTm[:, k, ti * 128:(ti + 1) * 128],
                                         rhs=wsl[ws][:, k * 256:(k + 1) * 256], start=(k == 0), stop=(k == NC8 - 1))
                        return r
                    sc.add("pe", mmt, reads=[("wsl", ws)] + [("hnTm", k) for k in range(NC8)], writes=[("ps", pb)])
                    if tp < 2:
                        sc.add("act", lambda e, ti=ti, tp=tp, pb=pb: e.activation(
                            out=zs[:, ti, tp * 256:(tp + 1) * 256], in_=ps[pb][:, 0:256], func=AF.Silu),
                            reads=[("ps", pb)], writes=[("zs", ti, tp)])
                    elif tp < 4:
                        isk = tp - 2
                        sc.add("act", lambda e, pb=pb: e.copy(out=qkf[:, 0:256], in_=ps[pb][:, 0:256]),
                               reads=[("ps", pb)], writes=["qkf"])
                        tile = t0 + ti
                        qv = qkf[:, 0:256].rearrange("p (h d) -> p h d", h=4)
                        cosb = rope[:, tile * 16:tile * 16 + 8].unsqueeze(1).broadcast_to([128, 4, 8])
                        sinb = rope[:, tile * 16 + 8:tile * 16 + 16].unsqueeze(1).broadcast_to([128, 4, 8])
                        x1 = qv[:, :, 0:8]
                        x2 = qv[:, :, 8:16]
                        r_ = [ropet[i][:, 0:4, :] for i in range(4)]
                        sc.add("dve", lambda e, x1=x1, cosb=cosb, r_=r_: e.tensor_tensor(out=r_[0], in0=x1, in1=cosb, op=ALU.mult),
                               reads=["qkf", "rope"], writes=[("ropet", 0)])
                        sc.add("dve", lambda e, x2=x2, sinb=sinb, r_=r_: e.tensor_tensor(out=r_[1], in0=x2, in1=sinb, op=ALU.mult),
                               reads=["qkf", "rope"], writes=[("ropet", 1)])
                        sc.add("dve", lambda e, x2=x2, cosb=cosb, r_=r_: e.tensor_tensor(out=r_[2], in0=x2, in1=cosb, op=ALU.mult),
                               reads=["qkf", "rope"], writes=[("ropet", 2)])
                        sc.add("dve", lambda e, x1=x1, sinb=sinb, r_=r_: e.tensor_tensor(out=r_[3], in0=x1, in1=sinb, op=ALU.mult),
                               reads=["qkf", "rope"], writes=[("ropet", 3)])
                        sc.add("dve", lambda e, x1=x1, r_=r_: e.tensor_tensor(out=x1, in0=r_[0], in1=r_[1], op=ALU.subtract),
                               reads=[("ropet", 0), ("ropet", 1)], writes=["qkf"])
                        sc.add("dve", lambda e, x2=x2, r_=r_: e.tensor_tensor(out=x2, in0=r_[2], in1=r_[3], op=ALU.add),
                               reads=[("ropet", 2), ("ropet", 3), "qkf"], writes=["qkf"])
                        pb2 = nb2(1)

                        def trq(e, pb2=pb2):
                            r = None
                            for hp in range(2):
                                r = e.transpose(out=ps[pb2][:, hp * 128:(hp + 1) * 128],
                                                in_=qkf[:, hp * 128:(hp + 1) * 128], identity=ident[:])
                            return r
                        sc.add("pe", trq, reads=["qkf", "ident"], writes=[("ps", pb2)])
                        if isk:
                            sc.add("act", lambda e, pb2=pb2, tile=tile: e.copy(
                                out=KT[:, :, tile * 128:(tile + 1) * 128],
                                in_=ps[pb2][:, 0:256].rearrange("p (a t) -> p a t", a=2)),
                                reads=[("ps", pb2)], writes=[("KT", tile)])
                        else:
                            sc.add("act", lambda e, pb2=pb2, ti=ti: e.copy(
                                out=QT[:, :, ti * 128:(ti + 1) * 128],
                                in_=ps[pb2][:, 0:256].rearrange("p (a t) -> p a t", a=2)),
                                reads=[("ps", pb2)], writes=[("QT", ti)])
                    elif tp == 4:
                        tile = t0 + ti
                        sc.add("act", lambda e, pb=pb, tile=tile: e.copy(
                            out=vp[:, tile, :, 0:64], in_=ps[pb][:, 0:256].rearrange("p (h d) -> p h d", h=4)),
                            reads=[("ps", pb), "vp_init"], writes=[("vp", tile)])
                    else:
                        sc.add("act", lambda e, pb=pb, ti=ti: e.copy(out=u_tm[:, ti, :], in_=ps[pb][:, 0:256]),
                               reads=[("ps", pb)], writes=[("u_tm", ti)])
            for ti in range(4):
                pb = nb2(0)

                def mmd(e, ti=ti, pb=pb):
                    r = None
                    for k in range(NC8):
                        r = e.matmul(ps[pb][:, 0:8], lhsT=hnTm[:, k, ti * 128:(ti + 1) * 128],
                                     rhs=wdt[:, k * 8:(k + 1) * 8], start=(k == 0), stop=(k == NC8 - 1))
                    return r
                sc.add("pe", mmd, reads=["wdt"] + [("hnTm", k) for k in range(NC8)], writes=[("ps", pb)])
                sc.add("dve", lambda e, ti=ti, pb=pb: e.tensor_tensor(
                    out=dtt[:, ti, :], in0=ps[pb][:, 0:8], in1=small[:, gsl("dt_bias%d" % l)], op=ALU.add),
                    reads=[("ps", pb), "small"], writes=[("dtt", ti)])
            sc.add("act", lambda e: e.activation(out=dtt[:], in_=dtt[:], func=AF.Exp),
                   reads=[("dtt", ti) for ti in range(4)], writes=["dtt_e"])
            sc.add("act", lambda e: e.activation(out=dtt[:], in_=dtt[:], func=AF.Ln, bias=1.0, scale=1.0),
                   reads=["dtt_e"], writes=["dtt_f"])
            sc.add("dve", lambda e: e.tensor_tensor(
                out=dAt[:], in0=dtt[:], in1=Aneg[:].unsqueeze(1).broadcast_to([128, 4, 8]), op=ALU.mult),
                reads=["dtt_f", "Aneg"], writes=["dAt"])

            for bi in range(2):
                blk = grp * 2 + bi
                sc.add("dve", lambda e, blk=blk: e.tensor_reduce(
                    out=kmf[:, :, blk], in_=KT[:, :, blk * 256:(blk + 1) * 256], axis=AX.X, op=ALU.add),
                    reads=[("KT", 2 * blk), ("KT", 2 * blk + 1)], writes=[("kmf", blk)])
                sc.add("dve", lambda e, blk=blk: e.tensor_scalar(
                    out=kmT[:, :, blk:blk + 1], in0=kmf[:, :, blk:blk + 1], scalar1=1.0 / 256, scalar2=0.0,
                    op0=ALU.mult, op1=ALU.add),
                    reads=[("kmf", blk)], writes=[("kmT", blk)])

            for ci in range(4):
                sc.add("dve", lambda e, ci=ci: e.tensor_tensor(
                    out=Dall[:].rearrange("p (h l) -> p h l", h=8), in0=dAt[:, ci, :].unsqueeze(2).broadcast_to([128, 8, 128]),
                    in1=tri[:].unsqueeze(1).broadcast_to([128, 8, 128]), op=ALU.mult),
                    reads=["dAt", "tri"], writes=["Dall"])

                def mm_acs(e, ci=ci):
                    e.matmul(ps[2][:, 0:8], lhsT=tri[:], rhs=dAt[:, ci, :], start=True, stop=True)
                    return e.matmul(ps[2][:, 8:16], lhsT=ones_f[:], rhs=dAt[:, ci, :], start=True, stop=True)
                sc.add("pe", mm_acs, reads=["dAt", "tri", "ones_f"], writes=[("ps", 2)])

                def mm_row(e):
                    r = None
                    for h in range(8):
                        reg = ps[4 + h // 4][:, (h % 4) * 128:(h % 4 + 1) * 128]
                        e.matmul(reg, lhsT=ones_f[:], rhs=Dall[:, h * 128:(h + 1) * 128], start=True, stop=False)
                        r = e.matmul(reg, lhsT=ident[:], rhs=negmask[:], start=False, stop=True)
                    return r
                sc.add("pe", mm_row, reads=["Dall", "ones_f", "ident", "negmask"], writes=[("ps", 4), ("ps", 5)])
                pbc = 3

                def mm_cb(e, ci=ci):
                    r = None
                    for g in range(2):
                        r = e.matmul(ps[pbc][:, g * 128:(g + 1) * 128], lhsT=bcT[:, g, ci * 128:(ci + 1) * 128],
                                     rhs=bcT[:, 2 + g, ci * 128:(ci + 1) * 128], start=True, stop=True)
                    return r
                sc.add("pe", mm_cb, reads=[("bcT", i) for i in range(4)], writes=[("ps", 3)])
                sc.add("dve", lambda e: e.tensor_copy(out=Ein[:, 0:16], in_=ps[2][:, 0:16]),
                       reads=[("ps", 2)], writes=["Ein0"])
                sc.add("dve", lambda e: e.tensor_scalar(out=negacs[:], in0=Ein[:, 0:8], scalar1=-1.0, scalar2=0.0,
                                                        op0=ALU.mult, op1=ALU.add), reads=["Ein0"], writes=["negacs"])
                sc.add("dve", lambda e: e.tensor_tensor(out=Ein[:, 16:24], in0=Ein[:, 8:16], in1=Ein[:, 0:8], op=ALU.subtract),
                       reads=["Ein0"], writes=["Ein1"])
                sc.add("act", lambda e: e.activation(out=Eout[:], in_=Ein[:], func=AF.Exp),
                       reads=["Ein0", "Ein1"], writes=["Eout"])
                for h in range(8):
                    sc.add("act", lambda e, h=h: e.activation(
                        out=seg[:, h, :], in_=ps[4 + h // 4][:, (h % 4) * 128:(h % 4 + 1) * 128], func=AF.Exp,
                        bias=negacs[:, h:h + 1], scale=1.0),
                        reads=[("ps", 4 + h // 4), "negacs"], writes=[("seg", h)])
                for h in range(8):
                    sc.add("dve", lambda e, h=h: e.tensor_tensor(
                        out=MT[:, h, :], in0=seg[:, h, :], in1=ps[pbc][:, (h // 4) * 128:(h // 4 + 1) * 128], op=ALU.mult),
                        reads=[("seg", h), ("ps", 3)], writes=[("MT", h)])
                xv = x_tm[:, ci, :].rearrange("p (h d) -> p h d", h=8)
                sc.add("dve", lambda e, ci=ci, xv=xv: e.tensor_tensor(
                    out=xdt[:].rearrange("p (h d) -> p h d", h=8), in0=xv,
                    in1=dtt[:, ci, :].unsqueeze(2).broadcast_to([128, 8, 64]), op=ALU.mult),
                    reads=[("x_tm", ci), "dtt_f"], writes=["xdt"])
                sc.add("dve", lambda e: e.tensor_tensor(
                    out=xdtd[:].rearrange("p (h d) -> p h d", h=8), in0=xdt[:].rearrange("p (h d) -> p h d", h=8),
                    in1=Eout[:, 16:24].unsqueeze(2).broadcast_to([128, 8, 64]), op=ALU.mult),
                    reads=["xdt", "Eout"], writes=["xdtd"])

                def mm_y(e):
                    r = None
                    for h in range(8):
                        r = e.matmul(ps[6][:, h * 64:(h + 1) * 64], lhsT=MT[:, h, :], rhs=xdt[:, h * 64:(h + 1) * 64],
                                     start=True, stop=True)
                    return r
                sc.add("pe", mm_y, reads=[("MT", h) for h in range(8)] + ["xdt"], writes=[("ps", 6)])

                def mm_off(e, ci=ci):
                    r = None
                    for g in range(2):
                        r = e.matmul(ps[7][:, g * 256:(g + 1) * 256], lhsT=bcT[:, 2 + g, ci * 128:(ci + 1) * 128],
                                     rhs=state_b[:, g * 256:(g + 1) * 256], start=True, stop=True)
                    return r
                sc.add("pe", mm_off, reads=[("bcT", 2), ("bcT", 3), "state_b"], writes=[("ps", 7)])
                sc.add("dve", lambda e: e.tensor_tensor(
                    out=y_sb[:].rearrange("p (h d) -> p h d", h=8), in0=ps[7][:].rearrange("p (h d) -> p h d", h=8),
                    in1=Eout[:, 0:8].unsqueeze(2).broadcast_to([128, 8, 64]), op=ALU.mult),
                    reads=[("ps", 7), "Eout"], writes=["y_sb"])
                sc.add("dve", lambda e: e.tensor_tensor(out=y_sb[:], in0=y_sb[:], in1=ps[6][:], op=ALU.add),
                       reads=[("ps", 6), "y_sb"], writes=["y_sb"])
                sc.add("dve", lambda e, xv=xv: e.tensor_tensor(
                    out=y2[:].rearrange("p (h d) -> p h d", h=8), in0=xv,
                    in1=small[:, gsl("d_skip%d" % l)].unsqueeze(2).broadcast_to([128, 8, 64]), op=ALU.mult),
                    reads=[("x_tm", ci), "small"], writes=["y2"])
                sc.add("dve", lambda e: e.tensor_tensor(out=y_sb[:], in0=y_sb[:], in1=y2[:], op=ALU.add),
                       reads=["y_sb", "y2"], writes=["y_sb"])
                sc.add("dve", lambda e, ci=ci: e.tensor_tensor(out=y_sb[:], in0=y_sb[:], in1=zs[:, ci, :], op=ALU.mult),
                       reads=["y_sb", ("zs", ci, 0), ("zs", ci, 1)], writes=["y_sb"])
                sc.add("dve", lambda e: e.tensor_tensor(out=y2[:], in0=y_sb[:], in1=y_sb[:], op=ALU.mult),
                       reads=["y_sb", "y2"], writes=["y2"])
                sc.add("dve", lambda e: e.tensor_reduce(out=ssq[:, 0:1], in_=y2[:], axis=AX.X, op=ALU.add),
                       reads=["y2"], writes=["ssq0"])
                sc.add("dve", lambda e: e.tensor_scalar(out=ssq[:, 1:2], in0=ssq[:, 0:1], scalar1=1.0 / 512, scalar2=EPS,
                                                        op0=ALU.mult, op1=ALU.add), reads=["ssq0"], writes=["ssq1"])
                sc.add("pool", lambda e: e.tensor_tensor(out=ssq[:, 2:3], in0=ssq[:, 1:2], in1=neghalf[:, 0:1], op=ALU.pow),
                       reads=["ssq1", "neghalf"], writes=["ssq2"])
                sc.add("dve", lambda e: e.tensor_scalar(out=y_sb[:], in0=y_sb[:], scalar1=ssq[:, 2:3], scalar2=0.0,
                                                        op0=ALU.mult, op1=ALU.add), reads=["y_sb", "ssq2"], writes=["y_sb"])
                pb = nb2(0)

                def tr_y(e, pb=pb):
                    r = None
                    for i in range(4):
                        r = e.transpose(out=ps[pb][:, i * 128:(i + 1) * 128], in_=y_sb[:, i * 128:(i + 1) * 128],
                                        identity=ident[:])
                    return r
                sc.add("pe", tr_y, reads=["y_sb", "ident"], writes=[("ps", pb)])
                for i in range(4):
                    sc.add("act", lambda e, i=i, ci=ci, pb=pb: e.activation(
                        out=ycatT[:, i, ci * 128:(ci + 1) * 128], in_=ps[pb][:, i * 128:(i + 1) * 128], func=AF.Copy,
                        scale=scol("ssd_norm%d" % l, i)),
                        reads=[("ps", pb), "small"], writes=[("ycatT", i, ci)])
                pbs = 2

                def mm_st(e, ci=ci):
                    r = None
                    for g in range(2):
                        r = e.matmul(ps[1][:, g * 256:(g + 1) * 256], lhsT=B_tm[:, ci, g * 128:(g + 1) * 128],
                                     rhs=xdtd[:, g * 256:(g + 1) * 256], start=True, stop=True)
                    return r
                sc.add("pe", mm_st, reads=[("B_tm", ci), "xdtd"], writes=[("ps", 1)])
                sc.add("dve", lambda e: e.tensor_tensor(
                    out=state_f[:].rearrange("p (h d) -> p h d", h=8), in0=state_f[:].rearrange("p (h d) -> p h d", h=8),
                    in1=Eout[:, 8:16].unsqueeze(2).broadcast_to([128, 8, 64]), op=ALU.mult),
                    reads=["state_f", "Eout", "state_b"], writes=["state_f"])
                sc.add("dve", lambda e: e.tensor_tensor(out=state_f[:], in0=state_f[:], in1=ps[1][:], op=ALU.add),
                       reads=["state_f", ("ps", 1)], writes=["state_f"])
                sc.add("pool", lambda e: e.tensor_copy(out=state_b[:], in_=state_f[:]),
                       reads=["state_f"], writes=["state_b"])

            for ti in range(4):
                tile = t0 + ti
                pb = nb2(1)

                def mm_pool(e, ti=ti, tile=tile, pb=pb):
                    r = None
                    for g in range(4):
                        bc = (3 * g + (1 if tile == 0 else 0)) * 128
                        bp = (3 * g + 2) * 128
                        first = tile == 0
                        r = e.matmul(ps[pb][:, g * 64:(g + 1) * 64], lhsT=bands[:, bc:bc + 128],
                                     rhs=u_tm[:, ti, g * 64:(g + 1) * 64], start=True, stop=first)
                        if not first:
                            prev = ti - 1 if ti > 0 else 4
                            r = e.matmul(ps[pb][:, g * 64:(g + 1) * 64], lhsT=bands[:, bp:bp + 128],
                                         rhs=u_tm[:, prev, g * 64:(g + 1) * 64], start=False, stop=True)
                    return r
                sc.add("pe", mm_pool, reads=["bands", ("u_tm", ti), ("u_tm", ti - 1 if ti > 0 else 4)], writes=[("ps", pb)])
                sc.add("act", lambda e, pb=pb: e.copy(out=d_tm[:], in_=ps[pb][:, 0:256]),
                       reads=[("ps", pb)], writes=["d_tm"])
                pb2 = nb2(1)

                def tr_d(e, pb2=pb2):
                    r = None
                    for i in range(2):
                        r = e.transpose(out=ps[pb2][:, i * 128:(i + 1) * 128], in_=d_tm[:, i * 128:(i + 1) * 128],
                                        identity=ident[:])
                    return r
                sc.add("pe", tr_d, reads=["d_tm", "ident"], writes=[("ps", pb2)])
                sc.add("act", lambda e, pb2=pb2, ti=ti: e.copy(
                    out=dT[:, :, ti * 128:(ti + 1) * 128], in_=ps[pb2][:, 0:256].rearrange("p (a t) -> p a t", a=2)),
                    reads=[("ps", pb2)], writes=[("dT", ti)])
            for pair in range(2):
                pb = nb2(0)
                sc.add("pe", lambda e, pair=pair, pb=pb: e.matmul(
                    ps[pb][:], lhsT=wp[:, pair * 128:(pair + 1) * 128], rhs=dT[:, pair, :], start=True, stop=True),
                    reads=["wp"] + [("dT", ti) for ti in range(4)], writes=[("ps", pb)])
                sc.add("act", lambda e, pair=pair, pb=pb: e.activation(
                    out=ycatT[:, 4 + pair, :], in_=ps[pb][:], func=AF.Copy, scale=scol("pool_scale%d" % l, pair)),
                    reads=[("ps", pb), "small"], writes=[("ycatT", 4 + pair, ci) for ci in range(4)])

            need_sel = (grp * 2) >= 4
            if need_sel:
                for ti in range(4):
                    qb = (t0 + ti) // 2
                    pb = nb2(1)

                    def mm_gate(e, ti=ti, qb=qb, pb=pb):
                        r = None
                        for h in range(4):
                            rows = slice((h % 2) * 64, (h % 2) * 64 + 64)
                            r = e.matmul(ps[pb][:, h * 8:h * 8 + qb], lhsT=QT[rows, h // 2, ti * 128:(ti + 1) * 128],
                                         rhs=kmT[rows, h // 2, 0:qb], start=True, stop=True)
                        return r
                    sc.add("pe", mm_gate, reads=[("QT", ti)] + [("kmT", b) for b in range(qb)], writes=[("ps", pb)])
                    sc.add("dve", lambda e, pb=pb, qb=qb: e.tensor_copy(
                        out=gate[:, :, 0:qb], in_=ps[pb][:, 0:32].rearrange("p (h n) -> p h n", h=4)[:, :, 0:qb]),
                        reads=[("ps", pb)], writes=["gate"])
                    sc.add("dve", lambda e, qb=qb: e.tensor_tensor(
                        out=cmpb[:, :, 0:qb, 0:qb],
                        in0=gate[:, :, 0:qb].unsqueeze(2).broadcast_to([128, 4, qb, qb]),
                        in1=gate[:, :, 0:qb].unsqueeze(3).broadcast_to([128, 4, qb, qb]), op=ALU.is_gt),
                        reads=["gate"], writes=["cmpb"])
                    sc.add("dve", lambda e, qb=qb: e.tensor_reduce(
                        out=cnt[:, :, 0:qb], in_=cmpb[:, :, 0:qb, 0:qb], axis=AX.X, op=ALU.add),
                        reads=["cmpb"], writes=["cnt"])
                    sc.add("dve", lambda e, ti=ti: e.memset(bias_tm[:, ti, :, :], 0.0), writes=[("bias_tm", ti)])
                    sc.add("dve", lambda e, ti=ti, qb=qb: e.tensor_scalar(
                        out=bias_tm[:, ti, :, 0:qb], in0=cnt[:, :, 0:qb], scalar1=2.5, scalar2=-30000.0,
                        op0=ALU.is_gt, op1=ALU.mult),
                        reads=["cnt", ("bias_tm", ti)], writes=[("bias_tm", ti)])
            for h in range(4):
                hp = h // 2
                rows = slice((h % 2) * 64, (h % 2) * 64 + 64)
                bs = h % 2
                if need_sel:
                    pb = nb2(1)

                    def tr_b(e, h=h, pb=pb):
                        r = None
                        for ti in range(4):
                            r = e.transpose(out=ps[pb][0:8, ti * 128:(ti + 1) * 128], in_=bias_tm[:, ti, h, :],
                                            identity=ident[:])
                        return r
                    sc.add("pe", tr_b, reads=[("bias_tm", ti) for ti in range(4)] + ["ident"], writes=[("ps", pb)])
                    sc.add("act", lambda e, pb=pb, bs=bs: e.copy(out=biasT[bs][:], in_=ps[pb][0:8, :]),
                           reads=[("ps", pb)], writes=[("biasT", bs)])
                acc = 6 + (h % 2)
                for kt in range(t0 + 4):
                    kb = kt // 2
                    if kt < t0:
                        c0 = 0
                    else:
                        c0 = (kt - t0) * 128
                    pb = 4 + (kt % 2)
                    past_cols = None
                    if need_sel:
                        if kb < grp * 2:
                            past_cols = (0, 512)
                        elif kb == grp * 2:
                            past_cols = (256, 512)

                    if kt < t0:
                        segs = [(0, 512, need_sel, False)]
                    else:
                        d0 = (kt - t0) * 128
                        segs = [(d0, d0 + 128, False, True)]
                        blk_end = 256 if (kt - t0) < 2 else 512
                        if d0 + 128 < blk_end:
                            segs.append((d0 + 128, blk_end, False, False))
                        if blk_end < 512:
                            if need_sel:
                                segs.append((blk_end, 512, True, False))
                            elif len(segs) > 1:
                                segs[-1] = (segs[-1][0], 512, False, False)
                            else:
                                segs.append((blk_end, 512, False, False))

                    def mm_s(e, kt=kt, pb=pb, segs=segs, rows=rows, hp=hp, kb=kb, bs=bs):
                        r = None
                        for (a0, a1, hb, hc) in segs:
                            reg = ps[pb][:, a0:a1]
                            r = e.matmul(reg, lhsT=KT[rows, hp, kt * 128:(kt + 1) * 128],
                                         rhs=QT[rows, hp, a0:a1], start=True, stop=not (hb or hc))
                            if hb:
                                r = e.matmul(reg, lhsT=esel[:, kb * 128:(kb + 1) * 128],
                                             rhs=biasT[bs][:, a0:a1], start=False, stop=True)
                            if hc:
                                r = e.matmul(reg, lhsT=ident_bf[:], rhs=cmask_bf[:], start=False, stop=True)
                        return r
                    rd = [("KT", kt)] + [("QT", ti) for ti in range(4)] + ["esel", ("biasT", bs), "ident_bf", "cmask"]
                    sc.add("pe", mm_s, reads=rd, writes=[("ps", pb)])
                    pt = kt % 2
                    sc.add("act", lambda e, pb=pb, c0=c0, pt=pt: e.activation(
                        out=PT[pt][:, c0:512], in_=ps[pb][:, c0:512], func=AF.Exp, scale=0.125),
                        reads=[("ps", pb)], writes=[("PT", pt)])

                    def mm_pv(e, kt=kt, c0=c0, pt=pt, h=h, t0=t0):
                        r = None
                        for ti in range(c0 // 128, 4):
                            tq = t0 + ti
                            r = e.matmul(ps[ti][:, 0:65], lhsT=PT[pt][:, ti * 128:(ti + 1) * 128],
                                         rhs=vp[:, kt, h, :], start=(kt == 0), stop=(kt == tq))
                        return r
                    sc.add("pe", mm_pv, reads=[("PT", pt), ("vp", kt)], writes=[("ps", ti) for ti in range(c0 // 128, 4)])
                for ti in range(4):
                    sc.add("dve", lambda e, ti=ti: e.reciprocal(out=rden[:, ti:ti + 1], in_=ps[ti][:, 64:65]),
                           reads=[("ps", ti)], writes=[("rden", ti)], name="recip g%d h%d ti%d" % (grp, h, ti))
                    sc.add("dve", lambda e, ti=ti, h=h: e.tensor_scalar(
                        out=att_tm[:, ti, h * 64:(h + 1) * 64], in0=ps[ti][:, 0:64], scalar1=rden[:, ti:ti + 1], scalar2=0.0,
                        op0=ALU.mult, op1=ALU.add),
                        reads=[("ps", ti), ("rden", ti)], writes=[("att_tm", h, ti)])
            for ti in range(4):
                pb = nb2(1)

                def tr_a(e, ti=ti, pb=pb):
                    r = None
                    for i in range(2):
                        r = e.transpose(out=ps[pb][:, i * 128:(i + 1) * 128], in_=att_tm[:, ti, i * 128:(i + 1) * 128],
                                        identity=ident[:])
                    return r
                sc.add("pe", tr_a, reads=[("att_tm", h, ti) for h in range(4)] + ["ident"], writes=[("ps", pb)])
                sc.add("act", lambda e, pb=pb, ti=ti: e.copy(
                    out=ycatT[:, 6:8, ti * 128:(ti + 1) * 128], in_=ps[pb][:, 0:256].rearrange("p (a t) -> p a t", a=2)),
                    reads=[("ps", pb)], writes=[("ycatT", 6, ti), ("ycatT", 7, ti)])

            ycr = [("ycatT", c, ci) for c in range(8) for ci in range(4)]
            for half in range(4):
                ws = load_w(wout_d[(l * 4 + half) * 128:(l * 4 + half + 1) * 128, :])
                for mm_ in range(2):
                    m = half * 2 + mm_
                    pb = nb2(0)

                    def mmo(e, ws=ws, mm_=mm_, pb=pb):
                        r = None
                        for k in range(NC8):
                            r = e.matmul(ps[pb][:], lhsT=wsl[ws][:, (mm_ * 8 + k) * 128:(mm_ * 8 + k + 1) * 128],
                                         rhs=ycatT[:, k, :], start=(k == 0), stop=(k == NC8 - 1))
                        return r
                    sc.add("pe", mmo, reads=[("wsl", ws)] + ycr, writes=[("ps", pb)])
                    s2 = m % 2
                    sc.add("dve", lambda e, pb=pb, m=m: e.tensor_copy(out=mT[:, m, :], in_=ps[pb][:]),
                           reads=[("ps", pb)], writes=[("xbcf", m)])
                    sc.add("act", lambda e, m=m, s2=s2: e.activation(out=sq[s2][:], in_=mT[:, m, :], func=AF.Square),
                           reads=[("xbcf", m)], writes=[("sq", s2)])
                    sc.add("pe", lambda e, s2=s2, m=m: e.matmul(
                        ps[7][:], lhsT=ones_bf[:], rhs=sq[s2][:], start=(m == 0), stop=(m == NC8 - 1)),
                        reads=[("sq", s2), "ones"], writes=[("ps", 7)])
            rstd_op(1, 7)
            postnorm_add(grp, 1, gpost, lambda m: mT[:, m, :], lambda m: ("xbcf", m))

    if stage == "all":
        plan = [(l, ["f1", "m", "f2"]) for l in range(DEPTH)]
    elif stage == "none":
        plan = []
    else:
        lay, parts = stage.split(":")
        plan = [(int(lay), parts.split("+"))]
    for l, stages in plan:
        if "f1" in stages:
            ffn(l, 0, "ff1_norm_pre", "ff1_norm_post")
        if "m" in stages:
            barrier()
            mixer(l)
            barrier()
        if "f2" in stages:
            ffn(l, 1, "ff2_norm_pre", "ff2_norm_post")

    barrier()
    for t in range(S // 128):
        slot = t % 2
        for q in range(2):
            pb = 6 + q

            def tr2(e, q=q, t=t, pb=pb):
                r = None
                for i in range(4):
                    c = 4 * q + i
                    r = e.transpose(out=ps[pb][:, i * 128:(i + 1) * 128], in_=xT[:, c, t * 128:(t + 1) * 128],
                                    identity=ident[:])
                return r
            sc.add("pe", tr2, reads=[xkey(4 * q + i, t // 4) for i in range(4)] + ["ident"], writes=[("ps", pb)])
            sc.add("act", lambda e, q=q, slot=slot, pb=pb: e.copy(
                out=xio[slot][:, q * 512:(q + 1) * 512], in_=ps[pb][:]),
                reads=[("ps", pb)], writes=[("xio", slot, q)])
        sc.add("sp", lambda e, t=t, slot=slot: e.dma_start(out=out_d[t * 128:(t + 1) * 128, :], in_=xio[slot]),
               reads=[("xio", slot, 0), ("xio", slot, 1)], writes=[("out", t)], dma=("out", slot))
    sc.add("sp", lambda e: e.nop(), reads=[("out", t) for t in range(S // 128)], writes=[])
    sc.emit()
    return nc


def prep_weights(inp):
    wgu = np.empty((DEPTH, 2, NJ, 128, 2, 8, 128), np.float32)
    wd = np.empty((DEPTH, 2, NC8, 128, NJ, 128), np.float32)
    win = np.empty((DEPTH, WIN_PIECES, 128, 2048), np.float32)
    wdt = np.empty((DEPTH, 128, 64), np.float32)
    wout = np.empty((DEPTH, 4, 128, 2, 8, 128), np.float32)
    wp = np.zeros((DEPTH, 128, 2, 128), np.float32)
    for l in range(DEPTH):
        for f, pfx in enumerate(("ff1", "ff2")):
            gate_up = {"ff1": (inp["ff1_w_gate"], inp["ff1_w_up"]), "ff2": (inp["ff2_w_gate"], inp["ff2_w_up"])}[pfx]
            for g in range(2):
                w = np.asarray(gate_up[g][l])
                wgu[l, f, :, :, g] = w.reshape(8, 128, NJ, 128).transpose(2, 1, 0, 3)
            w = np.asarray({"ff1": inp["ff1_w_down"], "ff2": inp["ff2_w_down"]}[pfx][l])
            wd[l, f] = w.reshape(NJ, 128, NC8, 128).transpose(2, 1, 0, 3)
        w = np.asarray(inp["w_in"][l])
        wk = w.reshape(8, 128, 2568)
        for fp in range(4):
            blk = wk[:, :, 512 + fp * 256:512 + (fp + 1) * 256].reshape(8, 128, 2, 128)
            win[l, fp] = blk.transpose(1, 2, 0, 3).reshape(128, 2048)
        wt = wk[:, :, TCOLS]
        for tp in range(6):
            blk = wt[:, :, tp * 256:(tp + 1) * 256]
            win[l, 4 + tp] = blk.transpose(1, 0, 2).reshape(128, 2048)
        wdt[l] = wk[:, :, 1536:1544].transpose(1, 0, 2).reshape(128, 64)
        wo = np.asarray(inp["w_out"][l]).reshape(8, 128, 4, 2, 128)
        wout[l] = wo.transpose(2, 1, 3, 0, 4)
        pw = np.asarray(inp["pool_w"][l])
        for g in range(4):
            pair, i = g // 2, g % 2
            wp[l, i * 64:(i + 1) * 64, pair, i * 64:(i + 1) * 64] = pw[g]
    return dict(
        wgu=wgu.reshape(DEPTH * 2 * NJ * 128, 2 * 8 * 128), wd=wd.reshape(DEPTH * 2 * NC8 * 128, NJ * 128),
        win=win.reshape(DEPTH * WIN_PIECES * 128, 2048), wdt=wdt.reshape(DEPTH * 128, 64),
        wout=wout.reshape(DEPTH * 4 * 128, 2048), wp=wp.reshape(DEPTH * 128, 256))


LAUNCH_PLAN = ["all"]


def kernel(stage=None, **inp):
    inp = {k: np.asarray(v) for k, v in inp.items()}
    small = pack_small(inp)
    assert small.shape[1] == NSMALL
    shared = prep_weights(inp)
    shared.update(make_consts())
    shared["small"] = small
    shared["ident"] = np.eye(128, dtype=np.float32)
    x = inp["x"]
    plan = LAUNCH_PLAN if stage is None else [stage]
    progs = {}
    for st in plan:
        if st not in progs:
            progs[st] = build(st)
        nc = progs[st]
        in_maps = [dict(shared, x=np.ascontiguousarray(x[b])) for b in range(8)]
        res = run_bass_kernel_spmd(nc, in_maps, core_ids=list(range(8)))
        x = np.stack([r["out"] for r in res.results], axis=0)
    return x
```

```python
import numpy as np
import concourse.bass as bass
import concourse.mybir as mybir
from concourse.bass_utils import run_bass_kernel_spmd

F32 = mybir.dt.float32
BF16 = mybir.dt.bfloat16
AF = mybir.ActivationFunctionType
ALU = mybir.AluOpType
AX = mybir.AxisListType

S = 2048
D = 1024
DFF = 2816
NJ = DFF // 128
NC8 = D // 128
DEPTH = 2
EPS = 1e-6
TG = 1024
NTG = S // TG
NSG = TG // 512


class Op:
    __slots__ = ("eng", "fn", "deps", "signal", "count", "dkey", "ndma", "name", "epoch")


class Sched:
    ENGS = ("pe", "act", "dve", "pool", "sp")

    def __init__(self, nc):
        self.nc = nc
        self.ops = {e: [] for e in self.ENGS}
        self.last_writer = {}
        self.readers = {}
        self.dcount = {}
        self.nops = 0
        self.epoch = 0

    def add(self, eng, fn, reads=(), writes=(), dma=None, ndma=1, name=""):
        op = Op()
        op.eng = eng
        op.fn = fn
        op.signal = False
        op.count = None
        op.dkey = dma
        op.ndma = ndma
        op.name = name
        op.epoch = self.epoch
        if dma is not None:
            self.dcount[dma] = self.dcount.get(dma, 0) + 16 * ndma
            op.count = self.dcount[dma]
        deps = {}
        raw = set()
        for b in reads:
            w = self.last_writer.get(b)
            if w is not None:
                deps[id(w)] = w
                raw.add(id(w))
        for b in writes:
            w = self.last_writer.get(b)
            if w is not None:
                deps[id(w)] = w
            rd = self.readers.get(b)
            if rd:
                for r in rd.values():
                    if isinstance(r, list):
                        for rr in r:
                            deps[id(rr)] = rr
                    else:
                        deps[id(r)] = r
        fdeps = []
        for k, d in deps.items():
            if d.dkey is not None:
                fdeps.append(d)
                continue
            if d.eng == eng:
                if eng == "pe":
                    continue
                fdeps.append(d)
                continue
            fdeps.append(d)
        for d in fdeps:
            d.signal = True
        op.deps = fdeps
        for b in reads:
            rd = self.readers.setdefault(b, {})
            if dma is not None:
                rd.setdefault("dma", []).append(op)
            else:
                rd[eng] = op
        for b in writes:
            self.last_writer[b] = op
            self.readers[b] = {}
        self.ops[eng].append(op)
        self.nops += 1
        return op

    def emit(self):
        nc = self.nc
        esem = {(e, ep): nc.alloc_semaphore("sem_%s_%d" % (e, ep)) for e in self.ENGS for ep in range(self.epoch + 1)}
        dsem = {k: nc.alloc_semaphore("dsem_%d" % i) for i, k in enumerate(self.dcount)}
        for e in self.ENGS:
            c = {}
            for op in self.ops[e]:
                if op.dkey is None and op.signal:
                    c[op.epoch] = c.get(op.epoch, 0) + 1
                    op.count = c[op.epoch]
        ops = self.ops

        def run(eng_name, eng):
            waited = {}
            for op in ops[eng_name]:
                for d in op.deps:
                    if d.dkey is not None:
                        sem = dsem[d.dkey]
                    else:
                        sem = esem[(d.eng, d.epoch)]
                    if waited.get(sem.num, 0) < d.count:
                        eng.wait_ge(sem, d.count)
                        waited[sem.num] = d.count
                r = op.fn(eng)
                if op.name:
                    rr_ = r[-1] if isinstance(r, (list, tuple)) else r
                    rr_.annotate(op.name)
                if op.dkey is not None:
                    if not isinstance(r, (list, tuple)):
                        r = [r]
                    assert len(r) == op.ndma, (op.name, len(r), op.ndma)
                    for ins in r:
                        ins.then_inc(dsem[op.dkey], 16)
                elif op.signal:
                    if isinstance(r, (list, tuple)):
                        r = r[-1]
                    r.then_inc(esem[(eng_name, op.epoch)], 1)

        with nc.Block() as block:
            @block.tensor
            def _(e):
                run("pe", e)

            @block.scalar
            def _(e):
                run("act", e)

            @block.vector
            def _(e):
                run("dve", e)

            @block.gpsimd
            def _(e):
                run("pool", e)

            @block.sync
            def _(e):
                run("sp", e)


def _colmajor(v):
    return np.ascontiguousarray(v.reshape(-1, 128).T)


SM_OFF = {}


def pack_small(inp):
    cols = []
    off = 0

    def put(name, arr):
        nonlocal off
        SM_OFF[name] = (off, arr.shape[1])
        cols.append(arr.astype(np.float32))
        off += arr.shape[1]

    for l in range(DEPTH):
        for nm in ("ff1_norm_pre", "ff1_norm_post", "mix_norm_pre", "mix_norm_post", "ff2_norm_pre", "ff2_norm_post"):
            put("%s%d" % (nm, l), _colmajor(inp[nm][l]))
    for l in range(DEPTH):
        put("conv_b%d" % l, _colmajor(inp["conv_b"][l]))
        for j in range(4):
            put("conv_w%d_%d" % (j, l), _colmajor(inp["conv_w"][l, j]))
        put("ssd_norm%d" % l, _colmajor(inp["ssd_norm"][l]))
        put("pool_scale%d" % l, _colmajor(inp["pool_scale"][l]))
        for nm in ("dt_bias", "a_log", "d_skip"):
            put("%s%d" % (nm, l), np.broadcast_to(inp[nm][l][None, :], (128, 8)))
    return np.ascontiguousarray(np.concatenate(cols, axis=1))


NSMALL = 6 * DEPTH * 8 + DEPTH * (8 + 32 + 4 + 2 + 24)
WIN_PIECES = 10
TCOLS = list(range(0, 512)) + list(range(1800, 2312)) + list(range(2312, 2568)) + list(range(1544, 1800))


def make_consts():
    c = {}
    k = np.arange(128)
    c["tri"] = (k[:, None] <= k[None, :]).astype(np.float32)
    c["negmask"] = np.where(k[None, :] >= k[:, None], 0.0, -30000.0).astype(np.float32)
    inv_freq = 500000.0 ** (-np.arange(0, 16, 2, dtype=np.float32) / 16.0)
    pos = (np.arange(16)[None, :, None] * 128 + k[:, None, None]).astype(np.float32)
    ang = pos * inv_freq[None, None, :]
    c["rope"] = np.concatenate([np.cos(ang), np.sin(ang)], axis=-1).reshape(128, 16 * 16).astype(np.float32)
    bands = []
    for w in (2, 4, 8, 16):
        s_ = k[:, None]
        t_ = k[None, :]
        cur = ((s_ <= t_) & (t_ - s_ < w)).astype(np.float32) / w - (s_ == t_)
        cur0 = ((s_ <= t_) & (t_ - s_ < w)).astype(np.float32) / np.minimum(t_ + 1, w) - (s_ == t_)
        prev = (((t_ + 128 - s_) < w)).astype(np.float32) / w
        bands += [cur, cur0, prev]
    c["bands"] = np.concatenate(bands, axis=1).astype(np.float32)
    es = np.zeros((8, 8, 128), np.float32)
    for r in range(8):
        es[r, r, :] = 1.0
    c["esel"] = es.reshape(8, 1024)
    return c


import os
ADD_ENG = os.environ.get("KADD", "dve")


def build(stage="all"):
    nc = bass.Bass("TRN2", target_bir_lowering=False)
    sc = Sched(nc)

    x_d = nc.dram_tensor("x", [S, D], F32, kind="ExternalInput").ap()
    out_d = nc.dram_tensor("out", [S, D], F32, kind="ExternalOutput").ap()
    small_d = nc.dram_tensor("small", [128, NSMALL], F32, kind="ExternalInput").ap()
    ident_d = nc.dram_tensor("ident", [128, 128], F32, kind="ExternalInput").ap()
    tri_d = nc.dram_tensor("tri", [128, 128], F32, kind="ExternalInput").ap()
    negmask_d = nc.dram_tensor("negmask", [128, 128], F32, kind="ExternalInput").ap()
    rope_d = nc.dram_tensor("rope", [128, 256], F32, kind="ExternalInput").ap()
    bands_d = nc.dram_tensor("bands", [128, 12 * 128], F32, kind="ExternalInput").ap()
    esel_d = nc.dram_tensor("esel", [8, 1024], F32, kind="ExternalInput").ap()
    wgu_d = nc.dram_tensor("wgu", [DEPTH * 2 * NJ * 128, 2 * 8 * 128], F32, kind="ExternalInput").ap()
    wd_d = nc.dram_tensor("wd", [DEPTH * 2 * NC8 * 128, NJ * 128], F32, kind="ExternalInput").ap()
    win_d = nc.dram_tensor("win", [DEPTH * WIN_PIECES * 128, 2048], F32, kind="ExternalInput").ap()
    wdt_d = nc.dram_tensor("wdt", [DEPTH * 128, 64], F32, kind="ExternalInput").ap()
    wout_d = nc.dram_tensor("wout", [DEPTH * 4 * 128, 2048], F32, kind="ExternalInput").ap()
    wp_d = nc.dram_tensor("wp", [DEPTH * 128, 256], F32, kind="ExternalInput").ap()

    def sb(name, shape, dt):
        return nc.alloc_sbuf_tensor(name, shape, dt)

    xT = sb("xT", [128, NC8, S], F32)
    small = sb("small_sb", [128, NSMALL], F32)
    g32 = sb("g32", [128, 6 * DEPTH * 8], F32)
    ident = sb("ident_sb", [128, 128], F32)
    ones_bf = sb("ones_bf", [128, 128], BF16)
    ones_f = sb("ones_f", [128, 128], F32)
    neghalf = sb("neghalf", [128, 8], F32)
    epsc = sb("epsc", [128, 8], F32)
    sq = [sb("sq%d" % i, [128, 512], BF16) for i in range(2)]
    rstd = [sb("rstd%d" % i, [128, 512], F32) for i in range(2)]
    tmp = [sb("tmp%d" % i, [128, 512], F32) for i in range(2)]
    tri = sb("tri_sb", [128, 128], F32)
    negmask = sb("negmask_sb", [128, 128], F32)
    cmask_bf = sb("cmask_bf", [128, 128], BF16)
    ident_bf = sb("ident_bf", [128, 128], BF16)
    rope = sb("rope_sb", [128, 256], F32)
    bands = sb("bands_sb", [128, 12 * 128], BF16)
    esel = sb("esel_sb", [8, 1024], BF16)
    ptmp = tmp
    ps = [nc.alloc_psum_tensor("ps%d" % i, [128, 512], F32) for i in range(8)]

    with nc.reset_on_exit():
        hnT = sb("hnT", [128, NC8, TG], BF16)
        aT = sb("aT", [128, NJ, TG], BF16)
        fT = sb("fT", [128, NC8, TG], F32)
        sil = [sb("sil%d" % i, [128, 512], F32) for i in range(2)]
        wgu = [sb("wgu%d" % i, [128, 2 * 8 * 128], BF16) for i in range(2)]
        wd = [sb("wd%d" % i, [128, NJ * 128], BF16) for i in range(2)]
        xio = [fT[:, i, :] for i in range(2)]
    hnTm = sb("hnTm", [128, NC8, 512], BF16)
    KT = sb("KT", [128, 2, S], BF16)
    vp = sb("vp", [128, 16, 4, 65], BF16)
    wsl = [sb("wsl%d" % i, [128, 2048], BF16) for i in range(2)]
    wdt = sb("wdt_sb", [128, 64], BF16)
    wp = sb("wp_sb", [128, 256], BF16)
    pre = [sb("pre%d" % i, [128, 515], F32) for i in range(2)]
    carry = sb("carry", [128, 8, 3], F32)
    cacc = [sb("cacc%d" % i, [128, 512], F32) for i in range(2)]
    xbcf = sb("xbcf", [128, 8, 512], F32)
    bcT = sb("bcT", [128, 4, 512], BF16)
    x_tm = sb("x_tm", [128, 4, 512], BF16)
    B_tm = sb("B_tm", [128, 4, 256], BF16)
    zs = sb("zs", [128, 4, 512], F32)
    dtt = sb("dtt", [128, 4, 8], F32)
    dAt = sb("dAt", [128, 4, 8], F32)
    Aneg = sb("Aneg", [128, 8], F32)
    qkf = sb("qkf", [128, 256], F32)
    ropet = [sb("ropet%d" % i, [128, 8, 8], F32) for i in range(4)]
    QT = sb("QT", [128, 2, 512], BF16)
    u_tm = sb("u_tm", [128, 5, 256], BF16)
    kmf = sb("kmf", [128, 2, 8], F32)
    kmT = sb("kmT", [128, 2, 8], BF16)
    state_f = sb("state_f", [128, 512], F32)
    state_b = sb("state_b", [128, 512], BF16)
    Dall = sb("Dall", [128, 1024], F32)
    seg = sb("seg", [128, 8, 128], BF16)
    MT = sb("MT", [128, 8, 128], BF16)
    Ein = sb("Ein", [128, 24], F32)
    Eout = sb("Eout", [128, 24], F32)
    negacs = sb("negacs", [128, 8], F32)
    xdt = sb("xdt", [128, 512], BF16)
    xdtd = sb("xdtd", [128, 512], BF16)
    y_sb = sb("y_sb", [128, 512], F32)
    y2 = sb("y2", [128, 512], F32)
    ssq = sb("ssq", [128, 4], F32)
    d_tm = sb("d_tm", [128, 256], F32)
    dT = sb("dT", [128, 2, 512], BF16)
    PT = [sb("PT%d" % i, [128, 512], BF16) for i in range(2)]
    gate = sb("gate", [128, 4, 8], F32)
    cmpb = sb("cmpb", [128, 4, 8, 8], F32)
    cnt = sb("cnt", [128, 4, 8], F32)
    bias_tm = sb("bias_tm", [128, 4, 4, 8], F32)
    biasT = [sb("biasT%d" % i, [8, 512], BF16) for i in range(2)]
    att_tm = sb("att_tm", [128, 4, 256], F32)
    rden = sb("rden", [128, 4], F32)
    ycatT = sb("ycatT", [128, NC8, 512], BF16)
    mT = xbcf

    def gsl(name):
        o, n = SM_OFF[name]
        return slice(o, o + n)

    def scol(name, i=0):
        o, n = SM_OFF[name]
        return small[:, o + i:o + i + 1]

    sc.add("sp", lambda e: e.dma_start(out=small[:], in_=small_d), writes=["small"], dma="small")
    sc.add("sp", lambda e: e.dma_start(out=ident[:], in_=ident_d), writes=["ident"], dma="ident")
    sc.add("sp", lambda e: e.dma_start(out=tri[:], in_=tri_d), writes=["tri"], dma="tri")
    sc.add("sp", lambda e: e.dma_start(out=negmask[:], in_=negmask_d), writes=["negmask"], dma="negmask")
    sc.add("sp", lambda e: e.dma_start(out=rope[:], in_=rope_d), writes=["rope"], dma="rope")
    sc.add("pool", lambda e: e.dma_start(out=bands[:], in_=bands_d), writes=["bands"], dma="bands")
    sc.add("pool", lambda e: e.dma_start(out=esel[:], in_=esel_d), writes=["esel"], dma="esel")
    sc.add("dve", lambda e: e.memset(ones_bf[:], 1.0), writes=["ones"])
    sc.add("dve", lambda e: e.memset(ones_f[:], 1.0), writes=["ones_f"])
    sc.add("pool", lambda e: e.memset(neghalf[:], -0.5), writes=["neghalf"])
    sc.add("dve", lambda e: e.memset(epsc[:], EPS), writes=["epsc"])
    sc.add("dve", lambda e: e.tensor_copy(out=cmask_bf[:], in_=negmask[:]), reads=["negmask"], writes=["cmask"])
    sc.add("dve", lambda e: e.tensor_copy(out=ident_bf[:], in_=ident[:]), reads=["ident"], writes=["ident_bf"])
    for l in range(DEPTH):
        for i, (nm, coef) in enumerate((("ff1_norm_pre", 1.0), ("ff1_norm_post", 0.5), ("mix_norm_pre", 1.0),
                                        ("mix_norm_post", 1.0), ("ff2_norm_pre", 1.0), ("ff2_norm_post", 0.5))):
            s_ = gsl("%s%d" % (nm, l))
            sc.add("dve", lambda e, s_=s_, coef=coef: e.tensor_scalar(
                out=g32[:, s_], in0=small[:, s_], scalar1=coef, scalar2=0.0, op0=ALU.mult, op1=ALU.add),
                reads=["small"], writes=[("g32", s_.start)])

    def barrier():
        lasts = []
        for e_ in sc.ENGS:
            for o_ in reversed(sc.ops[e_]):
                if o_.dkey is None:
                    lasts.append(o_)
                    break
        sc.epoch += 1
        for e in ("pe", "act", "dve", "pool", "sp"):
            op = sc.add(e, lambda eng: eng.nop(), name="barrier")
            for d in lasts:
                if d is not op:
                    d.signal = True
                    op.deps.append(d)

    def xkey(c, t512):
        return ("xT", c, t512)

    for t in range(S // 128):
        slot = t % 2
        sc.add("sp", lambda e, t=t, slot=slot: e.dma_start(out=xio[slot], in_=x_d[t * 128:(t + 1) * 128, :]),
               writes=[("xio", slot)], dma=("xio", slot))
        for q in range(2):
            pb = 6 + q

            def tr(e, q=q, slot=slot, pb=pb):
                r = None
                for i in range(4):
                    c = 4 * q + i
                    r = e.transpose(out=ps[pb][:, i * 128:(i + 1) * 128], in_=xio[slot][:, c * 128:(c + 1) * 128],
                                    identity=ident[:])
                return r
            sc.add("pe", tr, reads=[("xio", slot), "ident"], writes=[("ps", pb)])
            sc.add("dve", lambda e, q=q, t=t, pb=pb: e.tensor_copy(
                out=xT[:, 4 * q:4 * q + 4, t * 128:(t + 1) * 128],
                in_=ps[pb][:].rearrange("p (c t) -> p c t", c=4)),
                reads=[("ps", pb)], writes=[xkey(4 * q + i, t // 4) for i in range(4)])

    barrier()

    def rstd_op(sg, pb):
        sc.add("act", lambda e, sg=sg, pb=pb: e.activation(
            out=tmp[sg][:], in_=ps[pb][:], func=AF.Sqrt, scale=1.0 / 1024, bias=epsc[:, 0:1]),
            reads=[("ps", pb), "epsc"], writes=[("tmp", sg)])
        sc.add("dve", lambda e, sg=sg: e.reciprocal(out=rstd[sg][:], in_=tmp[sg][:]),
               reads=[("tmp", sg)], writes=[("rstd", sg)])

    def prenorm(t512, sg, gpre, dst, dkey):
        cols = slice(t512 * 512, (t512 + 1) * 512)
        for c in range(NC8):
            s2 = c % 2
            sc.add("act", lambda e, c=c, cols=cols, s2=s2: e.activation(
                out=sq[s2][:], in_=xT[:, c, cols], func=AF.Square),
                reads=[xkey(c, t512)], writes=[("sq", s2)])
            sc.add("pe", lambda e, c=c, s2=s2, sg=sg: e.matmul(
                ps[6 + sg][:], lhsT=ones_bf[:], rhs=sq[s2][:], start=(c == 0), stop=(c == NC8 - 1)),
                reads=[("sq", s2), "ones"], writes=[("ps", 6 + sg)])
        rstd_op(sg, 6 + sg)
        for c in range(NC8):
            sc.add("dve", lambda e, c=c, sg=sg, cols=cols: e.scalar_tensor_tensor(
                out=dst(c), in0=xT[:, c, cols],
                scalar=g32[:, gpre.start + c:gpre.start + c + 1], in1=rstd[sg][:],
                op0=ALU.mult, op1=ALU.mult),
                reads=[xkey(c, t512), ("rstd", sg), ("g32", gpre.start)], writes=[dkey(c)])

    def postnorm_add(t512, sg, gpost, src, skey):
        cols = slice(t512 * 512, (t512 + 1) * 512)
        for m in range(NC8):
            tb = m % 2
            sc.add("dve", lambda e, m=m, sg=sg, tb=tb: e.scalar_tensor_tensor(
                out=cacc_or_tmp(tb), in0=src(m),
                scalar=g32[:, gpost.start + m:gpost.start + m + 1], in1=rstd[sg][:],
                op0=ALU.mult, op1=ALU.mult),
                reads=[skey(m), ("rstd", sg), ("g32", gpost.start)], writes=[("tmp", tb)])
            sc.add(ADD_ENG, lambda e, m=m, cols=cols, tb=tb: e.tensor_tensor(
                out=xT[:, m, cols], in0=xT[:, m, cols], in1=cacc_or_tmp(tb), op=ALU.add),
                reads=[("tmp", tb), xkey(m, t512)], writes=[xkey(m, t512)])

    def cacc_or_tmp(tb):
        return ptmp[tb][:]

    def ffn(l, f, pre_name, post_name):
        gpre = gsl("%s%d" % (pre_name, l))
        gpost = gsl("%s%d" % (post_name, l))
        wrow = (l * 2 + f)
        for tg in range(NTG):
            for sg in range(NSG):
                prenorm(tg * NSG + sg, sg, gpre, lambda c, sg=sg: hnT[:, c, sg * 512:(sg + 1) * 512],
                        lambda c, sg=sg: ("hnT", c, sg))
            import os
            KCUT = int(os.environ.get("KCUT", "9")) if f == 1 else 9
            if KCUT < 1:
                continue
            it = 0
            for j in range(NJ):
                ws = j % 2
                r0 = (wrow * NJ + j) * 128
                sc.add("pool", lambda e, ws=ws, r0=r0: e.dma_start(out=wgu[ws][:], in_=wgu_d[r0:r0 + 128, :]),
                       writes=[("wgu", ws)], dma=("wgu", ws))
                for sg in range(NSG):
                    b = it % 2
                    it += 1

                    def mm(e, ws=ws, sg=sg, b=b):
                        r = None
                        for g in range(2):
                            for k in range(NC8):
                                r = e.matmul(ps[2 * g + b][:], lhsT=wgu[ws][:, (g * 8 + k) * 128:(g * 8 + k + 1) * 128],
                                             rhs=hnT[:, k, sg * 512:(sg + 1) * 512], start=(k == 0), stop=(k == NC8 - 1))
                        return r
                    sc.add("pe", mm, reads=[("wgu", ws)] + [("hnT", k, sg) for k in range(NC8)],
                           writes=[("ps", b), ("ps", 2 + b)])
                    sc.add("act", lambda e, b=b: e.activation(out=sil[b][:], in_=ps[b][:], func=AF.Silu),
                           reads=[("ps", b)], writes=[("sil", b)])
                    sc.add("dve", lambda e, b=b, j=j, sg=sg: e.tensor_tensor(
                        out=aT[:, j, sg * 512:(sg + 1) * 512], in0=sil[b][:], in1=ps[2 + b][:], op=ALU.mult),
                        reads=[("sil", b), ("ps", 2 + b)], writes=[("aT", j, sg)])
            if KCUT < 2:
                continue
            it = 0
            pend = [None]
            for m in range(NC8):
                ws = m % 2
                r0 = (wrow * NC8 + m) * 128
                hw_ = NJ * 64
                sc.add("pool", lambda e, ws=ws, r0=r0, hw_=hw_: [
                    e.dma_start(out=wd[ws][:, h * hw_:(h + 1) * hw_], in_=wd_d[r0:r0 + 128, h * hw_:(h + 1) * hw_])
                    for h in range(2)],
                       writes=[("wd", ws)], dma=("wd", ws), ndma=2)
                for sg in range(NSG):
                    b = 4 + (it % 2)
                    it += 1

                    def mm2(e, ws=ws, sg=sg, b=b):
                        r = None
                        for j in range(NJ):
                            r = e.matmul(ps[b][:], lhsT=wd[ws][:, j * 128:(j + 1) * 128],
                                         rhs=aT[:, j, sg * 512:(sg + 1) * 512], start=(j == 0), stop=(j == NJ - 1))
                        return r
                    sc.add("pe", mm2, reads=[("wd", ws)] + [("aT", j, sg) for j in range(NJ)], writes=[("ps", b)])
                    if pend[0] is not None:
                        pend[0]()
                        pend[0] = None
                    s2 = it % 2
                    sc.add("dve", lambda e, b=b, m=m, sg=sg: e.tensor_copy(
                        out=fT[:, m, sg * 512:(sg + 1) * 512], in_=ps[b][:]),
                        reads=[("ps", b)], writes=[("fT", m, sg)])
                    sc.add("act", lambda e, m=m, sg=sg, s2=s2: e.activation(
                        out=sq[s2][:], in_=fT[:, m, sg * 512:(sg + 1) * 512], func=AF.Square),
                        reads=[("fT", m, sg)], writes=[("sq", s2)])
                    pend[0] = (lambda s2=s2, sg=sg, m=m: sc.add("pe", lambda e: e.matmul(
                        ps[6 + sg][:], lhsT=ones_bf[:], rhs=sq[s2][:], start=(m == 0), stop=(m == NC8 - 1)),
                        reads=[("sq", s2), "ones"], writes=[("ps", 6 + sg)]))
            if pend[0] is not None:
                pend[0]()
                pend[0] = None
            if KCUT < 3:
                continue
            for sg in range(NSG):
                rstd_op(sg, 6 + sg)
                postnorm_add(tg * NSG + sg, sg, gpost, lambda m, sg=sg: fT[:, m, sg * 512:(sg + 1) * 512],
                             lambda m, sg=sg: ("fT", m, sg))

    def mixer(l):
        gpre = gsl("mix_norm_pre%d" % l)
        gpost = gsl("mix_norm_post%d" % l)
        psb = [0]

        def nb2(pair):
            psb[0] += 1
            return pair * 2 + (psb[0] % 2)

        wcnt = [0]

        def load_w(src_ap):
            ws = wcnt[0] % 2
            wcnt[0] += 1
            sc.add("pool", lambda e, ws=ws, src_ap=src_ap: e.dma_start(out=wsl[ws][:], in_=src_ap),
                   writes=[("wsl", ws)], dma=("wsl", ws))
            return ws

        sc.add("pool", lambda e: e.dma_start(out=wdt[:], in_=wdt_d[l * 128:(l + 1) * 128, :]), writes=["wdt"], dma="wdt")
        sc.add("pool", lambda e: e.dma_start(out=wp[:], in_=wp_d[l * 128:(l + 1) * 128, :]), writes=["wp"], dma="wp")
        sc.add("dve", lambda e: e.memset(vp[:], 1.0), writes=["vp_init"] + [("vp", t_) for t_ in range(16)])
        sc.add("dve", lambda e: e.memset(carry[:], 0.0), writes=[("carry", c) for c in range(8)])
        sc.add("dve", lambda e: e.memset(state_f[:], 0.0), writes=["state_f"])
        sc.add("dve", lambda e: e.memset(state_b[:], 0.0), writes=["state_b"])
        sc.add("dve", lambda e: e.memset(u_tm[:, 4, :], 0.0), writes=[("u_tm", 4)])
        sc.add("act", lambda e: e.activation(out=Aneg[:], in_=small[:, gsl("a_log%d" % l)], func=AF.Exp),
               reads=["small"], writes=["Aneg0"])
        sc.add("dve", lambda e: e.tensor_scalar(out=Aneg[:], in0=Aneg[:], scalar1=-1.0, scalar2=0.0,
                                                op0=ALU.mult, op1=ALU.add), reads=["Aneg0"], writes=["Aneg"])

        for grp in range(4):
            t0 = grp * 4
            prenorm(grp, 0, gpre, lambda c: hnTm[:, c, :], lambda c: ("hnTm", c))
            if grp > 0:
                sc.add("pool", lambda e: e.tensor_copy(out=u_tm[:, 4, :], in_=u_tm[:, 3, :]),
                       reads=[("u_tm", 3)], writes=[("u_tm", 4)])
            for fp in range(4):
                ws = load_w(win_d[(l * WIN_PIECES + fp) * 128:(l * WIN_PIECES + fp + 1) * 128, :])
                for cc in range(2):
                    ch = fp * 2 + cc
                    pb = nb2(0)

                    def mmf(e, ws=ws, cc=cc, pb=pb):
                        r = None
                        for k in range(NC8):
                            r = e.matmul(ps[pb][:], lhsT=wsl[ws][:, (cc * 8 + k) * 128:(cc * 8 + k + 1) * 128],
                                         rhs=hnTm[:, k, :], start=(k == 0), stop=(k == NC8 - 1))
                        return r
                    sc.add("pe", mmf, reads=[("wsl", ws)] + [("hnTm", k) for k in range(NC8)], writes=[("ps", pb)])
                    pr = ch % 2
                    sc.add("dve", lambda e, pr=pr, ch=ch: e.tensor_copy(out=pre[pr][:, 0:3], in_=carry[:, ch, :]),
                           reads=[("carry", ch)], writes=[("pre", pr, 0)])
                    sc.add("act", lambda e, pr=pr, pb=pb: e.copy(out=pre[pr][:, 3:515], in_=ps[pb][:]),
                           reads=[("ps", pb)], writes=[("pre", pr, 1)])
                    sc.add("dve", lambda e, pr=pr, ch=ch: e.tensor_copy(out=carry[:, ch, :], in_=pre[pr][:, 512:515]),
                           reads=[("pre", pr, 1)], writes=[("carry", ch)])
                    sc.add("dve", lambda e, pr=pr, ch=ch: e.tensor_scalar(
                        out=cacc[pr][:], in0=pre[pr][:, 0:512], scalar1=scol("conv_w0_%d" % l, ch), scalar2=0.0,
                        op0=ALU.mult, op1=ALU.add),
                        reads=[("pre", pr, 0), ("pre", pr, 1), "small"], writes=[("cacc", pr)])
                    for j in range(1, 4):
                        sc.add("dve", lambda e, pr=pr, ch=ch, j=j: e.scalar_tensor_tensor(
                            out=cacc[pr][:], in0=pre[pr][:, j:j + 512], scalar=scol("conv_w%d_%d" % (j, l), ch),
                            in1=cacc[pr][:], op0=ALU.mult, op1=ALU.add),
                            reads=[("pre", pr, 0), ("pre", pr, 1), ("cacc", pr), "small"], writes=[("cacc", pr)])
                    sc.add("act", lambda e, pr=pr, ch=ch: e.activation(
                        out=xbcf[:, ch, :], in_=cacc[pr][:], func=AF.Silu, bias=scol("conv_b%d" % l, ch), scale=1.0),
                        reads=[("cacc", pr), "small"], writes=[("xbcf", ch)])
                    if ch >= 4:
                        sc.add("pool", lambda e, ch=ch: e.tensor_copy(out=bcT[:, ch - 4, :], in_=xbcf[:, ch, :]),
                               reads=[("xbcf", ch)], writes=[("bcT", ch - 4)])
            for ti in range(4):
                for half in range(2):
                    pb = nb2(1)
                    chs = [0, 1, 2, 3] if half == 0 else [4, 5]

                    def trx(e, ti=ti, chs=chs, pb=pb):
                        r = None
                        for i, ch in enumerate(chs):
                            r = e.transpose(out=ps[pb][:, i * 128:(i + 1) * 128],
                                            in_=xbcf[:, ch, ti * 128:(ti + 1) * 128], identity=ident[:])
                        return r
                    sc.add("pe", trx, reads=[("xbcf", ch) for ch in chs] + ["ident"], writes=[("ps", pb)])
                    if half == 0:
                        sc.add("act", lambda e, ti=ti, pb=pb: e.copy(out=x_tm[:, ti, :], in_=ps[pb][:]),
                               reads=[("ps", pb)], writes=[("x_tm", ti)])
                    else:
                        sc.add("act", lambda e, ti=ti, pb=pb: e.copy(out=B_tm[:, ti, :], in_=ps[pb][:, 0:256]),
                               reads=[("ps", pb)], writes=[("B_tm", ti)])
            for tp in range(6):
                ws = load_w(win_d[(l * WIN_PIECES + 4 + tp) * 128:(l * WIN_PIECES + 5 + tp) * 128, :])
                for ti in range(4):
                    pb = nb2(0)

                    def mmt(e, ws=ws, ti=ti, pb=pb):
                        r = None
                        for k in range(NC8):
                            r = e.matmul(ps[pb][:, 0:256], lhsT=hnTm[:, k, ti * 128:(ti + 1) * 128],
                                         rhs=wsl[ws][:, k * 256:(k + 1) * 256], start=(k == 0), stop=(k == NC8 - 1))
                        return r
                    sc.add("pe", mmt, reads=[("wsl", ws)] + [("hnTm", k) for k in range(NC8)], writes=[("ps", pb)])
                    if tp < 2:
                        sc.add("act", lambda e, ti=ti, tp=tp, pb=pb: e.activation(
                            out=zs[:, ti, tp * 256:(tp + 1) * 256], in_=ps[pb][:, 0:256], func=AF.Silu),
                            reads=[("ps", pb)], writes=[("zs", ti, tp)])
                    elif tp < 4:
                        isk = tp - 2
                        sc.add("act", lambda e, pb=pb: e.copy(out=qkf[:, 0:256], in_=ps[pb][:, 0:256]),
                               reads=[("ps", pb)], writes=["qkf"])
                        tile = t0 + ti
                        qv = qkf[:, 0:256].rearrange("p (h d) -> p h d", h=4)
                        cosb = rope[:, tile * 16:tile * 16 + 8].unsqueeze(1).broadcast_to([128, 4, 8])
                        sinb = rope[:, tile * 16 + 8:tile * 16 + 16].unsqueeze(1).broadcast_to([128, 4, 8])
                        x1 = qv[:, :, 0:8]
                        x2 = qv[:, :, 8:16]
                        r_ = [ropet[i][:, 0:4, :] for i in range(4)]
                        sc.add("dve", lambda e, x1=x1, cosb=cosb, r_=r_: e.tensor_tensor(out=r_[0], in0=x1, in1=cosb, op=ALU.mult),
                               reads=["qkf", "rope"], writes=[("ropet", 0)])
                        sc.add("dve", lambda e, x2=x2, sinb=sinb, r_=r_: e.tensor_tensor(out=r_[1], in0=x2, in1=sinb, op=ALU.mult),
                               reads=["qkf", "rope"], writes=[("ropet", 1)])
                        sc.add("dve", lambda e, x2=x2, cosb=cosb, r_=r_: e.tensor_tensor(out=r_[2], in0=x2, in1=cosb, op=ALU.mult),
                               reads=["qkf", "rope"], writes=[("ropet", 2)])
                        sc.add("dve", lambda e, x1=x1, sinb=sinb, r_=r_: e.tensor_tensor(out=r_[3], in0=x1, in1=sinb, op=ALU.mult),
                               reads=["qkf", "rope"], writes=[("ropet", 3)])
                        sc.add("dve", lambda e, x1=x1, r_=r_: e.tensor_tensor(out=x1, in0=r_[0], in1=r_[1], op=ALU.subtract),
                               reads=[("ropet", 0), ("ropet", 1)], writes=["qkf"])
                        sc.add("dve", lambda e, x2=x2, r_=r_: e.tensor_tensor(out=x2, in0=r_[2], in1=r_[3], op=ALU.add),
                               reads=[("ropet", 2), ("ropet", 3), "qkf"], writes=["qkf"])
                        pb2 = nb2(1)

                        def trq(e, pb2=pb2):
                            r = None
                            for hp in range(2):
                                r = e.transpose(out=ps[pb2][:, hp * 128:(hp + 1) * 128],
                                                in_=qkf[:, hp * 128:(hp + 1) * 128], identity=ident[:])
                            return r
                        sc.add("pe", trq, reads=["qkf", "ident"], writes=[("ps", pb2)])
                        if isk:
                            sc.add("act", lambda e, pb2=pb2, tile=tile: e.copy(
                                out=KT[:, :, tile * 128:(tile + 1) * 128],
                                in_=ps[pb2][:, 0:256].rearrange("p (a t) -> p a t", a=2)),
                                reads=[("ps", pb2)], writes=[("KT", tile)])
                        else:
                            sc.add("act", lambda e, pb2=pb2, ti=ti: e.copy(
                                out=QT[:, :, ti * 128:(ti + 1) * 128],
                                in_=ps[pb2][:, 0:256].rearrange("p (a t) -> p a t", a=2)),
                                reads=[("ps", pb2)], writes=[("QT", ti)])
                    elif tp == 4:
                        tile = t0 + ti
                        sc.add("act", lambda e, pb=pb, tile=tile: e.copy(
                            out=vp[:, tile, :, 0:64], in_=ps[pb][:, 0:256].rearrange("p (h d) -> p h d", h=4)),
                            reads=[("ps", pb), "vp_init"], writes=[("vp", tile)])
                    else:
                        sc.add("act", lambda e, pb=pb, ti=ti: e.copy(out=u_tm[:, ti, :], in_=ps[pb][:, 0:256]),
                               reads=[("ps", pb)], writes=[("u_tm", ti)])
            for ti in range(4):
                pb = nb2(0)

                def mmd(e, ti=ti, pb=pb):
                    r = None
                    for k in range(NC8):
                        r = e.matmul(ps[pb][:, 0:8], lhsT=hnTm[:, k, ti * 128:(ti + 1) * 128],
                                     rhs=wdt[:, k * 8:(k + 1) * 8], start=(k == 0), stop=(k == NC8 - 1))
                    return r
                sc.add("pe", mmd, reads=["wdt"] + [("hnTm", k) for k in range(NC8)], writes=[("ps", pb)])
                sc.add("dve", lambda e, ti=ti, pb=pb: e.tensor_tensor(
                    out=dtt[:, ti, :], in0=ps[pb][:, 0:8], in1=small[:, gsl("dt_bias%d" % l)], op=ALU.add),
                    reads=[("ps", pb), "small"], writes=[("dtt", ti)])
            sc.add("act", lambda e: e.activation(out=dtt[:], in_=dtt[:], func=AF.Exp),
                   reads=[("dtt", ti) for ti in range(4)], writes=["dtt_e"])
            sc.add("act", lambda e: e.activation(out=dtt[:], in_=dtt[:], func=AF.Ln, bias=1.0, scale=1.0),
                   reads=["dtt_e"], writes=["dtt_f"])
            sc.add("dve", lambda e: e.tensor_tensor(
                out=dAt[:], in0=dtt[:], in1=Aneg[:].unsqueeze(1).broadcast_to([128, 4, 8]), op=ALU.mult),
                reads=["dtt_f", "Aneg"], writes=["dAt"])

            for bi in range(2):
                blk = grp * 2 + bi
                sc.add("dve", lambda e, blk=blk: e.tensor_reduce(
                    out=kmf[:, :, blk], in_=KT[:, :, blk * 256:(blk + 1) * 256], axis=AX.X, op=ALU.add),
                    reads=[("KT", 2 * blk), ("KT", 2 * blk + 1)], writes=[("kmf", blk)])
                sc.add("dve", lambda e, blk=blk: e.tensor_scalar(
                    out=kmT[:, :, blk:blk + 1], in0=kmf[:, :, blk:blk + 1], scalar1=1.0 / 256, scalar2=0.0,
                    op0=ALU.mult, op1=ALU.add),
                    reads=[("kmf", blk)], writes=[("kmT", blk)])

            for ci in range(4):
                sc.add("dve", lambda e, ci=ci: e.tensor_tensor(
                    out=Dall[:].rearrange("p (h l) -> p h l", h=8), in0=dAt[:, ci, :].unsqueeze(2).broadcast_to([128, 8, 128]),
                    in1=tri[:].unsqueeze(1).broadcast_to([128, 8, 128]), op=ALU.mult),
                    reads=["dAt", "tri"], writes=["Dall"])

                def mm_acs(e, ci=ci):
                    e.matmul(ps[2][:, 0:8], lhsT=tri[:], rhs=dAt[:, ci, :], start=True, stop=True)
                    return e.matmul(ps[2][:, 8:16], lhsT=ones_f[:], rhs=dAt[:, ci, :], start=True, stop=True)
                sc.add("pe", mm_acs, reads=["dAt", "tri", "ones_f"], writes=[("ps", 2)])

                def mm_row(e):
                    r = None
                    for h in range(8):
                        reg = ps[4 + h // 4][:, (h % 4) * 128:(h % 4 + 1) * 128]
                        e.matmul(reg, lhsT=ones_f[:], rhs=Dall[:, h * 128:(h + 1) * 128], start=True, stop=False)
                        r = e.matmul(reg, lhsT=ident[:], rhs=negmask[:], start=False, stop=True)
                    return r
                sc.add("pe", mm_row, reads=["Dall", "ones_f", "ident", "negmask"], writes=[("ps", 4), ("ps", 5)])
                pbc = 3

                def mm_cb(e, ci=ci):
                    r = None
                    for g in range(2):
                        r = e.matmul(ps[pbc][:, g * 128:(g + 1) * 128], lhsT=bcT[:, g, ci * 128:(ci + 1) * 128],
                                     rhs=bcT[:, 2 + g, ci * 128:(ci + 1) * 128], start=True, stop=True)
                    return r
                sc.add("pe", mm_cb, reads=[("bcT", i) for i in range(4)], writes=[("ps", 3)])
                sc.add("dve", lambda e: e.tensor_copy(out=Ein[:, 0:16], in_=ps[2][:, 0:16]),
                       reads=[("ps", 2)], writes=["Ein0"])
                sc.add("dve", lambda e: e.tensor_scalar(out=negacs[:], in0=Ein[:, 0:8], scalar1=-1.0, scalar2=0.0,
                                                        op0=ALU.mult, op1=ALU.add), reads=["Ein0"], writes=["negacs"])
                sc.add("dve", lambda e: e.tensor_tensor(out=Ein[:, 16:24], in0=Ein[:, 8:16], in1=Ein[:, 0:8], op=ALU.subtract),
                       reads=["Ein0"], writes=["Ein1"])
                sc.add("act", lambda e: e.activation(out=Eout[:], in_=Ein[:], func=AF.Exp),
                       reads=["Ein0", "Ein1"], writes=["Eout"])
                for h in range(8):
                    sc.add("act", lambda e, h=h: e.activation(
                        out=seg[:, h, :], in_=ps[4 + h // 4][:, (h % 4) * 128:(h % 4 + 1) * 128], func=AF.Exp,
                        bias=negacs[:, h:h + 1], scale=1.0),
                        reads=[("ps", 4 + h // 4), "negacs"], writes=[("seg", h)])
                for h in range(8):
                    sc.add("dve", lambda e, h=h: e.tensor_tensor(
                        out=MT[:, h, :], in0=seg[:, h, :], in1=ps[pbc][:, (h // 4) * 128:(h // 4 + 1) * 128], op=ALU.mult),
                        reads=[("seg", h), ("ps", 3)], writes=[("MT", h)])
                xv = x_tm[:, ci, :].rearrange("p (h d) -> p h d", h=8)
                sc.add("dve", lambda e, ci=ci, xv=xv: e.tensor_tensor(
                    out=xdt[:].rearrange("p (h d) -> p h d", h=8), in0=xv,
                    in1=dtt[:, ci, :].unsqueeze(2).broadcast_to([128, 8, 64]), op=ALU.mult),
                    reads=[("x_tm", ci), "dtt_f"], writes=["xdt"])
                sc.add("dve", lambda e: e.tensor_tensor(
                    out=xdtd[:].rearrange("p (h d) -> p h d", h=8), in0=xdt[:].rearrange("p (h d) -> p h d", h=8),
                    in1=Eout[:, 16:24].unsqueeze(2).broadcast_to([128, 8, 64]), op=ALU.mult),
                    reads=["xdt", "Eout"], writes=["xdtd"])

                def mm_y(e):
                    r = None
                    for h in range(8):
                        r = e.matmul(ps[6][:, h * 64:(h + 1) * 64], lhsT=MT[:, h, :], rhs=xdt[:, h * 64:(h + 1) * 64],
                                     start=True, stop=True)
                    return r
                sc.add("pe", mm_y, reads=[("MT", h) for h in range(8)] + ["xdt"], writes=[("ps", 6)])

                def mm_off(e, ci=ci):
                    r = None
                    for g in range(2):
                        r = e.matmul(ps[7][:, g * 256:(g + 1) * 256], lhsT=bcT[:, 2 + g, ci * 128:(ci + 1) * 128],
                                     rhs=state_b[:, g * 256:(g + 1) * 256], start=True, stop=True)
                    return r
                sc.add("pe", mm_off, reads=[("bcT", 2), ("bcT", 3), "state_b"], writes=[("ps", 7)])
                sc.add("dve", lambda e: e.tensor_tensor(
                    out=y_sb[:].rearrange("p (h d) -> p h d", h=8), in0=ps[7][:].rearrange("p (h d) -> p h d", h=8),
                    in1=Eout[:, 0:8].unsqueeze(2).broadcast_to([128, 8, 64]), op=ALU.mult),
                    reads=[("ps", 7), "Eout"], writes=["y_sb"])
                sc.add("dve", lambda e: e.tensor_tensor(out=y_sb[:], in0=y_sb[:], in1=ps[6][:], op=ALU.add),
                       reads=[("ps", 6), "y_sb"], writes=["y_sb"])
                sc.add("dve", lambda e, xv=xv: e.tensor_tensor(
                    out=y2[:].rearrange("p (h d) -> p h d", h=8), in0=xv,
                    in1=small[:, gsl("d_skip%d" % l)].unsqueeze(2).broadcast_to([128, 8, 64]), op=ALU.mult),
                    reads=[("x_tm", ci), "small"], writes=["y2"])
                sc.add("dve", lambda e: e.tensor_tensor(out=y_sb[:], in0=y_sb[:], in1=y2[:], op=ALU.add),
                       reads=["y_sb", "y2"], writes=["y_sb"])
                sc.add("dve", lambda e, ci=ci: e.tensor_tensor(out=y_sb[:], in0=y_sb[:], in1=zs[:, ci, :], op=ALU.mult),
                       reads=["y_sb", ("zs", ci, 0), ("zs", ci, 1)], writes=["y_sb"])
                sc.add("dve", lambda e: e.tensor_tensor(out=y2[:], in0=y_sb[:], in1=y_sb[:], op=ALU.mult),
                       reads=["y_sb", "y2"], writes=["y2"])
                sc.add("dve", lambda e: e.tensor_reduce(out=ssq[:, 0:1], in_=y2[:], axis=AX.X, op=ALU.add),
                       reads=["y2"], writes=["ssq0"])
                sc.add("dve", lambda e: e.tensor_scalar(out=ssq[:, 1:2], in0=ssq[:, 0:1], scalar1=1.0 / 512, scalar2=EPS,
                                                        op0=ALU.mult, op1=ALU.add), reads=["ssq0"], writes=["ssq1"])
                sc.add("pool", lambda e: e.tensor_tensor(out=ssq[:, 2:3], in0=ssq[:, 1:2], in1=neghalf[:, 0:1], op=ALU.pow),
                       reads=["ssq1", "neghalf"], writes=["ssq2"])
                sc.add("dve", lambda e: e.tensor_scalar(out=y_sb[:], in0=y_sb[:], scalar1=ssq[:, 2:3], scalar2=0.0,
                                                        op0=ALU.mult, op1=ALU.add), reads=["y_sb", "ssq2"], writes=["y_sb"])
                pb = nb2(0)

                def tr_y(e, pb=pb):
                    r = None
                    for i in range(4):
                        r = e.transpose(out=ps[pb][:, i * 128:(i + 1) * 128], in_=y_sb[:, i * 128:(i + 1) * 128],
                                        identity=ident[:])
                    return r
                sc.add("pe", tr_y, reads=["y_sb", "ident"], writes=[("ps", pb)])
                for i in range(4):
                    sc.add("act", lambda e, i=i, ci=ci, pb=pb: e.activation(
                        out=ycatT[:, i, ci * 128:(ci + 1) * 128], in_=ps[pb][:, i * 128:(i + 1) * 128], func=AF.Copy,
                        scale=scol("ssd_norm%d" % l, i)),
                        reads=[("ps", pb), "small"], writes=[("ycatT", i, ci)])
                pbs = 2

                def mm_st(e, ci=ci):
                    r = None
                    for g in range(2):
                        r = e.matmul(ps[1][:, g * 256:(g + 1) * 256], lhsT=B_tm[:, ci, g * 128:(g + 1) * 128],
                                     rhs=xdtd[:, g * 256:(g + 1) * 256], start=True, stop=True)
                    return r
                sc.add("pe", mm_st, reads=[("B_tm", ci), "xdtd"], writes=[("ps", 1)])
                sc.add("dve", lambda e: e.tensor_tensor(
                    out=state_f[:].rearrange("p (h d) -> p h d", h=8), in0=state_f[:].rearrange("p (h d) -> p h d", h=8),
                    in1=Eout[:, 8:16].unsqueeze(2).broadcast_to([128, 8, 64]), op=ALU.mult),
                    reads=["state_f", "Eout", "state_b"], writes=["state_f"])
                sc.add("dve", lambda e: e.tensor_tensor(out=state_f[:], in0=state_f[:], in1=ps[1][:], op=ALU.add),
                       reads=["state_f", ("ps", 1)], writes=["state_f"])
                sc.add("pool", lambda e: e.tensor_copy(out=state_b[:], in_=state_f[:]),
                       reads=["state_f"], writes=["state_b"])

            for ti in range(4):
                tile = t0 + ti
                pb = nb2(1)

                def mm_pool(e, ti=ti, tile=tile, pb=pb):
                    r = None
                    for g in range(4):
                        bc = (3 * g + (1 if tile == 0 else 0)) * 128
                        bp = (3 * g + 2) * 128
                        first = tile == 0
                        r = e.matmul(ps[pb][:, g * 64:(g + 1) * 64], lhsT=bands[:, bc:bc + 128],
                                     rhs=u_tm[:, ti, g * 64:(g + 1) * 64], start=True, stop=first)
                        if not first:
                            prev = ti - 1 if ti > 0 else 4
                            r = e.matmul(ps[pb][:, g * 64:(g + 1) * 64], lhsT=bands[:, bp:bp + 128],
                                         rhs=u_tm[:, prev, g * 64:(g + 1) * 64], start=False, stop=True)
                    return r
                sc.add("pe", mm_pool, reads=["bands", ("u_tm", ti), ("u_tm", ti - 1 if ti > 0 else 4)], writes=[("ps", pb)])
                sc.add("act", lambda e, pb=pb: e.copy(out=d_tm[:], in_=ps[pb][:, 0:256]),
                       reads=[("ps", pb)], writes=["d_tm"])
                pb2 = nb2(1)

                def tr_d(e, pb2=pb2):
                    r = None
                    for i in range(2):
                        r = e.transpose(out=ps[pb2][:, i * 128:(i + 1) * 128], in_=d_tm[:, i * 128:(i + 1) * 128],
                                        identity=ident[:])
                    return r
                sc.add("pe", tr_d, reads=["d_tm", "ident"], writes=[("ps", pb2)])
                sc.add("act", lambda e, pb2=pb2, ti=ti: e.copy(
                    out=dT[:, :, ti * 128:(ti + 1) * 128], in_=ps[pb2][:, 0:256].rearrange("p (a t) -> p a t", a=2)),
                    reads=[("ps", pb2)], writes=[("dT", ti)])
            for pair in range(2):
                pb = nb2(0)
                sc.add("pe", lambda e, pair=pair, pb=pb: e.matmul(
                    ps[pb][:], lhsT=wp[:, pair * 128:(pair + 1) * 128], rhs=dT[:, pair, :], start=True, stop=True),
                    reads=["wp"] + [("dT", ti) for ti in range(4)], writes=[("ps", pb)])
                sc.add("act", lambda e, pair=pair, pb=pb: e.activation(
                    out=ycatT[:, 4 + pair, :], in_=ps[pb][:], func=AF.Copy, scale=scol("pool_scale%d" % l, pair)),
                    reads=[("ps", pb), "small"], writes=[("ycatT", 4 + pair, ci) for ci in range(4)])

            need_sel = (grp * 2) >= 4
            if need_sel:
                for ti in range(4):
                    qb = (t0 + ti) // 2
                    pb = nb2(1)

                    def mm_gate(e, ti=ti, qb=qb, pb=pb):
                        r = None
                        for h in range(4):
                            rows = slice((h % 2) * 64, (h % 2) * 64 + 64)
                            r = e.matmul(ps[pb][:, h * 8:h * 8 + qb], lhsT=QT[rows, h // 2, ti * 128:(ti + 1) * 128],
                                         rhs=kmT[rows, h // 2, 0:qb], start=True, stop=True)
                        return r
                    sc.add("pe", mm_gate, reads=[("QT", ti)] + [("kmT", b) for b in range(qb)], writes=[("ps", pb)])
                    sc.add("dve", lambda e, pb=pb, qb=qb: e.tensor_copy(
                        out=gate[:, :, 0:qb], in_=ps[pb][:, 0:32].rearrange("p (h n) -> p h n", h=4)[:, :, 0:qb]),
                        reads=[("ps", pb)], writes=["gate"])
                    sc.add("dve", lambda e, qb=qb: e.tensor_tensor(
                        out=cmpb[:, :, 0:qb, 0:qb],
                        in0=gate[:, :, 0:qb].unsqueeze(2).broadcast_to([128, 4, qb, qb]),
                        in1=gate[:, :, 0:qb].unsqueeze(3).broadcast_to([128, 4, qb, qb]), op=ALU.is_gt),
                        reads=["gate"], writes=["cmpb"])
                    sc.add("dve", lambda e, qb=qb: e.tensor_reduce(
                        out=cnt[:, :, 0:qb], in_=cmpb[:, :, 0:qb, 0:qb], axis=AX.X, op=ALU.add),
                        reads=["cmpb"], writes=["cnt"])
                    sc.add("dve", lambda e, ti=ti: e.memset(bias_tm[:, ti, :, :], 0.0), writes=[("bias_tm", ti)])
                    sc.add("dve", lambda e, ti=ti, qb=qb: e.tensor_scalar(
                        out=bias_tm[:, ti, :, 0:qb], in0=cnt[:, :, 0:qb], scalar1=2.5, scalar2=-30000.0,
                        op0=ALU.is_gt, op1=ALU.mult),
                        reads=["cnt", ("bias_tm", ti)], writes=[("bias_tm", ti)])
            for h in range(4):
                hp = h // 2
                rows = slice((h % 2) * 64, (h % 2) * 64 + 64)
                bs = h % 2
                if need_sel:
                    pb = nb2(1)

                    def tr_b(e, h=h, pb=pb):
                        r = None
                        for ti in range(4):
                            r = e.transpose(out=ps[pb][0:8, ti * 128:(ti + 1) * 128], in_=bias_tm[:, ti, h, :],
                                            identity=ident[:])
                        return r
                    sc.add("pe", tr_b, reads=[("bias_tm", ti) for ti in range(4)] + ["ident"], writes=[("ps", pb)])
                    sc.add("act", lambda e, pb=pb, bs=bs: e.copy(out=biasT[bs][:], in_=ps[pb][0:8, :]),
                           reads=[("ps", pb)], writes=[("biasT", bs)])
                acc = 6 + (h % 2)
                for kt in range(t0 + 4):
                    kb = kt // 2
                    if kt < t0:
                        c0 = 0
                    else:
                        c0 = (kt - t0) * 128
                    pb = 4 + (kt % 2)
                    past_cols = None
                    if need_sel:
                        if kb < grp * 2:
                            past_cols = (0, 512)
                        elif kb == grp * 2:
                            past_cols = (256, 512)

                    if kt < t0:
                        segs = [(0, 512, need_sel, False)]
                    else:
                        d0 = (kt - t0) * 128
                        segs = [(d0, d0 + 128, False, True)]
                        blk_end = 256 if (kt - t0) < 2 else 512
                        if d0 + 128 < blk_end:
                            segs.append((d0 + 128, blk_end, False, False))
                        if blk_end < 512:
                            if need_sel:
                                segs.append((blk_end, 512, True, False))
                            elif len(segs) > 1:
                                segs[-1] = (segs[-1][0], 512, False, False)
                            else:
                                segs.append((blk_end, 512, False, False))

                    def mm_s(e, kt=kt, pb=pb, segs=segs, rows=rows, hp=hp, kb=kb, bs=bs):
                        r = None
                        for (a0, a1, hb, hc) in segs:
                            reg = ps[pb][:, a0:a1]
                            r = e.matmul(reg, lhsT=KT[rows, hp, kt * 128:(kt + 1) * 128],
                                         rhs=QT[rows, hp, a0:a1], start=True, stop=not (hb or hc))
                            if hb:
                                r = e.matmul(reg, lhsT=esel[:, kb * 128:(kb + 1) * 128],
                                             rhs=biasT[bs][:, a0:a1], start=False, stop=True)
                            if hc:
                                r = e.matmul(reg, lhsT=ident_bf[:], rhs=cmask_bf[:], start=False, stop=True)
                        return r
                    rd = [("KT", kt)] + [("QT", ti) for ti in range(4)] + ["esel", ("biasT", bs), "ident_bf", "cmask"]
                    sc.add("pe", mm_s, reads=rd, writes=[("ps", pb)])
                    pt = kt % 2
                    sc.add("act", lambda e, pb=pb, c0=c0, pt=pt: e.activation(
                        out=PT[pt][:, c0:512], in_=ps[pb][:, c0:512], func=AF.Exp, scale=0.125),
                        reads=[("ps", pb)], writes=[("PT", pt)])

                    def mm_pv(e, kt=kt, c0=c0, pt=pt, h=h, t0=t0):
                        r = None
                        for ti in range(c0 // 128, 4):
                            tq = t0 + ti
                            r = e.matmul(ps[ti][:, 0:65], lhsT=PT[pt][:, ti * 128:(ti + 1) * 128],
                                         rhs=vp[:, kt, h, :], start=(kt == 0), stop=(kt == tq))
                        return r
                    sc.add("pe", mm_pv, reads=[("PT", pt), ("vp", kt)], writes=[("ps", ti) for ti in range(c0 // 128, 4)])
                for ti in range(4):
                    sc.add("dve", lambda e, ti=ti: e.reciprocal(out=rden[:, ti:ti + 1], in_=ps[ti][:, 64:65]),
                           reads=[("ps", ti)], writes=[("rden", ti)], name="recip g%d h%d ti%d" % (grp, h, ti))
                    sc.add("dve", lambda e, ti=ti, h=h: e.tensor_scalar(
                        out=att_tm[:, ti, h * 64:(h + 1) * 64], in0=ps[ti][:, 0:64], scalar1=rden[:, ti:ti + 1], scalar2=0.0,
                        op0=ALU.mult, op1=ALU.add),
                        reads=[("ps", ti), ("rden", ti)], writes=[("att_tm", h, ti)])
            for ti in range(4):
                pb = nb2(1)

                def tr_a(e, ti=ti, pb=pb):
                    r = None
                    for i in range(2):
                        r = e.transpose(out=ps[pb][:, i * 128:(i + 1) * 128], in_=att_tm[:, ti, i * 128:(i + 1) * 128],
                                        identity=ident[:])
                    return r
                sc.add("pe", tr_a, reads=[("att_tm", h, ti) for h in range(4)] + ["ident"], writes=[("ps", pb)])
                sc.add("act", lambda e, pb=pb, ti=ti: e.copy(
                    out=ycatT[:, 6:8, ti * 128:(ti + 1) * 128], in_=ps[pb][:, 0:256].rearrange("p (a t) -> p a t", a=2)),
                    reads=[("ps", pb)], writes=[("ycatT", 6, ti), ("ycatT", 7, ti)])

            ycr = [("ycatT", c, ci) for c in range(8) for ci in range(4)]
            for half in range(4):
                ws = load_w(wout_d[(l * 4 + half) * 128:(l * 4 + half + 1) * 128, :])
                for mm_ in range(2):
                    m = half * 2 + mm_
                    pb = nb2(0)

                    def mmo(e, ws=ws, mm_=mm_, pb=pb):
                        r = None
                        for k in range(NC8):
                            r = e.matmul(ps[pb][:], lhsT=wsl[ws][:, (mm_ * 8 + k) * 128:(mm_ * 8 + k + 1) * 128],
                                         rhs=ycatT[:, k, :], start=(k == 0), stop=(k == NC8 - 1))
                        return r
                    sc.add("pe", mmo, reads=[("wsl", ws)] + ycr, writes=[("ps", pb)])
                    s2 = m % 2
                    sc.add("dve", lambda e, pb=pb, m=m: e.tensor_copy(out=mT[:, m, :], in_=ps[pb][:]),
                           reads=[("ps", pb)], writes=[("xbcf", m)])
                    sc.add("act", lambda e, m=m, s2=s2: e.activation(out=sq[s2][:], in_=mT[:, m, :], func=AF.Square),
                           reads=[("xbcf", m)], writes=[("sq", s2)])
                    sc.add("pe", lambda e, s2=s2, m=m: e.matmul(
                        ps[7][:], lhsT=ones_bf[:], rhs=sq[s2][:], start=(m == 0), stop=(m == NC8 - 1)),
                        reads=[("sq", s2), "ones"], writes=[("ps", 7)])
            rstd_op(1, 7)
            postnorm_add(grp, 1, gpost, lambda m: mT[:, m, :], lambda m: ("xbcf", m))

    if stage == "all":
        plan = [(l, ["f1", "m", "f2"]) for l in range(DEPTH)]
    elif stage == "none":
        plan = []
    else:
        lay, parts = stage.split(":")
        plan = [(int(lay), parts.split("+"))]
    for l, stages in plan:
        if "f1" in stages:
            ffn(l, 0, "ff1_norm_pre", "ff1_norm_post")
        if "m" in stages:
            barrier()
            mixer(l)
            barrier()
        if "f2" in stages:
            ffn(l, 1, "ff2_norm_pre", "ff2_norm_post")

    barrier()
    for t in range(S // 128):
        slot = t % 2
        for q in range(2):
            pb = 6 + q

            def tr2(e, q=q, t=t, pb=pb):
                r = None
                for i in range(4):
                    c = 4 * q + i
                    r = e.transpose(out=ps[pb][:, i * 128:(i + 1) * 128], in_=xT[:, c, t * 128:(t + 1) * 128],
                                    identity=ident[:])
                return r
            sc.add("pe", tr2, reads=[xkey(4 * q + i, t // 4) for i in range(4)] + ["ident"], writes=[("ps", pb)])
            sc.add("act", lambda e, q=q, slot=slot, pb=pb: e.copy(
                out=xio[slot][:, q * 512:(q + 1) * 512], in_=ps[pb][:]),
                reads=[("ps", pb)], writes=[("xio", slot, q)])
        sc.add("sp", lambda e, t=t, slot=slot: e.dma_start(out=out_d[t * 128:(t + 1) * 128, :], in_=xio[slot]),
               reads=[("xio", slot, 0), ("xio", slot, 1)], writes=[("out", t)], dma=("out", slot))
    sc.add("sp", lambda e: e.nop(), reads=[("out", t) for t in range(S // 128)], writes=[])
    sc.emit()
    return nc


def prep_weights(inp):
    wgu = np.empty((DEPTH, 2, NJ, 128, 2, 8, 128), np.float32)
    wd = np.empty((DEPTH, 2, NC8, 128, NJ, 128), np.float32)
    win = np.empty((DEPTH, WIN_PIECES, 128, 2048), np.float32)
    wdt = np.empty((DEPTH, 128, 64), np.float32)
    wout = np.empty((DEPTH, 4, 128, 2, 8, 128), np.float32)
    wp = np.zeros((DEPTH, 128, 2, 128), np.float32)
    for l in range(DEPTH):
        for f, pfx in enumerate(("ff1", "ff2")):
            gate_up = {"ff1": (inp["ff1_w_gate"], inp["ff1_w_up"]), "ff2": (inp["ff2_w_gate"], inp["ff2_w_up"])}[pfx]
            for g in range(2):
                w = np.asarray(gate_up[g][l])
                wgu[l, f, :, :, g] = w.reshape(8, 128, NJ, 128).transpose(2, 1, 0, 3)
            w = np.asarray({"ff1": inp["ff1_w_down"], "ff2": inp["ff2_w_down"]}[pfx][l])
            wd[l, f] = w.reshape(NJ, 128, NC8, 128).transpose(2, 1, 0, 3)
        w = np.asarray(inp["w_in"][l])
        wk = w.reshape(8, 128, 2568)
        for fp in range(4):
            blk = wk[:, :, 512 + fp * 256:512 + (fp + 1) * 256].reshape(8, 128, 2, 128)
            win[l, fp] = blk.transpose(1, 2, 0, 3).reshape(128, 2048)
        wt = wk[:, :, TCOLS]
        for tp in range(6):
            blk = wt[:, :, tp * 256:(tp + 1) * 256]
            win[l, 4 + tp] = blk.transpose(1, 0, 2).reshape(128, 2048)
        wdt[l] = wk[:, :, 1536:1544].transpose(1, 0, 2).reshape(128, 64)
        wo = np.asarray(inp["w_out"][l]).reshape(8, 128, 4, 2, 128)
        wout[l] = wo.transpose(2, 1, 3, 0, 4)
        pw = np.asarray(inp["pool_w"][l])
        for g in range(4):
            pair, i = g // 2, g % 2
            wp[l, i * 64:(i + 1) * 64, pair, i * 64:(i + 1) * 64] = pw[g]
    return dict(
        wgu=wgu.reshape(DEPTH * 2 * NJ * 128, 2 * 8 * 128), wd=wd.reshape(DEPTH * 2 * NC8 * 128, NJ * 128),
        win=win.reshape(DEPTH * WIN_PIECES * 128, 2048), wdt=wdt.reshape(DEPTH * 128, 64),
        wout=wout.reshape(DEPTH * 4 * 128, 2048), wp=wp.reshape(DEPTH * 128, 256))


LAUNCH_PLAN = ["all"]


def kernel(stage=None, **inp):
    inp = {k: np.asarray(v) for k, v in inp.items()}
    small = pack_small(inp)
    assert small.shape[1] == NSMALL
    shared = prep_weights(inp)
    shared.update(make_consts())
    shared["small"] = small
    shared["ident"] = np.eye(128, dtype=np.float32)
    x = inp["x"]
    plan = LAUNCH_PLAN if stage is None else [stage]
    progs = {}
    for st in plan:
        if st not in progs:
            progs[st] = build(st)
        nc = progs[st]
        in_maps = [dict(shared, x=np.ascontiguousarray(x[b])) for b in range(8)]
        res = run_bass_kernel_spmd(nc, in_maps, core_ids=list(range(8)))
        x = np.stack([r["out"] for r in res.results], axis=0)
    return x
```
